# Optimizing a Trainium2 kernel written in Bass

```python
import jax, jax.numpy as jnp
from jax import lax
import numpy as np

D_MODEL = 2048
BATCH = 4
SEQ = 2048
DEPTH = 2

CONV_DIM = D_MODEL // 2
CONV_GROUPS = 8
CONV_GROUP_DIM = CONV_DIM // CONV_GROUPS
CONV_WIDTH = 31
CONV_PAD = (CONV_WIDTH - 1) // 2
SGU_DIM = D_MODEL // 2
SGU_HEADS = 8
SGU_HEAD_DIM = SGU_DIM // SGU_HEADS
CHUNK = 128
IN_DIM = 2 * CONV_DIM + 2 * SGU_DIM
MIX_DIM = CONV_DIM + SGU_DIM
FNET_GROUPS = 8
FNET_GROUP_DIM = D_MODEL // FNET_GROUPS
D_FF = ((8 * D_MODEL + 3 * 256 - 1) // (3 * 256)) * 256
N_EVEN = (DEPTH + 1) // 2
N_ODD = DEPTH // 2
RMS_EPS = 1e-6
LN_EPS = 1e-5

kernel_name = "hybrid_conv_sgu_fnet_encoder"


def rms_norm(x, g):
    xf = x.astype(jnp.float32)
    y = xf * lax.rsqrt(jnp.mean(xf * xf, axis=-1, keepdims=True) + RMS_EPS)
    return (y * g.astype(jnp.float32)).astype(x.dtype)


def layer_norm(x, g, b):
    xf = x.astype(jnp.float32)
    mu = jnp.mean(xf, axis=-1, keepdims=True)
    xc = xf - mu
    y = xc * lax.rsqrt(jnp.mean(xc * xc, axis=-1, keepdims=True) + LN_EPS)
    return (y * g.astype(jnp.float32) + b.astype(jnp.float32)).astype(x.dtype)


def conv_module(z, dw_w, dw_b, ln_g, ln_b):
    a, gate = jnp.split(z, 2, axis=-1)
    y = a * jax.nn.sigmoid(gate)
    y = lax.conv_general_dilated(
        y, dw_w[:, None, :], window_strides=(1,),
        padding=[(CONV_PAD, CONV_PAD)],
        dimension_numbers=("NWC", "WIO", "NWC"),
        feature_group_count=CONV_DIM) + dw_b
    b, s, _ = y.shape
    y = layer_norm(y.reshape(b, s, CONV_GROUPS, CONV_GROUP_DIM),
                   ln_g.reshape(CONV_GROUPS, CONV_GROUP_DIM),
                   ln_b.reshape(CONV_GROUPS, CONV_GROUP_DIM)).reshape(b, s, CONV_DIM)
    return jax.nn.silu(y)


def spatial_gating(z, ln_g, ln_b, w_s, b_s):
    z = jax.nn.gelu(z, approximate=False)
    u, v = jnp.split(z, 2, axis=-1)
    v = layer_norm(v, ln_g, ln_b)
    b, s, _ = v.shape
    v = v.reshape(b, s // CHUNK, CHUNK, SGU_HEADS, SGU_HEAD_DIM)
    mixed = jnp.einsum("hpq,bnqhc->bnphc", w_s, v) + b_s.T[None, None, :, :, None]
    return u * mixed.reshape(b, s, SGU_DIM)


def fourier_mix(h):
    b, s, d = h.shape
    hg = h.astype(jnp.float32).reshape(b, s, FNET_GROUPS, FNET_GROUP_DIM)
    y = jnp.fft.fft2(hg, axes=(1, 3), norm="ortho").real
    return y.reshape(b, s, d).astype(h.dtype)


def swiglu(h, w_gate, w_up, w_down):
    return (jax.nn.silu(h @ w_gate) * (h @ w_up)) @ w_down


def setup_inputs(seed: int = 0) -> dict:
    key = jax.random.key(seed)
    ks = jax.random.split(key, 24)
    f32 = jnp.float32

    def nrm(k, shape, scale):
        return jax.random.normal(k, shape, f32) * scale

    def gain(k, shape):
        return 1.0 + 0.02 * jax.random.normal(k, shape, f32)

    return {
        "x": jax.random.normal(ks[0], (BATCH, SEQ, D_MODEL), f32),
        "mix_norm_g": gain(ks[1], (DEPTH, D_MODEL)),
        "ffn_norm_g": gain(ks[2], (DEPTH, D_MODEL)),
        "final_norm_g": gain(ks[3], (D_MODEL,)),
        "ab_w_in": nrm(ks[4], (N_EVEN, D_MODEL, IN_DIM), D_MODEL ** -0.5),
        "conv_dw_w": nrm(ks[5], (N_EVEN, CONV_WIDTH, CONV_DIM), CONV_WIDTH ** -0.5),
        "conv_dw_b": nrm(ks[6], (N_EVEN, CONV_DIM), 0.02),
        "conv_ln_g": gain(ks[7], (N_EVEN, CONV_DIM)),
        "conv_ln_b": nrm(ks[8], (N_EVEN, CONV_DIM), 0.02),
        "sgu_ln_g": gain(ks[9], (N_EVEN, SGU_DIM)),
        "sgu_ln_b": nrm(ks[10], (N_EVEN, SGU_DIM), 0.02),
        "sgu_w": nrm(ks[11], (N_EVEN, SGU_HEADS, CHUNK, CHUNK), CHUNK ** -0.5),
        "sgu_b": 1.0 + nrm(ks[12], (N_EVEN, SGU_HEADS, CHUNK), 0.01),
        "ab_w_out": nrm(ks[13], (N_EVEN, MIX_DIM, D_MODEL), MIX_DIM ** -0.5),
        "fnet_w_out": nrm(ks[14], (N_ODD, D_MODEL, D_MODEL), D_MODEL ** -0.5),
        "fnet_b_out": nrm(ks[15], (N_ODD, D_MODEL), 0.02),
        "ffn_w_gate": nrm(ks[16], (DEPTH, D_MODEL, D_FF), D_MODEL ** -0.5),
        "ffn_w_up": nrm(ks[17], (DEPTH, D_MODEL, D_FF), D_MODEL ** -0.5),
        "ffn_w_down": nrm(ks[18], (DEPTH, D_FF, D_MODEL), D_FF ** -0.5),
    }


def reference(x, mix_norm_g, ffn_norm_g, final_norm_g, ab_w_in, conv_dw_w,
              conv_dw_b, conv_ln_g, conv_ln_b, sgu_ln_g, sgu_ln_b, sgu_w, sgu_b,
              ab_w_out, fnet_w_out, fnet_b_out, ffn_w_gate, ffn_w_up, ffn_w_down):
    for layer in range(DEPTH):
        h = rms_norm(x, mix_norm_g[layer])
        if layer % 2 == 0:
            i = layer // 2
            z = h @ ab_w_in[i]
            z_conv = z[..., : 2 * CONV_DIM]
            z_sgu = z[..., 2 * CONV_DIM:]
            y_conv = conv_module(z_conv, conv_dw_w[i], conv_dw_b[i],
                                 conv_ln_g[i], conv_ln_b[i])
            y_sgu = spatial_gating(z_sgu, sgu_ln_g[i], sgu_ln_b[i],
                                   sgu_w[i], sgu_b[i])
            x = x + jnp.concatenate([y_conv, y_sgu], axis=-1) @ ab_w_out[i]
        else:
            j = layer // 2
            x = x + fourier_mix(h) @ fnet_w_out[j] + fnet_b_out[j]
        h = rms_norm(x, ffn_norm_g[layer])
        x = x + swiglu(h, ffn_w_gate[layer], ffn_w_up[layer], ffn_w_down[layer])
    return rms_norm(x, final_norm_g)
```

```python
import math
from contextlib import ExitStack

import numpy as np
import ml_dtypes

import concourse.bass as bass
import concourse.mybir as mybir
from concourse.bass_utils import run_bass_kernel_spmd

F32 = mybir.dt.float32
BF16 = mybir.dt.bfloat16
ALU = mybir.AluOpType
AF = mybir.ActivationFunctionType
AX = mybir.AxisListType

D = 2048
KC = 16
T = 1024
S_LEN = 2048
DFF = 5632
FC = 44
FH = 22
HALO = 15
TE = 1056
RMS_EPS = 1e-6
LN_EPS = 1e-5

V_MIXG0, V_FFNG0, V_MIXG1, V_FFNG1, V_FING = 0, 16, 32, 48, 64
V_CONVB, V_CLNG, V_CLNB = 80, 88, 96
V_FNETB = 104
V_SLNG, V_SLNB = 120, 128
V_CONVW = 136
NVEC = 136 + 248

SBUF_BASE = 16512
DEBUG_STOP = None
DBG_FC = None
DBG_DOWN = True
DBG_FH = 2
DBG_L1 = 3
SBUF_END = 229376 - 2048


class Op:
    __slots__ = ("eng", "idx", "fn", "deps", "dma", "need", "sig", "dsem", "dval", "uid")

    def __init__(self, eng, idx, fn, deps, dma, uid):
        self.eng, self.idx, self.fn, self.deps, self.dma = eng, idx, fn, deps, dma
        self.need = False
        self.sig = 0
        self.dsem = None
        self.dval = 0
        self.uid = uid


class Sched:
    ENGS = ("pe", "act", "dve", "pool", "sp")
    SEG = 2000
    ND = 6

    def __init__(self):
        self.ops = {e: [] for e in self.ENGS}
        self.state = {}
        self.uid = 0

    def add(self, eng, fn, reads=(), writes=(), dma=False):
        deps = {}
        wset = set(writes)
        for k in reads:
            if k in wset:
                continue
            st = self.state.get(k)
            if st is not None and st[0] is not None:
                deps[st[0].uid] = st[0]
        for k in wset:
            st = self.state.get(k)
            if st is not None:
                if st[0] is not None:
                    deps[st[0].uid] = st[0]
                for r in st[1].values():
                    deps[r.uid] = r
        self.uid += 1
        op = Op(eng, len(self.ops[eng]), fn, None, dma, self.uid)
        fd = []
        rset = set(reads)
        for d in deps.values():
            if d.dma or dma or d.eng != eng:
                fd.append(d)
            elif eng != "pe" and self._is_raw(d, rset):
                fd.append(d)
        op.deps = fd
        self.ops[eng].append(op)
        for k in reads:
            if k in wset:
                continue
            st = self.state.setdefault(k, [None, {}])
            st[1][("d", op.uid) if dma else eng] = op
        for k in wset:
            self.state[k] = [op, {}]
        return op

    def _is_raw(self, d, rset):
        for k in rset:
            st = self.state.get(k)
            if st is not None and st[0] is d:
                return True
        return False

    def finalize(self, nc, stack):
        for e in self.ENGS:
            for op in self.ops[e]:
                for d in op.deps:
                    d.need = True
        self.sems = {}
        for e in self.ENGS:
            n = 0
            nd = 0
            for op in self.ops[e]:
                if op.dma:
                    op.dsem = (e, nd % self.ND)
                    op.dval = 16 * (nd // self.ND + 1)
                    nd += 1
                elif op.need:
                    n += 1
                    op.sig = n
            nseg = (n + self.SEG - 1) // self.SEG
            for s in range(nseg):
                self.sems[(e, "c", s)] = stack.enter_context(nc.semaphore(f"s_{e}_{s}"))
            for s in range(min(nd, self.ND)):
                self.sems[(e, "d", s)] = stack.enter_context(nc.semaphore(f"d_{e}_{s}"))

    def emit(self, eng, e):
        w_sig = {x: 0 for x in self.ENGS}
        w_dma = {}
        for op in self.ops[eng]:
            waits = []
            if op.dma and op.dval > 16:
                key = (eng, "d", op.dsem[1])
                prev = op.dval - 16
                if w_dma.get(key, 0) < prev:
                    waits.append((self.sems[key], prev))
                    w_dma[key] = prev
            best = {}
            for d in op.deps:
                if d.dma:
                    key = (d.eng, "d", d.dsem[1])
                    if w_dma.get(key, 0) < d.dval:
                        waits.append((self.sems[key], d.dval))
                        w_dma[key] = d.dval
                elif d.sig > best.get(d.eng, 0):
                    best[d.eng] = d.sig
            for x, sg in best.items():
                if w_sig[x] < sg:
                    waits.append((self.sems[(x, "c", (sg - 1) // self.SEG)], (sg - 1) % self.SEG + 1))
                    w_sig[x] = sg
            for sem, val in waits:
                e.wait_ge(sem, val)
            if op.fn is None:
                continue
            ins = op.fn(e)
            if op.dma:
                ins.then_inc(self.sems[(eng, "d", op.dsem[1])], 16)
            elif op.need:
                ins.then_inc(self.sems[(eng, "c", (op.sig - 1) // self.SEG)], 1)


def build(mode):
    do1 = mode in ("p1", "fused")
    do2 = mode in ("p2", "fused")
    nc = bass.Bass("TRN2", target_bir_lowering=False)
    S = Sched()

    def din(name, shape, dt=F32):
        return nc.dram_tensor(name, list(shape), dt, kind="ExternalInput").ap()

    def dout(name, shape, dt=F32):
        return nc.dram_tensor(name, list(shape), dt, kind="ExternalOutput").ap()

    xT_d = din("xT", [128, KC, T])
    vec_d = din("vec", [128, NVEC])
    if do1:
        xh_d = din("xh", [128, KC, 32])
        sgubc_d = din("sgubc", [128, 3072])
        wsT_d = din("wsT", [128, 1024])
        wconv_d = din("w_conv", [8, 128, 4096])
        wu_d = din("w_u", [8, 128, 2048])
        wv_d = din("w_v", [2, 128, 8192])
        wo_d = din("w_o", [16, 128, 2048])
        wgu0_d = din("w_gu0", [FC, 128, 4096])
        wd0_d = din("w_d0", [2, 16, 128, FH * 128])
        ccs_d = din("ccs", [128, 2, 512], BF16)
    if do2:
        csn_d = din("csn", [128, 2, 16, T], BF16)
        wf_d = din("w_f", [16, 128, 2048])
        wgu1_d = din("w_gu1", [FC, 128, 4096])
        wd1_d = din("w_d1", [2, 16, 128, FH * 128])
        out_d = dout("outT", [128, KC, T])
    if mode == "p1":
        x1_d = dout("x1T", [128, KC, T])
        abown_d = dout("ab_own", [16, 2, 128, 8, 128], BF16)
    elif mode == "p2":
        abfull_d = din("ab_full", [2, 16, 2, 128, 8, 128], BF16)
    else:
        xTo_d = din("xTo", [128, KC, T])
        xho_d = din("xho", [128, KC, 32])
        abfull_d = nc.dram_tensor("ab_full_i", [2, 16, 2, 128, 8, 128], BF16, kind="Internal").ap()

    cur = [SBUF_BASE]

    def region(nbytes):
        o = cur[0]
        cur[0] += (nbytes + 31) // 32 * 32
        assert cur[0] <= SBUF_END, ("sbuf overflow", cur[0])
        return o

    NRING = 3
    SLOT_B = 8192
    R_OFF = region(NRING * SLOT_B)
    VEC_OFF = region(NVEC * 4)
    ONES_OFF = region(3 * 256)
    CCS_OFF = region(2048)
    WST_OFF = region(2048)
    SQ_OFF = region(4 * 1024)
    RSTD_OFF = region(2 * 2048)
    STD_OFF = region(2 * 2048)
    EPS_OFF = region(64)
    X_OFF = region(KC * T * 4)
    H_OFF = region(KC * TE * 2)
    M_OFF = region(KC * T * 2)
    S_OFF = cur[0]
    S_SIZE = SBUF_END - S_OFF
    assert S_SIZE >= 32768 + 2048, S_SIZE

    cnt = [0]

    def sb(name, shape, dt, off):
        cnt[0] += 1
        return nc.alloc_sbuf_tensor_at(f"{name}{cnt[0]}", list(shape), dt, offset=off)

    xT = sb("xT", [128, KC, T], F32, X_OFF)
    hT = sb("hT", [128, KC, TE], BF16, H_OFF)
    mixT = sb("mixT", [128, KC, T], BF16, M_OFF)
    ring = [sb("ring", [128, 4096], BF16, R_OFF + i * SLOT_B) for i in range(NRING)]
    vec = sb("vec", [128, NVEC], F32, VEC_OFF)
    onesD = sb("onesD", [128, 128], BF16, ONES_OFF)
    onesG = sb("onesG", [128, 128], BF16, ONES_OFF + 256)
    ccs = sb("ccs", [128, 2, 512], BF16, CCS_OFF)
    wsT = sb("wsT", [128, 8, 128], BF16, WST_OFF)
    sqb = [sb("sq", [128, 512], BF16, SQ_OFF + i * 1024) for i in range(4)]
    rstdb = [sb("rstd", [128, 512], F32, RSTD_OFF + i * 2048) for i in range(2)]
    stdb = [sb("std", [128, 512], F32, STD_OFF + i * 2048) for i in range(2)]

    psum = [nc.alloc_psum_tensor(f"psb{i}", [128, 512], F32) for i in range(8)]
    bank_ctr = [0]

    def next_bank():
        b = bank_ctr[0] % 8
        bank_ctr[0] += 1
        return b

    ring_ctr = [0]

    def next_slot():
        s = ring_ctr[0] % NRING
        ring_ctr[0] += 1
        return s

    def vcol(c):
        return vec[:, c:c + 1]

    def mm(b, n, lhsT, rhs, start, stop, reads):
        out = psum[b][:, 0:n] if isinstance(n, int) else n
        S.add("pe", lambda e: e.matmul(out, lhsT, rhs, start=start, stop=stop),
              reads=reads, writes=[("ps", b)])

    def act(out, in_, func, reads, writes, bias=None, scale=None):
        kw = {}
        if bias is not None:
            kw["bias"] = bias
        if scale is not None:
            kw["scale"] = scale
        S.add("act", lambda e: e.activation(out, in_, func, **kw), reads=reads, writes=writes)

    def dve_tt(out, in0, in1, op, reads, writes):
        S.add("dve", lambda e: e.tensor_tensor(out, in0, in1, op), reads=reads, writes=writes)

    def dve_stt(out, in0, scalar, in1, op0, op1, reads, writes):
        S.add("dve", lambda e: e.scalar_tensor_tensor(out, in0, scalar, in1, op0, op1),
              reads=reads, writes=writes)

    def dve_ts(out, in0, s1, s2, op0, op1, reads, writes):
        S.add("dve", lambda e: e.tensor_scalar(out, in0, s1, s2, op0, op1), reads=reads, writes=writes)

    def dma(eng, out, in_, reads, writes):
        S.add(eng, lambda e: e.dma_start(out=out, in_=in_), reads=reads, writes=writes, dma=True)

    S.add("dve", lambda e: e.memset(onesD[:], 1.0 / D), writes=["onesD"])
    S.add("dve", lambda e: e.memset(onesG[:], 1.0 / 128.0), writes=["onesG"])
    dma("sp", vec[:], vec_d, [], ["vec"])

    norm_ctr = [0]

    def rmsnorm(gbase, blocks, src, dst, srckey, dstkey):
        for (c0, n, bk) in blocks:
            b = next_bank()
            i = norm_ctr[0] % 2
            norm_ctr[0] += 1
            for kc in range(KC):
                q = sqb[kc % 4]
                act(q[:, 0:n], src(kc, c0, n), AF.Square, reads=[srckey(kc, bk)], writes=[("sq", kc % 4)])
                mm(b, n, onesD[:], q[:, 0:n], kc == 0, kc == KC - 1, reads=[("sq", kc % 4), "onesD"])
            act(stdb[i][:, 0:n], psum[b][:, 0:n], AF.Sqrt, reads=[("ps", b)], writes=[("std", i)],
                bias=eps_rms[:, 0:1], scale=1.0)
            S.add("dve", lambda e, i=i, n=n: e.reciprocal(rstdb[i][:, 0:n], stdb[i][:, 0:n]),
                  reads=[("std", i)], writes=[("rstd", i)])
            for kc in range(KC):
                dve_stt(dst(kc, c0, n), src(kc, c0, n), vcol(gbase + kc), rstdb[i][:, 0:n],
                        ALU.mult, ALU.mult,
                        reads=[srckey(kc, bk), ("rstd", i), "vec"], writes=[dstkey(kc, bk)])

    epst = sb("eps", [128, 4], F32, EPS_OFF)
    eps_rms = epst[:, 0:1]
    eps_ln = epst[:, 1:2]
    S.add("dve", lambda e: e.memset(epst[:, 0:1], RMS_EPS), writes=["eps0"])
    S.add("dve", lambda e: e.memset(epst[:, 1:2], LN_EPS), writes=["eps1"])
    MAINB = [(0, 512, 0), (512, 512, 1)]

    def xsrc(kc, c0, n):
        return xT[:, kc, c0:c0 + n]

    def hdst(kc, c0, n):
        return hT[:, kc, c0:c0 + n]

    def xkey(kc, bk):
        return ("xT", kc, bk)

    def hkey(kc, bk):
        return ("hT", kc, bk)

    def load_xT(src_d):
        for q in range(4):
            dma("sp", xT[:, 4 * q:4 * q + 4, :], src_d[:, 4 * q:4 * q + 4, :], [],
                [("xT", kc, bk) for kc in range(4 * q, 4 * q + 4) for bk in (0, 1)])

    def wload(src_ap, ncols, extra_reads=()):
        s = next_slot()
        dma("pool", ring[s][:, 0:ncols], src_ap, list(extra_reads), [("ring", s)])
        return s

    def ffn(layer, wgu_d, wd_d):
        gb = V_FFNG0 if layer == 0 else V_FFNG1
        rmsnorm(gb, MAINB, xsrc, hdst, xkey, hkey)
        aT = sb("aT", [128, FH, T], BF16, M_OFF)
        sg = [sb("sg", [128, 512], F32, M_OFF + FH * T * 2 + i * 2048) for i in range(2)]
        sgc = 0
        for fh in range(DBG_FH):
            for fc in range(FH if DBG_FC is None else DBG_FC):
                f = fh * FH + fc
                s = wload(wgu_d[f], 4096)
                W = ring[s][:, :].rearrange("p (a k n) -> p a k n", a=2, k=KC)
                for half in range(2):
                    bg, bu = next_bank(), next_bank()
                    for kc in range(KC):
                        mm(bg, 512, W[:, 0, kc, :], hT[:, kc, half * 512:(half + 1) * 512], kc == 0, kc == KC - 1,
                           reads=[("ring", s), hkey(kc, half)])
                    for kc in range(KC):
                        mm(bu, 512, W[:, 1, kc, :], hT[:, kc, half * 512:(half + 1) * 512], kc == 0, kc == KC - 1,
                           reads=[("ring", s), hkey(kc, half)])
                    j = sgc % 2
                    sgc += 1
                    act(sg[j][:], psum[bg][:], AF.Silu, reads=[("ps", bg), "Mreg"], writes=[("sg", j)])
                    dve_tt(aT[:, fc, half * 512:(half + 1) * 512], sg[j][:], psum[bu][:], ALU.mult,
                           reads=[("sg", j), ("ps", bu)], writes=[("aT", fc, half)])
            for n in range(KC if DBG_DOWN else 0):
                s = wload(wd_d[fh, n], FH * 128)
                W = ring[s][:, 0:FH * 128].rearrange("p (k n) -> p k n", k=FH)
                for half in range(2):
                    b = next_bank()
                    for fc in range(FH):
                        mm(b, 512, W[:, fc, :], aT[:, fc, half * 512:(half + 1) * 512], fc == 0, fc == FH - 1,
                           reads=[("ring", s), ("aT", fc, half)])
                    dve_tt(xT[:, n, half * 512:(half + 1) * 512], psum[b][:], xT[:, n, half * 512:(half + 1) * 512],
                           ALU.add, reads=[("ps", b)], writes=[xkey(n, half)])

    def part1(xT_d, xh_d, abown_d, pss):
        load_xT(xT_d)
        check(1)
        xh = sb("xh", [128, KC, 32], F32, S_OFF)
        dma("sp", xh[:], xh_d, [], ["xh"])
        rmsnorm(V_MIXG0, MAINB, xsrc, hdst, xkey, hkey)
        rmsnorm(V_MIXG0, [(0, 32, 2)],
                lambda kc, c0, n: xh[:, kc, 0:32], lambda kc, c0, n: hT[:, kc, 1024:1056],
                lambda kc, bk: "xh", hkey)

        ax = [X_OFF]

        def aX(nbytes):
            o = ax[0]
            ax[0] += (nbytes + 31) // 32 * 32
            assert ax[0] <= X_OFF + KC * T * 4, "X arena overflow"
            return o

        as_ = [S_OFF]

        def aS(nbytes):
            o = as_[0]
            as_[0] += (nbytes + 31) // 32 * 32
            assert as_[0] <= SBUF_END, "S arena overflow"
            return o

        Wv = [sb("Wv", [128, KC, 512], BF16, aX(16384)) for _ in range(2)]
        glu = [sb("glu", [128, TE], F32, aX(TE * 4)) for _ in range(2)]
        ycv = [sb("ycv", [128, T], F32, aX(T * 4)) for _ in range(2)]
        ybf = sb("ybf", [128, T], BF16, aX(T * 2))
        ysq = sb("ysq", [128, T], BF16, aX(T * 2))
        sig = [sb("sig", [128, 512], F32, aX(2048)) for _ in range(2)]
        m2 = sb("m2", [128, 512], F32, aX(2048))
        varb = sb("varb", [128, 512], F32, aX(2048))
        tcen = sb("tcen", [128, 512], F32, aX(2048))
        sgubc = sb("sgubc", [128, 3072], F32, aS(12288))
        vg = [sb("vg", [128, 1024], F32, aS(4096)) for _ in range(2)]
        vln = [sb("vln", [128, 1024], BF16, aS(2048)) for _ in range(2)]
        sptmp = sb("sptmp", [128, 512], F32, aS(2048))
        stats = sb("stats", [128, 2, 6], F32, aS(64))
        mv = sb("mv", [128, 2], F32, aS(32))
        sdv = sb("sdv", [128, 2], F32, aS(32))

        xdead = [xkey(kc, bk) for kc in range(KC) for bk in (0, 1)]
        S.add("dve", lambda e: e.memset(m2[:, 0:1], 0.0), reads=[], writes=xdead + ["arenaX"])

        dma("sp", sgubc[:], sgubc_d, [], ["sgubc", "xh"])
        dma("pool", wsT[:].rearrange("p a b -> p (a b)"), wsT_d, [], ["wsT"])

        check(2)
        for c in range(8):
            s = wload(wconv_d[c], 4096)
            W = ring[s][:, :].rearrange("p (a k n) -> p a k n", a=2, k=KC)
            g = glu[c % 2]
            gk = ("glu", c % 2)
            for (c0, n, bk) in [(0, 512, 0), (512, 512, 1), (1024, 32, 2)]:
                ba, bg = next_bank(), next_bank()
                for kc in range(KC):
                    mm(ba, n, W[:, 0, kc, :], hT[:, kc, c0:c0 + n], kc == 0, kc == KC - 1,
                       reads=[("ring", s), hkey(kc, bk)])
                for kc in range(KC):
                    mm(bg, n, W[:, 1, kc, :], hT[:, kc, c0:c0 + n], kc == 0, kc == KC - 1,
                       reads=[("ring", s), hkey(kc, bk)])
                j = bk % 2
                act(sig[j][:, 0:n], psum[bg][:, 0:n], AF.Sigmoid, reads=[("ps", bg), "arenaX"], writes=[("sig", j)])
                if bk < 2:
                    dve_tt(g[:, HALO + c0:HALO + c0 + n], psum[ba][:, 0:n], sig[j][:, 0:n], ALU.mult,
                           reads=[("ps", ba), ("sig", j), "arenaX"], writes=[gk + (bk,)])
                else:
                    dve_tt(g[:, 0:HALO], psum[ba][:, 0:HALO], sig[j][:, 0:HALO], ALU.mult,
                           reads=[("ps", ba), ("sig", j), "arenaX"], writes=[gk + (2,)])
                    dve_tt(g[:, HALO + T:HALO + T + HALO], psum[ba][:, HALO:2 * HALO], sig[j][:, HALO:2 * HALO],
                           ALU.mult, reads=[("ps", ba), ("sig", j)], writes=[gk + (3,)])
            y = ycv[c % 2]
            yk = ("ycv", c % 2)
            gkeys = [gk + (i,) for i in range(4)]
            dve_ts(y[:], g[:, 0:T], vcol(V_CONVW + c * 31), vcol(V_CONVB + c), ALU.mult, ALU.add,
                   reads=gkeys + ["vec"], writes=[yk])
            for j in range(1, 31):
                dve_stt(y[:], g[:, j:j + T], vcol(V_CONVW + c * 31 + j), y[:], ALU.mult, ALU.add,
                        reads=gkeys + [yk], writes=[yk])
            act(ybf[:], y[:], AF.Copy, reads=[yk], writes=["ybf"])
            act(ysq[:], y[:], AF.Square, reads=[yk], writes=["ysq"])
            for half in range(2):
                hs = slice(half * 512, (half + 1) * 512)
                bm, bq = next_bank(), next_bank()
                mm(bm, 512, onesG[:], ybf[:, hs], True, True, reads=["ybf", "onesG"])
                mm(bq, 512, onesG[:], ysq[:, hs], True, True, reads=["ysq", "onesG"])
                act(m2[:], psum[bm][:], AF.Square, reads=[("ps", bm)], writes=["m2"])
                dve_tt(varb[:], psum[bq][:], m2[:], ALU.subtract, reads=[("ps", bq), "m2"], writes=["varb"])
                act(varb[:], varb[:], AF.Sqrt, reads=["varb"], writes=["varb"], bias=eps_ln, scale=1.0)
                S.add("dve", lambda e: e.reciprocal(varb[:], varb[:]), reads=["varb"], writes=["varb"])
                dve_tt(tcen[:], y[:, hs], psum[bm][:], ALU.subtract, reads=[yk, ("ps", bm)], writes=["tcen"])
                dve_tt(tcen[:], tcen[:], varb[:], ALU.mult, reads=["tcen", "varb"], writes=["tcen"])
                act(mixT[:, c, hs], tcen[:], AF.Silu, reads=["tcen", "vec"], writes=[("mixT", c, half)],
                    bias=vcol(V_CLNB + c), scale=vcol(V_CLNG + c))

        check(3)
        for hc in range(8):
            s = wload(wu_d[hc], 2048)
            W = ring[s][:, 0:2048].rearrange("p (k n) -> p k n", k=KC)
            for half in range(2):
                b = next_bank()
                for kc in range(KC):
                    mm(b, 512, W[:, kc, :], hT[:, kc, half * 512:(half + 1) * 512], kc == 0, kc == KC - 1,
                       reads=[("ring", s), hkey(kc, half)])
                act(mixT[:, 8 + hc, half * 512:(half + 1) * 512], psum[b][:], AF.Gelu,
                    reads=[("ps", b)], writes=[("mixT", 8 + hc, half)])

        check(4)
        for nb in range(2):
            dma("pool", Wv[nb][:].rearrange("p k n -> p (k n)"), wv_d[nb], ["arenaX"], [("Wv", nb)])
        lng_bc = sgubc[:, 0:1024]
        lnb_bc = sgubc[:, 1024:2048]
        bs_bc = sgubc[:, 2048:3072].rearrange("p (h q) -> p h q", h=8)
        for tt in range(8):
            v = vg[tt % 2]
            vk = ("vg", tt % 2)
            for nb in range(2):
                b = next_bank()
                for kc in range(KC):
                    mm(b, 512, hT[:, kc, tt * 128:(tt + 1) * 128], Wv[nb][:, kc, :], kc == 0, kc == KC - 1,
                       reads=[("Wv", nb), hkey(kc, tt // 4)])
                act(v[:, nb * 512:(nb + 1) * 512], psum[b][:], AF.Gelu, reads=[("ps", b)],
                    writes=[vk + (nb,)])
            for nb in range(2):
                S.add("dve", lambda e, v=v, nb=nb: e.bn_stats(stats[:, nb, :], v[:, nb * 512:(nb + 1) * 512]),
                      reads=[vk + (nb,)], writes=[("stats", nb)])
            S.add("dve", lambda e: e.bn_aggr(mv[:], stats[:].rearrange("p a b -> p (a b)")),
                  reads=[("stats", 0), ("stats", 1)], writes=["mv"])
            act(sdv[:, 0:1], mv[:, 1:2], AF.Sqrt, reads=["mv"], writes=["sdv0"], bias=eps_ln, scale=1.0)
            S.add("dve", lambda e: e.reciprocal(sdv[:, 1:2], sdv[:, 0:1]), reads=["sdv0"], writes=["sdv1"])
            dve_ts(v[:], v[:], mv[:, 0:1], sdv[:, 1:2], ALU.subtract, ALU.mult,
                   reads=[vk + (0,), vk + (1,), "mv", "sdv1"], writes=[vk + (0,), vk + (1,)])
            dve_tt(v[:], v[:], lng_bc, ALU.mult, reads=[vk + (0,), vk + (1,), "sgubc"],
                   writes=[vk + (0,), vk + (1,)])
            vl = vln[tt % 2]
            vlk = ("vln", tt % 2)
            dve_tt(vl[:], v[:], lnb_bc, ALU.add, reads=[vk + (0,), vk + (1,), "sgubc"], writes=[vlk])
            for hg in range(2):
                b = next_bank()
                for hh in range(4):
                    hd = hg * 4 + hh
                    mm(b, psum[b][:, hh * 128:(hh + 1) * 128], vl[:, hd * 128:(hd + 1) * 128], wsT[:, hd, :],
                       True, True, reads=[vlk, "wsT"])
                dve_tt(sptmp[:].rearrange("p (h q) -> p h q", h=4), psum[b][:].rearrange("p (h q) -> p h q", h=4),
                       bs_bc[:, hg * 4:hg * 4 + 4, :], ALU.add, reads=[("ps", b), "sgubc"], writes=["sptmp"])
                mo = mixT[:, 8 + hg * 4:8 + hg * 4 + 4, tt * 128:(tt + 1) * 128]
                dve_tt(mo, sptmp[:].rearrange("p (h q) -> p h q", h=4), mo, ALU.mult,
                       reads=["sptmp"] + [("mixT", 8 + hg * 4 + hh, tt // 4) for hh in range(4)],
                       writes=[("mixT", 8 + hg * 4 + hh, tt // 4) for hh in range(4)])

        check(5)
        arena_keys = [k for k in S.state if isinstance(k, tuple) and k[0] in
                      ("glu", "ycv", "sig", "Wv")] + ["ybf", "ysq", "m2", "varb", "tcen", "arenaX"]
        for q in range(4):
            dma("sp", xT[:, 4 * q:4 * q + 4, :], xT_d[:, 4 * q:4 * q + 4, :], [],
                arena_keys + [("xT", kc, bk) for kc in range(4 * q, 4 * q + 4) for bk in (0, 1)])
        for n in range(KC):
            s = wload(wo_d[n], 2048)
            W = ring[s][:, 0:2048].rearrange("p (k n) -> p k n", k=KC)
            for half in range(2):
                b = next_bank()
                for kc in range(KC):
                    mm(b, 512, W[:, kc, :], mixT[:, kc, half * 512:(half + 1) * 512], kc == 0, kc == KC - 1,
                       reads=[("ring", s), ("mixT", kc, half)])
                dve_tt(xT[:, n, half * 512:(half + 1) * 512], psum[b][:], xT[:, n, half * 512:(half + 1) * 512],
                       ALU.add, reads=[("ps", b)], writes=[xkey(n, half)])

        check(6)
        skeys = [k for k in S.state if isinstance(k, tuple) and k[0] in ("vg", "vln", "stats", "mixT")] + \
                ["sgubc", "sptmp", "mv", "sdv0", "sdv1", "xh"]
        S.add("dve", lambda e: e.memset(epst[:, 2:3], 0.0), reads=[], writes=skeys + ["Mreg"])
        ffn(0, wgu0_d, wd0_d)

        check(7)
        if mode == "p1":
            for q in range(4):
                dma("sp", x1_d[:, 4 * q:4 * q + 4, :], xT[:, 4 * q:4 * q + 4, :],
                    [("xT", kc, bk) for kc in range(4 * q, 4 * q + 4) for bk in (0, 1)], [("x1out", q)])
            x1done[0] = True

        dma("sp", ccs[:], ccs_d, [], ["ccs"])
        rmsnorm(V_MIXG1, MAINB, xsrc, hdst, xkey, hkey)
        stg = [sb("stg", [128, 512], BF16, S_OFF + 20480 + i * 1024) for i in range(4)]
        sc = 0
        for tt in range(8 if DBG_L1 >= 2 else 0):
            for g in range(8):
                b = next_bank()
                for j in range(2):
                    mm(b, 512, hT[:, 2 * g + j, tt * 128:(tt + 1) * 128], ccs[:, j, :], j == 0, j == 1,
                       reads=[hkey(2 * g + j, tt // 4), "ccs"])
                k = sc % 4
                sc += 1
                if sc % 2 == 0:
                    act(stg[k][:], psum[b][:], AF.Copy, reads=[("ps", b)], writes=[("stg", k)])
                else:
                    S.add("dve", lambda e, k=k, b=b: e.tensor_copy(stg[k][:], psum[b][:]),
                          reads=[("ps", b)], writes=[("stg", k)])
                for ab in range(2 if DBG_L1 >= 3 else 0):
                    dma("sp", abown_d[2 * g:2 * g + 2, ab, :, tt, :].rearrange("c p j -> p c j"),
                        stg[k][:, ab * 256:(ab + 1) * 256].rearrange("p (c j) -> p c j", c=2),
                        [("stg", k)], [("abown", pss, tt, g, ab)])

    x1done = [False]

    class _Stop(Exception):
        pass

    def check(k):
        if DEBUG_STOP is not None and k > DEBUG_STOP:
            raise _Stop()

    if mode == "p1":
        try:
            part1(xT_d, xh_d, abown_d, 0)
        except _Stop:
            pass
        if not x1done[0]:
            for q in range(4):
                dma("sp", x1_d[:, 4 * q:4 * q + 4, :], xT[:, 4 * q:4 * q + 4, :],
                    [("xT", kc, bk) for kc in range(4 * q, 4 * q + 4) for bk in (0, 1)], [("x1out", q)])
    elif mode == "fused":
        part1(xTo_d, xho_d, abfull_d[0], 0)
        part1(xT_d, xh_d, abfull_d[1], 1)
        ab_keys = [k for k in S.state if isinstance(k, tuple) and k[0] == "abown"]
        S.add("sp", None, reads=ab_keys, writes=[])

    if do2:
        if mode == "p2":
            load_xT(xT_d)
        Cs = sb("Cs", [128, 16, T], BF16, H_OFF)
        Ss = sb("Ss", [128, 16, T], BF16, S_OFF)
        hkeys = [hkey(kc, bk) for kc in range(KC) for bk in (0, 1, 2)]
        skeys2 = [k for k in S.state if isinstance(k, tuple) and k[0] in ("aT", "sg", "stg")]
        S.add("act", lambda e: e.activation(epst[:, 2:3], epst[:, 0:1], AF.Copy), reads=[],
              writes=[k for k in S.state if isinstance(k, tuple) and k[0] == "aT"] + ["YTfence"])
        for hh in range(2):
            dma("sp", Cs[:, 8 * hh:8 * hh + 8, :], csn_d[:, 0, 8 * hh:8 * hh + 8, :], [], hkeys + [("Cs", hh)])
            dma("sp", Ss[:, 8 * hh:8 * hh + 8, :], csn_d[:, 1, 8 * hh:8 * hh + 8, :], [], skeys2 + [("Ss", hh)])
        YT = mixT
        for ck in range(16):
            s = next_slot()
            sv = ring[s][:, :].rearrange("p (a r f) -> p a r f", a=2, r=2)
            for ab in range(2):
                dma("sp", sv[:, ab, :, :], abfull_d[:, ck, ab, :, :, :].rearrange("r p t j -> p r (t j)"),
                    [], [("ring", s)])
            for kb in range(2):
                b = next_bank()
                i = 0
                for ab in range(2):
                    Mx = Cs if ab == 0 else Ss
                    for st in range(16):
                        r, tt = st // 8, st % 8
                        mm(b, 512, sv[:, ab, r, tt * 128:(tt + 1) * 128], Mx[:, st, kb * 512:(kb + 1) * 512],
                           i == 0, i == 31,
                           reads=[("ring", s), ("Cs" if ab == 0 else "Ss", st // 8)])
                        i += 1
                act(YT[:, ck, kb * 512:(kb + 1) * 512], psum[b][:], AF.Copy, reads=[("ps", b)],
                    writes=[("YT", ck, kb)])
        for n in range(KC):
            s = next_slot()
            dma("pool", ring[s][:, 0:2048], wf_d[n], [], [("ring", s)])
            W = ring[s][:, 0:2048].rearrange("p (k n) -> p k n", k=KC)
            for half in range(2):
                b = next_bank()
                for kc in range(KC):
                    mm(b, 512, W[:, kc, :], YT[:, kc, half * 512:(half + 1) * 512], kc == 0, kc == KC - 1,
                       reads=[("ring", s), ("YT", kc, half)])
                dve_stt(xT[:, n, half * 512:(half + 1) * 512], psum[b][:], vcol(V_FNETB + n),
                        xT[:, n, half * 512:(half + 1) * 512], ALU.add, ALU.add,
                        reads=[("ps", b), "vec"], writes=[xkey(n, half)])
        ykeys = [("YT", ck, kb) for ck in range(16) for kb in range(2)] + [("Ss", 0), ("Ss", 1), ("Cs", 0), ("Cs", 1)]
        S.add("dve", lambda e: e.memset(epst[:, 3:4], 0.0), reads=[], writes=ykeys + ["Mreg"] + hkeys)
        ffn(1, wgu1_d, wd1_d)
        rmsnorm(V_FING, MAINB, xsrc, xsrc, xkey, xkey)
        okeys = []
        for q in range(4):
            dma("sp", out_d[:, 4 * q:4 * q + 4, :], xT[:, 4 * q:4 * q + 4, :],
                [("xT", kc, bk) for kc in range(4 * q, 4 * q + 4) for bk in (0, 1)], [("out", q)])
            okeys.append(("out", q))
        S.add("sp", None, reads=okeys, writes=[])
    else:
        okeys = [("x1out", q) for q in range(4)] + \
                [("abown", 0, tt, g, ab) for tt in range(8) for g in range(8) for ab in range(2)]
        okeys = [k for k in okeys if k in S.state]
        S.add("sp", None, reads=okeys, writes=[])

    with ExitStack() as stack:
        stack.enter_context(nc.allow_low_precision("bf16 matmul operands, fp32 accumulation"))
        S.finalize(nc, stack)
        with nc.Block() as block:
            @block.tensor
            def _(e):
                S.emit("pe", e)

            @block.scalar
            def _(e):
                S.emit("act", e)

            @block.vector
            def _(e):
                S.emit("dve", e)

            @block.gpsimd
            def _(e):
                S.emit("pool", e)

            @block.sync
            def _(e):
                S.emit("sp", e)
    return nc


def _pc(v):
    v = np.asarray(v, np.float32)
    return np.ascontiguousarray(v.reshape(-1, 128).T)


def _wtile(W, ncols_per_tile):
    K, N = W.shape
    kc = K // 128
    nt = N // ncols_per_tile
    a = W.reshape(kc, 128, nt, ncols_per_tile).transpose(2, 1, 0, 3)
    return np.ascontiguousarray(a).reshape(nt, 128, kc * ncols_per_tile)


_CONST_CACHE = {}


def _dft_consts():
    if "c" in _CONST_CACHE:
        return _CONST_CACHE["c"]
    bf = ml_dtypes.bfloat16
    c = np.arange(256, dtype=np.float64)
    th = 2 * np.pi * np.outer(c, c) / 256.0
    cc = (np.cos(th) / 16.0).reshape(2, 128, 256)
    sc = (np.sin(th) / 16.0).reshape(2, 128, 256)
    ccs = np.concatenate([cc, sc], axis=2).transpose(1, 0, 2)
    ccs = np.ascontiguousarray(ccs).astype(np.float32).astype(bf)
    csn = {}
    for half in range(2):
        for order in ("seq", "oo"):
            if order == "seq":
                s = np.arange(S_LEN, dtype=np.int64)
            else:
                oth = 1 - half
                s = np.concatenate([np.arange(T) + oth * T, np.arange(T) + half * T]).astype(np.int64)
            k = np.arange(T, dtype=np.int64) + half * T
            ph = (np.outer(s, k) % S_LEN).astype(np.float64) * (2 * np.pi / S_LEN)
            sc_ = 1.0 / math.sqrt(S_LEN)
            co = (np.cos(ph) * sc_).reshape(16, 128, T).transpose(1, 0, 2)
            si = (-np.sin(ph) * sc_).reshape(16, 128, T).transpose(1, 0, 2)
            a = np.stack([co, si], axis=1)
            csn[(half, order)] = np.ascontiguousarray(a).astype(np.float32).astype(bf)
    _CONST_CACHE["c"] = (ccs, csn)
    return ccs, csn


_NC_CACHE = {}


def _get_nc(mode):
    if mode not in _NC_CACHE:
        _NC_CACHE[mode] = build(mode)
    return _NC_CACHE[mode]


FUSED = True


def kernel(x, mix_norm_g, ffn_norm_g, final_norm_g, ab_w_in, conv_dw_w, conv_dw_b, conv_ln_g,
           conv_ln_b, sgu_ln_g, sgu_ln_b, sgu_w, sgu_b, ab_w_out, fnet_w_out, fnet_b_out,
           ffn_w_gate, ffn_w_up, ffn_w_down):
    f32 = np.float32
    x = np.asarray(x, f32)
    ccs, csn = _dft_consts()

    vec = np.zeros((128, NVEC), f32)
    vec[:, V_MIXG0:V_MIXG0 + 16] = _pc(mix_norm_g[0])
    vec[:, V_FFNG0:V_FFNG0 + 16] = _pc(ffn_norm_g[0])
    vec[:, V_MIXG1:V_MIXG1 + 16] = _pc(mix_norm_g[1])
    vec[:, V_FFNG1:V_FFNG1 + 16] = _pc(ffn_norm_g[1])
    vec[:, V_FING:V_FING + 16] = _pc(final_norm_g)
    vec[:, V_CONVB:V_CONVB + 8] = _pc(conv_dw_b[0])
    vec[:, V_CLNG:V_CLNG + 8] = _pc(conv_ln_g[0])
    vec[:, V_CLNB:V_CLNB + 8] = _pc(conv_ln_b[0])
    vec[:, V_FNETB:V_FNETB + 16] = _pc(fnet_b_out[0])
    cw = np.asarray(conv_dw_w[0], f32)
    vec[:, V_CONVW:V_CONVW + 248] = cw.reshape(31, 8, 128).transpose(2, 1, 0).reshape(128, 248)

    sgubc = np.concatenate([np.asarray(sgu_ln_g[0], f32), np.asarray(sgu_ln_b[0], f32),
                            np.asarray(sgu_b[0], f32).reshape(-1)])
    sgubc = np.ascontiguousarray(np.broadcast_to(sgubc[None, :], (128, 3072)))
    wsT = np.ascontiguousarray(np.asarray(sgu_w[0], f32).transpose(2, 0, 1)).reshape(128, 1024)

    w_in = np.asarray(ab_w_in[0], f32)
    ta = _wtile(w_in[:, 0:1024], 128).reshape(8, 128, 1, 2048)
    tg = _wtile(w_in[:, 1024:2048], 128).reshape(8, 128, 1, 2048)
    w_conv = np.ascontiguousarray(np.concatenate([ta, tg], axis=2)).reshape(8, 128, 4096)
    w_u = _wtile(w_in[:, 2048:3072], 128)
    w_v = _wtile(w_in[:, 3072:4096], 512)
    w_o = _wtile(np.asarray(ab_w_out[0], f32), 128)
    w_f = _wtile(np.asarray(fnet_w_out[0], f32), 128)

    def gu(l):
        a = _wtile(np.asarray(ffn_w_gate[l], f32), 128).reshape(FC, 128, 1, 2048)
        b = _wtile(np.asarray(ffn_w_up[l], f32), 128).reshape(FC, 128, 1, 2048)
        return np.ascontiguousarray(np.concatenate([a, b], axis=2)).reshape(FC, 128, 4096)

    def dn(l):
        W = np.asarray(ffn_w_down[l], f32)
        a = W.reshape(2, FH, 128, 16, 128).transpose(0, 3, 2, 1, 4)
        return np.ascontiguousarray(a).reshape(2, 16, 128, FH * 128)

    w_gu0, w_gu1, w_d0, w_d1 = gu(0), gu(1), dn(0), dn(1)

    xT_l, xh_l = [], []
    for c in range(8):
        b, half = c // 2, c % 2
        xs = x[b, half * T:(half + 1) * T, :]
        xT_l.append(np.ascontiguousarray(xs.T.reshape(KC, 128, T).transpose(1, 0, 2)))
        hal = np.zeros((32, D), f32)
        if half == 1:
            hal[0:HALO] = x[b, T - HALO:T, :]
        else:
            hal[HALO:2 * HALO] = x[b, T:T + HALO, :]
        xh_l.append(np.ascontiguousarray(hal.T.reshape(KC, 128, 32).transpose(1, 0, 2)))

    cores = list(range(8))
    if FUSED:
        nc = _get_nc("fused")
        in_maps = []
        for c in cores:
            o = c ^ 1
            in_maps.append({"xT": xT_l[c], "vec": vec, "xh": xh_l[c], "xTo": xT_l[o], "xho": xh_l[o],
                            "sgubc": sgubc, "wsT": wsT,
                            "w_conv": w_conv, "w_u": w_u, "w_v": w_v, "w_o": w_o, "w_gu0": w_gu0,
                            "w_d0": w_d0, "ccs": ccs, "csn": csn[(c % 2, "oo")], "w_f": w_f, "w_gu1": w_gu1,
                            "w_d1": w_d1})
        res = run_bass_kernel_spmd(nc, in_maps, core_ids=cores)
        outs = [r["outT"] for r in res.results]
    else:
        nc1 = _get_nc("p1")
        in_maps = []
        for c in cores:
            in_maps.append({"xT": xT_l[c], "vec": vec, "xh": xh_l[c], "sgubc": sgubc, "wsT": wsT,
                            "w_conv": w_conv, "w_u": w_u, "w_v": w_v, "w_o": w_o, "w_gu0": w_gu0,
                            "w_d0": w_d0, "ccs": ccs})
        r1 = run_bass_kernel_spmd(nc1, in_maps, core_ids=cores).results
        nc2 = _get_nc("p2")
        in_maps = []
        for c in cores:
            p = c - (c % 2)
            abf = np.ascontiguousarray(np.stack([r1[p]["ab_own"], r1[p + 1]["ab_own"]], axis=0))
            in_maps.append({"xT": r1[c]["x1T"], "vec": vec, "ab_full": abf, "csn": csn[(c % 2, "seq")],
                            "w_f": w_f, "w_gu1": w_gu1, "w_d1": w_d1})
        res = run_bass_kernel_spmd(nc2, in_maps, core_ids=cores)
        outs = [r["outT"] for r in res.results]

    out = np.empty((4, S_LEN, D), f32)
    for c in cores:
        b, half = c // 2, c % 2
        oT = np.asarray(outs[c], f32)
        out[b, half * T:(half + 1) * T, :] = oT.transpose(1, 0, 2).reshape(D, T).T
    return out
```

```python
import math
from contextlib import ExitStack

import numpy as np
import ml_dtypes

import concourse.bass as bass
import concourse.mybir as mybir
from concourse.bass_utils import run_bass_kernel_spmd

F32 = mybir.dt.float32
BF16 = mybir.dt.bfloat16
ALU = mybir.AluOpType
AF = mybir.ActivationFunctionType
AX = mybir.AxisListType

D = 2048
KC = 16
T = 1024
S_LEN = 2048
DFF = 5632
FC = 44
FH = 22
HALO = 15
TE = 1056
RMS_EPS = 1e-6
LN_EPS = 1e-5

V_MIXG0, V_FFNG0, V_MIXG1, V_FFNG1, V_FING = 0, 16, 32, 48, 64
V_CONVB, V_CLNG, V_CLNB = 80, 88, 96
V_FNETB = 104
V_SLNG, V_SLNB = 120, 128
V_CONVW = 136
NVEC = 136 + 248

SBUF_BASE = 16512
DEBUG_STOP = None
DBG_FC = None
DBG_DOWN = True
DBG_FH = 2
DBG_L1 = 3
SBUF_END = 229376 - 2048


class Op:
    __slots__ = ("eng", "idx", "fn", "deps", "dma", "need", "sig", "dsem", "dval", "uid")

    def __init__(self, eng, idx, fn, deps, dma, uid):
        self.eng, self.idx, self.fn, self.deps, self.dma = eng, idx, fn, deps, dma
        self.need = False
        self.sig = 0
        self.dsem = None
        self.dval = 0
        self.uid = uid


class Sched:
    ENGS = ("pe", "act", "dve", "pool", "sp")
    SEG = 2000
    ND = 6

    def __init__(self):
        self.ops = {e: [] for e in self.ENGS}
        self.state = {}
        self.uid = 0

    def add(self, eng, fn, reads=(), writes=(), dma=False):
        deps = {}
        wset = set(writes)
        for k in reads:
            if k in wset:
                continue
            st = self.state.get(k)
            if st is not None and st[0] is not None:
                deps[st[0].uid] = st[0]
        for k in wset:
            st = self.state.get(k)
            if st is not None:
                if st[0] is not None:
                    deps[st[0].uid] = st[0]
                for r in st[1].values():
                    deps[r.uid] = r
        self.uid += 1
        op = Op(eng, len(self.ops[eng]), fn, None, dma, self.uid)
        fd = []
        rset = set(reads)
        for d in deps.values():
            if d.dma or dma or d.eng != eng:
                fd.append(d)
            elif eng != "pe" and self._is_raw(d, rset):
                fd.append(d)
        op.deps = fd
        self.ops[eng].append(op)
        for k in reads:
            if k in wset:
                continue
            st = self.state.setdefault(k, [None, {}])
            st[1][("d", op.uid) if dma else eng] = op
        for k in wset:
            self.state[k] = [op, {}]
        return op

    def _is_raw(self, d, rset):
        for k in rset:
            st = self.state.get(k)
            if st is not None and st[0] is d:
                return True
        return False

    def finalize(self, nc, stack):
        for e in self.ENGS:
            for op in self.ops[e]:
                for d in op.deps:
                    d.need = True
        self.sems = {}
        for e in self.ENGS:
            n = 0
            nd = 0
            for op in self.ops[e]:
                if op.dma:
                    op.dsem = (e, nd % self.ND)
                    op.dval = 16 * (nd // self.ND + 1)
                    nd += 1
                elif op.need:
                    n += 1
                    op.sig = n
            nseg = (n + self.SEG - 1) // self.SEG
            for s in range(nseg):
                self.sems[(e, "c", s)] = stack.enter_context(nc.semaphore(f"s_{e}_{s}"))
            for s in range(min(nd, self.ND)):
                self.sems[(e, "d", s)] = stack.enter_context(nc.semaphore(f"d_{e}_{s}"))

    def emit(self, eng, e):
        w_sig = {x: 0 for x in self.ENGS}
        w_dma = {}
        for op in self.ops[eng]:
            waits = []
            if op.dma and op.dval > 16:
                key = (eng, "d", op.dsem[1])
                prev = op.dval - 16
                if w_dma.get(key, 0) < prev:
                    waits.append((self.sems[key], prev))
                    w_dma[key] = prev
            best = {}
            for d in op.deps:
                if d.dma:
                    key = (d.eng, "d", d.dsem[1])
                    if w_dma.get(key, 0) < d.dval:
                        waits.append((self.sems[key], d.dval))
                        w_dma[key] = d.dval
                elif d.sig > best.get(d.eng, 0):
                    best[d.eng] = d.sig
            for x, sg in best.items():
                if w_sig[x] < sg:
                    waits.append((self.sems[(x, "c", (sg - 1) // self.SEG)], (sg - 1) % self.SEG + 1))
                    w_sig[x] = sg
            for sem, val in waits:
                e.wait_ge(sem, val)
            if op.fn is None:
                continue
            ins = op.fn(e)
            if op.dma:
                ins.then_inc(self.sems[(eng, "d", op.dsem[1])], 16)
            elif op.need:
                ins.then_inc(self.sems[(eng, "c", (op.sig - 1) // self.SEG)], 1)


def build(mode):
    do1 = mode in ("p1", "fused")
    do2 = mode in ("p2", "fused")
    nc = bass.Bass("TRN2", target_bir_lowering=False)
    S = Sched()

    def din(name, shape, dt=F32):
        return nc.dram_tensor(name, list(shape), dt, kind="ExternalInput").ap()

    def dout(name, shape, dt=F32):
        return nc.dram_tensor(name, list(shape), dt, kind="ExternalOutput").ap()

    xT_d = din("xT", [128, KC, T])
    vec_d = din("vec", [128, NVEC])
    if do1:
        xh_d = din("xh", [128, KC, 32])
        sgubc_d = din("sgubc", [128, 3072])
        wsT_d = din("wsT", [128, 1024])
        wconv_d = din("w_conv", [8, 128, 4096])
        wdiag_d = din("w_diag", [8, 128, 31 * 128])
        wu_d = din("w_u", [8, 128, 2048])
        wv_d = din("w_v", [2, 128, 8192])
        wo_d = din("w_o", [16, 128, 2048])
        wgu0_d = din("w_gu0", [FC, 128, 4096])
        wd0_d = din("w_d0", [2, 16, 128, FH * 128])
        ccs_d = din("ccs", [128, 2, 512], BF16)
    if do2:
        csn_d = din("csn", [128, 2, 16, T], BF16)
        wf_d = din("w_f", [16, 128, 2048])
        wgu1_d = din("w_gu1", [FC, 128, 4096])
        wd1_d = din("w_d1", [2, 16, 128, FH * 128])
        out_d = dout("outT", [128, KC, T])
    if mode == "p1":
        x1_d = dout("x1T", [128, KC, T])
        abown_d = dout("ab_own", [16, 2, 128, 8, 128], BF16)
    elif mode == "p2":
        abfull_d = din("ab_full", [2, 16, 2, 128, 8, 128], BF16)
    else:
        xTo_d = din("xTo", [128, KC, T])
        xho_d = din("xho", [128, KC, 32])
        abfull_d = nc.dram_tensor("ab_full_i", [2, 16, 2, 128, 8, 128], BF16, kind="Internal").ap()

    cur = [SBUF_BASE]

    def region(nbytes):
        o = cur[0]
        cur[0] += (nbytes + 31) // 32 * 32
        assert cur[0] <= SBUF_END, ("sbuf overflow", cur[0])
        return o

    NRING = 3
    SLOT_B = 8192
    R_OFF = region(NRING * SLOT_B)
    VEC_OFF = region(NVEC * 4)
    ONES_OFF = region(3 * 256)
    CCS_OFF = region(2048)
    WST_OFF = region(2048)
    SQ_OFF = region(4 * 1024)
    RSTD_OFF = region(2 * 2048)
    STD_OFF = region(2 * 2048)
    EPS_OFF = region(64)
    X_OFF = region(KC * T * 4)
    H_OFF = region(KC * TE * 2)
    M_OFF = region(KC * T * 2)
    S_OFF = cur[0]
    S_SIZE = SBUF_END - S_OFF
    assert S_SIZE >= 32768 + 2048, S_SIZE

    cnt = [0]

    def sb(name, shape, dt, off):
        cnt[0] += 1
        return nc.alloc_sbuf_tensor_at(f"{name}{cnt[0]}", list(shape), dt, offset=off)

    xT = sb("xT", [128, KC, T], F32, X_OFF)
    hT = sb("hT", [128, KC, TE], BF16, H_OFF)
    mixT = sb("mixT", [128, KC, T], BF16, M_OFF)
    ring = [sb("ring", [128, 4096], BF16, R_OFF + i * SLOT_B) for i in range(NRING)]
    vec = sb("vec", [128, NVEC], F32, VEC_OFF)
    onesD = sb("onesD", [128, 128], BF16, ONES_OFF)
    onesG = sb("onesG", [128, 128], BF16, ONES_OFF + 256)
    ccs = sb("ccs", [128, 2, 512], BF16, CCS_OFF)
    wsT = sb("wsT", [128, 8, 128], BF16, WST_OFF)
    sqb = [sb("sq", [128, 512], BF16, SQ_OFF + i * 1024) for i in range(4)]
    rstdb = [sb("rstd", [128, 512], F32, RSTD_OFF + i * 2048) for i in range(2)]
    stdb = [sb("std", [128, 512], F32, STD_OFF + i * 2048) for i in range(2)]

    psum = [nc.alloc_psum_tensor(f"psb{i}", [128, 512], F32) for i in range(8)]
    bank_ctr = [0]

    def next_bank():
        b = bank_ctr[0] % 8
        bank_ctr[0] += 1
        return b

    ring_ctr = [0]

    def next_slot():
        s = ring_ctr[0] % NRING
        ring_ctr[0] += 1
        return s

    def vcol(c):
        return vec[:, c:c + 1]

    def mm(b, n, lhsT, rhs, start, stop, reads):
        out = psum[b][:, 0:n] if isinstance(n, int) else n
        S.add("pe", lambda e: e.matmul(out, lhsT, rhs, start=start, stop=stop),
              reads=reads, writes=[("ps", b)])

    def act(out, in_, func, reads, writes, bias=None, scale=None):
        kw = {}
        if bias is not None:
            kw["bias"] = bias
        if scale is not None:
            kw["scale"] = scale
        S.add("act", lambda e: e.activation(out, in_, func, **kw), reads=reads, writes=writes)

    def dve_tt(out, in0, in1, op, reads, writes):
        S.add("dve", lambda e: e.tensor_tensor(out, in0, in1, op), reads=reads, writes=writes)

    def dve_stt(out, in0, scalar, in1, op0, op1, reads, writes):
        S.add("dve", lambda e: e.scalar_tensor_tensor(out, in0, scalar, in1, op0, op1),
              reads=reads, writes=writes)

    def dve_ts(out, in0, s1, s2, op0, op1, reads, writes):
        S.add("dve", lambda e: e.tensor_scalar(out, in0, s1, s2, op0, op1), reads=reads, writes=writes)

    def dma(eng, out, in_, reads, writes):
        S.add(eng, lambda e: e.dma_start(out=out, in_=in_), reads=reads, writes=writes, dma=True)

    S.add("dve", lambda e: e.memset(onesD[:], 1.0 / D), writes=["onesD"])
    S.add("dve", lambda e: e.memset(onesG[:], 1.0 / 128.0), writes=["onesG"])
    dma("sp", vec[:], vec_d, [], ["vec"])

    norm_ctr = [0]

    def rmsnorm(gbase, blocks, src, dst, srckey, dstkey):
        for (c0, n, bk) in blocks:
            b = next_bank()
            i = norm_ctr[0] % 2
            norm_ctr[0] += 1
            for kc in range(KC):
                q = sqb[kc % 4]
                act(q[:, 0:n], src(kc, c0, n), AF.Square, reads=[srckey(kc, bk)], writes=[("sq", kc % 4)])
                mm(b, n, onesD[:], q[:, 0:n], kc == 0, kc == KC - 1, reads=[("sq", kc % 4), "onesD"])
            act(stdb[i][:, 0:n], psum[b][:, 0:n], AF.Sqrt, reads=[("ps", b)], writes=[("std", i)],
                bias=eps_rms[:, 0:1], scale=1.0)
            S.add("dve", lambda e, i=i, n=n: e.reciprocal(rstdb[i][:, 0:n], stdb[i][:, 0:n]),
                  reads=[("std", i)], writes=[("rstd", i)])
            for kc in range(KC):
                dve_stt(dst(kc, c0, n), src(kc, c0, n), vcol(gbase + kc), rstdb[i][:, 0:n],
                        ALU.mult, ALU.mult,
                        reads=[srckey(kc, bk), ("rstd", i), "vec"], writes=[dstkey(kc, bk)])

    epst = sb("eps", [128, 4], F32, EPS_OFF)
    eps_rms = epst[:, 0:1]
    eps_ln = epst[:, 1:2]
    S.add("dve", lambda e: e.memset(epst[:, 0:1], RMS_EPS), writes=["eps0"])
    S.add("dve", lambda e: e.memset(epst[:, 1:2], LN_EPS), writes=["eps1"])
    MAINB = [(0, 512, 0), (512, 512, 1)]

    def xsrc(kc, c0, n):
        return xT[:, kc, c0:c0 + n]

    def hdst(kc, c0, n):
        return hT[:, kc, c0:c0 + n]

    def xkey(kc, bk):
        return ("xT", kc, bk)

    def hkey(kc, bk):
        return ("hT", kc, bk)

    def load_xT(src_d):
        for q in range(4):
            dma("sp", xT[:, 4 * q:4 * q + 4, :], src_d[:, 4 * q:4 * q + 4, :], [],
                [("xT", kc, bk) for kc in range(4 * q, 4 * q + 4) for bk in (0, 1)])

    def wload(src_ap, ncols, extra_reads=()):
        s = next_slot()
        dma("pool", ring[s][:, 0:ncols], src_ap, list(extra_reads), [("ring", s)])
        return s

    def ffn(layer, wgu_d, wd_d):
        gb = V_FFNG0 if layer == 0 else V_FFNG1
        rmsnorm(gb, MAINB, xsrc, hdst, xkey, hkey)
        aT = sb("aT", [128, FH, T], BF16, M_OFF)
        sg = [sb("sg", [128, 512], F32, M_OFF + FH * T * 2 + i * 2048) for i in range(2)]
        sgc = 0
        for fh in range(DBG_FH):
            for fc in range(FH if DBG_FC is None else DBG_FC):
                f = fh * FH + fc
                s = wload(wgu_d[f], 4096)
                W = ring[s][:, :].rearrange("p (a k n) -> p a k n", a=2, k=KC)
                for half in range(2):
                    bg, bu = next_bank(), next_bank()
                    for kc in range(KC):
                        mm(bg, 512, W[:, 0, kc, :], hT[:, kc, half * 512:(half + 1) * 512], kc == 0, kc == KC - 1,
                           reads=[("ring", s), hkey(kc, half)])
                    for kc in range(KC):
                        mm(bu, 512, W[:, 1, kc, :], hT[:, kc, half * 512:(half + 1) * 512], kc == 0, kc == KC - 1,
                           reads=[("ring", s), hkey(kc, half)])
                    j = sgc % 2
                    sgc += 1
                    act(sg[j][:], psum[bg][:], AF.Silu, reads=[("ps", bg), "Mreg"], writes=[("sg", j)])
                    dve_tt(aT[:, fc, half * 512:(half + 1) * 512], sg[j][:], psum[bu][:], ALU.mult,
                           reads=[("sg", j), ("ps", bu)], writes=[("aT", fc, half)])
            for n in range(KC if DBG_DOWN else 0):
                s = wload(wd_d[fh, n], FH * 128)
                W = ring[s][:, 0:FH * 128].rearrange("p (k n) -> p k n", k=FH)
                for half in range(2):
                    b = next_bank()
                    for fc in range(FH):
                        mm(b, 512, W[:, fc, :], aT[:, fc, half * 512:(half + 1) * 512], fc == 0, fc == FH - 1,
                           reads=[("ring", s), ("aT", fc, half)])
                    dve_tt(xT[:, n, half * 512:(half + 1) * 512], psum[b][:], xT[:, n, half * 512:(half + 1) * 512],
                           ALU.add, reads=[("ps", b)], writes=[xkey(n, half)])

    def part1(xT_d, xh_d, abown_d, pss):
        load_xT(xT_d)
        check(1)
        xh = sb("xh", [128, KC, 32], F32, S_OFF)
        dma("sp", xh[:], xh_d, [], ["xh"])
        rmsnorm(V_MIXG0, MAINB, xsrc, hdst, xkey, hkey)
        rmsnorm(V_MIXG0, [(0, 32, 2)],
                lambda kc, c0, n: xh[:, kc, 0:32], lambda kc, c0, n: hT[:, kc, 1024:1056],
                lambda kc, bk: "xh", hkey)

        ax = [X_OFF]

        def aX(nbytes):
            o = ax[0]
            ax[0] += (nbytes + 31) // 32 * 32
            assert ax[0] <= X_OFF + KC * T * 4, "X arena overflow"
            return o

        as_ = [S_OFF]

        def aS(nbytes):
            o = as_[0]
            as_[0] += (nbytes + 31) // 32 * 32
            assert as_[0] <= SBUF_END, "S arena overflow"
            return o

        Wv = [sb("Wv", [128, KC, 512], BF16, aX(16384)) for _ in range(2)]
        glu = [sb("glu", [128, TE], BF16, aX(TE * 2)) for _ in range(2)]
        ycv = [sb("ycv", [128, T], F32, aX(T * 4)) for _ in range(2)]
        ybf = sb("ybf", [128, T], BF16, aX(T * 2))
        ysq = sb("ysq", [128, T], BF16, aX(T * 2))
        sig = [sb("sig", [128, 512], F32, aX(2048)) for _ in range(2)]
        m2 = sb("m2", [128, 512], F32, aX(2048))
        varb = sb("varb", [128, 512], F32, aX(2048))
        tcen = sb("tcen", [128, 512], F32, aX(2048))
        sgubc = sb("sgubc", [128, 3072], F32, aS(12288))
        vg = [sb("vg", [128, 1024], F32, aS(4096)) for _ in range(2)]
        vln = [sb("vln", [128, 1024], BF16, aS(2048)) for _ in range(2)]
        sptmp = sb("sptmp", [128, 512], F32, aS(2048))
        stats = sb("stats", [128, 2, 6], F32, aS(64))
        mv = sb("mv", [128, 2], F32, aS(32))
        sdv = sb("sdv", [128, 2], F32, aS(32))
        m2h = [m2, sb("m2b", [128, 512], F32, aS(2048))]
        varbh = [varb, sb("varbb", [128, 512], F32, aS(2048))]
        tcenh = [tcen, sb("tcenb", [128, 512], F32, aS(2048))]

        xdead = [xkey(kc, bk) for kc in range(KC) for bk in (0, 1)]
        S.add("dve", lambda e: e.memset(m2[:, 0:1], 0.0), reads=[], writes=xdead + ["arenaX"])

        dma("sp", sgubc[:], sgubc_d, [], ["sgubc", "xh"])
        dma("pool", wsT[:].rearrange("p a b -> p (a b)"), wsT_d, [], ["wsT"])

        check(2)
        poolA = [0, 1, 2, 3]
        poolB = [4, 5]
        poolC = [6, 7]
        pctr = {"A": 0, "B": 0, "C": 0}

        def pb(which):
            lst = {"A": poolA, "B": poolB, "C": poolC}[which]
            b_ = lst[pctr[which] % len(lst)]
            pctr[which] += 1
            return b_

        for nb in range(2):
            dma("pool", Wv[nb][:].rearrange("p k n -> p (k n)"), wv_d[nb], ["arenaX"], [("Wv", nb)])
        lng_bc = sgubc[:, 0:1024]
        lnb_bc = sgubc[:, 1024:2048]
        bs_bc = sgubc[:, 2048:3072].rearrange("p (h q) -> p h q", h=8)

        def conv_A(c):
            s = wload(wconv_d[c], 4096)
            W = ring[s][:, :].rearrange("p (a k n) -> p a k n", a=2, k=KC)
            g = glu[c % 2]
            gk = ("glu", c % 2)
            for (c0, n, bk) in [(0, 512, 0), (512, 512, 1), (1024, 32, 2)]:
                ba, bg = pb("A"), pb("A")
                for kc in range(KC):
                    mm(ba, n, W[:, 0, kc, :], hT[:, kc, c0:c0 + n], kc == 0, kc == KC - 1,
                       reads=[("ring", s), hkey(kc, bk)])
                for kc in range(KC):
                    mm(bg, n, W[:, 1, kc, :], hT[:, kc, c0:c0 + n], kc == 0, kc == KC - 1,
                       reads=[("ring", s), hkey(kc, bk)])
                j = bk % 2
                act(sig[j][:, 0:n], psum[bg][:, 0:n], AF.Sigmoid, reads=[("ps", bg), "arenaX"], writes=[("sig", j)])
                if bk < 2:
                    dve_tt(g[:, HALO + c0:HALO + c0 + n], psum[ba][:, 0:n], sig[j][:, 0:n], ALU.mult,
                           reads=[("ps", ba), ("sig", j), "arenaX"], writes=[gk + (bk,)])
                else:
                    dve_tt(g[:, 0:HALO], psum[ba][:, 0:HALO], sig[j][:, 0:HALO], ALU.mult,
                           reads=[("ps", ba), ("sig", j), "arenaX"], writes=[gk + (2,)])
                    dve_tt(g[:, HALO + T:HALO + T + HALO], psum[ba][:, HALO:2 * HALO], sig[j][:, HALO:2 * HALO],
                           ALU.mult, reads=[("ps", ba), ("sig", j)], writes=[gk + (3,)])

        def conv_B(c):
            g = glu[c % 2]
            gk = ("glu", c % 2)
            sd = wload(wdiag_d[c], 31 * 128)
            Dg = ring[sd][:, 0:31 * 128].rearrange("p (j n) -> p j n", j=31)
            gkeys = [gk + (i,) for i in range(4)]
            y = ycv[c % 2]
            yk = ("ycv", c % 2)
            for half in range(2):
                hs = slice(half * 512, (half + 1) * 512)
                by = pb("B")
                for j in range(31):
                    mm(by, 512, Dg[:, j, :], g[:, half * 512 + j:half * 512 + j + 512], j == 0, j == 30,
                       reads=[("ring", sd)] + gkeys)
                act(y[:, hs], psum[by][:], AF.Identity, reads=[("ps", by), "vec"], writes=[yk + (half,)],
                    bias=vcol(V_CONVB + c), scale=1.0)
                act(ybf[:, hs], psum[by][:], AF.Identity, reads=[("ps", by), "vec"], writes=[("ybf", half)],
                    bias=vcol(V_CONVB + c), scale=1.0)
                act(ysq[:, hs], psum[by][:], AF.Square, reads=[("ps", by), "vec"], writes=[("ysq", half)],
                    bias=vcol(V_CONVB + c), scale=1.0)

        def conv_C(c):
            y = ycv[c % 2]
            yk = ("ycv", c % 2)
            for half in range(2):
                hs = slice(half * 512, (half + 1) * 512)
                m2_, varb_, tcen_ = m2h[half], varbh[half], tcenh[half]
                bm, bq = pb("B"), pb("B")
                mm(bm, 512, onesG[:], ybf[:, hs], True, True, reads=[("ybf", half), "onesG"])
                mm(bq, 512, onesG[:], ysq[:, hs], True, True, reads=[("ysq", half), "onesG"])
                act(m2_[:], psum[bm][:], AF.Square, reads=[("ps", bm)], writes=[("m2", half)])
                dve_tt(varb_[:], psum[bq][:], m2_[:], ALU.subtract, reads=[("ps", bq), ("m2", half)],
                       writes=[("varb", half)])
                act(varb_[:], varb_[:], AF.Sqrt, reads=[("varb", half)], writes=[("varb", half)],
                    bias=eps_ln, scale=1.0)
                S.add("dve", lambda e, v_=varb_: e.reciprocal(v_[:], v_[:]), reads=[("varb", half)],
                      writes=[("varb", half)])
                dve_tt(tcen_[:], y[:, hs], psum[bm][:], ALU.subtract, reads=[yk + (half,), ("ps", bm)],
                       writes=[("tcen", half)])
                dve_tt(tcen_[:], tcen_[:], varb_[:], ALU.mult, reads=[("tcen", half), ("varb", half)],
                       writes=[("tcen", half)])
                act(mixT[:, c, hs], tcen_[:], AF.Silu, reads=[("tcen", half), "vec"], writes=[("mixT", c, half)],
                    bias=vcol(V_CLNB + c), scale=vcol(V_CLNG + c))

        def sgu_u(hc):
            s = wload(wu_d[hc], 2048)
            W = ring[s][:, 0:2048].rearrange("p (k n) -> p k n", k=KC)
            for half in range(2):
                b = pb("C")
                for kc in range(KC):
                    mm(b, 512, W[:, kc, :], hT[:, kc, half * 512:(half + 1) * 512], kc == 0, kc == KC - 1,
                       reads=[("ring", s), hkey(kc, half)])
                act(mixT[:, 8 + hc, half * 512:(half + 1) * 512], psum[b][:], AF.Gelu,
                    reads=[("ps", b)], writes=[("mixT", 8 + hc, half)])

        def sgu_v_ln(tt):
            v = vg[tt % 2]
            vk = ("vg", tt % 2)
            for nb in range(2):
                b = pb("C")
                for kc in range(KC):
                    mm(b, 512, hT[:, kc, tt * 128:(tt + 1) * 128], Wv[nb][:, kc, :], kc == 0, kc == KC - 1,
                       reads=[("Wv", nb), hkey(kc, tt // 4)])
                act(v[:, nb * 512:(nb + 1) * 512], psum[b][:], AF.Gelu, reads=[("ps", b)],
                    writes=[vk + (nb,)])
            for nb in range(2):
                S.add("dve", lambda e, v=v, nb=nb: e.bn_stats(stats[:, nb, :], v[:, nb * 512:(nb + 1) * 512]),
                      reads=[vk + (nb,)], writes=[("stats", nb)])
            S.add("dve", lambda e: e.bn_aggr(mv[:], stats[:].rearrange("p a b -> p (a b)")),
                  reads=[("stats", 0), ("stats", 1)], writes=["mv"])
            act(sdv[:, 0:1], mv[:, 1:2], AF.Sqrt, reads=["mv"], writes=["sdv0"], bias=eps_ln, scale=1.0)
            S.add("dve", lambda e: e.reciprocal(sdv[:, 1:2], sdv[:, 0:1]), reads=["sdv0"], writes=["sdv1"])
            dve_ts(v[:], v[:], mv[:, 0:1], sdv[:, 1:2], ALU.subtract, ALU.mult,
                   reads=[vk + (0,), vk + (1,), "mv", "sdv1"], writes=[vk + (0,), vk + (1,)])
            dve_tt(v[:], v[:], lng_bc, ALU.mult, reads=[vk + (0,), vk + (1,), "sgubc"],
                   writes=[vk + (0,), vk + (1,)])
            dve_tt(vln[tt % 2][:], v[:], lnb_bc, ALU.add, reads=[vk + (0,), vk + (1,), "sgubc"],
                   writes=[("vln", tt % 2)])

        def sgu_spatial(tt):
            vl = vln[tt % 2]
            vlk = ("vln", tt % 2)
            for hg in range(2):
                b = pb("C")
                for hh in range(4):
                    hd = hg * 4 + hh
                    mm(b, psum[b][:, hh * 128:(hh + 1) * 128], vl[:, hd * 128:(hd + 1) * 128], wsT[:, hd, :],
                       True, True, reads=[vlk, "wsT"])
                dve_tt(sptmp[:].rearrange("p (h q) -> p h q", h=4), psum[b][:].rearrange("p (h q) -> p h q", h=4),
                       bs_bc[:, hg * 4:hg * 4 + 4, :], ALU.add, reads=[("ps", b), "sgubc"], writes=["sptmp"])
                mo = mixT[:, 8 + hg * 4:8 + hg * 4 + 4, tt * 128:(tt + 1) * 128]
                dve_tt(mo, sptmp[:].rearrange("p (h q) -> p h q", h=4), mo, ALU.mult,
                       reads=["sptmp"] + [("mixT", 8 + hg * 4 + hh, tt // 4) for hh in range(4)],
                       writes=[("mixT", 8 + hg * 4 + hh, tt // 4) for hh in range(4)])

        for hc in range(8):
            sgu_u(hc)
        for c in range(8):
            conv_A(c)
            conv_B(c)
            conv_C(c)
            sgu_v_ln(c)
            if c >= 1:
                sgu_spatial(c - 1)
        sgu_spatial(7)

        check(5)
        arena_keys = [k for k in S.state if isinstance(k, tuple) and k[0] in
                      ("glu", "ycv", "sig", "Wv", "ybf", "ysq", "m2", "varb", "tcen")] + ["arenaX"]
        for q in range(4):
            dma("sp", xT[:, 4 * q:4 * q + 4, :], xT_d[:, 4 * q:4 * q + 4, :], [],
                arena_keys + [("xT", kc, bk) for kc in range(4 * q, 4 * q + 4) for bk in (0, 1)])
        for n in range(KC):
            s = wload(wo_d[n], 2048)
            W = ring[s][:, 0:2048].rearrange("p (k n) -> p k n", k=KC)
            for half in range(2):
                b = next_bank()
                for kc in range(KC):
                    mm(b, 512, W[:, kc, :], mixT[:, kc, half * 512:(half + 1) * 512], kc == 0, kc == KC - 1,
                       reads=[("ring", s), ("mixT", kc, half)])
                dve_tt(xT[:, n, half * 512:(half + 1) * 512], psum[b][:], xT[:, n, half * 512:(half + 1) * 512],
                       ALU.add, reads=[("ps", b)], writes=[xkey(n, half)])

        check(6)
        skeys = [k for k in S.state if isinstance(k, tuple) and k[0] in ("vg", "vln", "stats", "mixT", "m2", "varb", "tcen")] + \
                ["sgubc", "sptmp", "mv", "sdv0", "sdv1", "xh"]
        S.add("dve", lambda e: e.memset(epst[:, 2:3], 0.0), reads=[], writes=skeys + ["Mreg"])
        ffn(0, wgu0_d, wd0_d)

        check(7)
        if mode == "p1":
            for q in range(4):
                dma("sp", x1_d[:, 4 * q:4 * q + 4, :], xT[:, 4 * q:4 * q + 4, :],
                    [("xT", kc, bk) for kc in range(4 * q, 4 * q + 4) for bk in (0, 1)], [("x1out", q)])
            x1done[0] = True

        dma("sp", ccs[:], ccs_d, [], ["ccs"])
        rmsnorm(V_MIXG1, MAINB, xsrc, hdst, xkey, hkey)
        stg = [sb("stg", [128, 512], BF16, S_OFF + 20480 + i * 1024) for i in range(4)]
        sc = 0
        for tt in range(8 if DBG_L1 >= 2 else 0):
            for g in range(8):
                b = next_bank()
                for j in range(2):
                    mm(b, 512, hT[:, 2 * g + j, tt * 128:(tt + 1) * 128], ccs[:, j, :], j == 0, j == 1,
                       reads=[hkey(2 * g + j, tt // 4), "ccs"])
                k = sc % 4
                sc += 1
                if sc % 2 == 0:
                    act(stg[k][:], psum[b][:], AF.Copy, reads=[("ps", b)], writes=[("stg", k)])
                else:
                    S.add("dve", lambda e, k=k, b=b: e.tensor_copy(stg[k][:], psum[b][:]),
                          reads=[("ps", b)], writes=[("stg", k)])
                for ab in range(2 if DBG_L1 >= 3 else 0):
                    dma("sp", abown_d[2 * g:2 * g + 2, ab, :, tt, :].rearrange("c p j -> p c j"),
                        stg[k][:, ab * 256:(ab + 1) * 256].rearrange("p (c j) -> p c j", c=2),
                        [("stg", k)], [("abown", pss, tt, g, ab)])

    x1done = [False]

    class _Stop(Exception):
        pass

    def check(k):
        if DEBUG_STOP is not None and k > DEBUG_STOP:
            raise _Stop()

    if mode == "p1":
        try:
            part1(xT_d, xh_d, abown_d, 0)
        except _Stop:
            pass
        if not x1done[0]:
            for q in range(4):
                dma("sp", x1_d[:, 4 * q:4 * q + 4, :], xT[:, 4 * q:4 * q + 4, :],
                    [("xT", kc, bk) for kc in range(4 * q, 4 * q + 4) for bk in (0, 1)], [("x1out", q)])
    elif mode == "fused":
        part1(xTo_d, xho_d, abfull_d[0], 0)
        part1(xT_d, xh_d, abfull_d[1], 1)
        ab_keys = [k for k in S.state if isinstance(k, tuple) and k[0] == "abown"]
        S.add("sp", None, reads=ab_keys, writes=[])

    if do2:
        if mode == "p2":
            load_xT(xT_d)
        Cs = sb("Cs", [128, 16, T], BF16, H_OFF)
        Ss = sb("Ss", [128, 16, T], BF16, S_OFF)
        hkeys = [hkey(kc, bk) for kc in range(KC) for bk in (0, 1, 2)]
        skeys2 = [k for k in S.state if isinstance(k, tuple) and k[0] in ("aT", "sg", "stg")]
        S.add("act", lambda e: e.activation(epst[:, 2:3], epst[:, 0:1], AF.Copy), reads=[],
              writes=[k for k in S.state if isinstance(k, tuple) and k[0] == "aT"] + ["YTfence"])
        for hh in range(2):
            dma("sp", Cs[:, 8 * hh:8 * hh + 8, :], csn_d[:, 0, 8 * hh:8 * hh + 8, :], [], hkeys + [("Cs", hh)])
            dma("sp", Ss[:, 8 * hh:8 * hh + 8, :], csn_d[:, 1, 8 * hh:8 * hh + 8, :], [], skeys2 + [("Ss", hh)])
        YT = mixT
        for ck in range(16):
            s = next_slot()
            sv = ring[s][:, :].rearrange("p (a r f) -> p a r f", a=2, r=2)
            for ab in range(2):
                dma("sp", sv[:, ab, :, :], abfull_d[:, ck, ab, :, :, :].rearrange("r p t j -> p r (t j)"),
                    [], [("ring", s)])
            for kb in range(2):
                b = next_bank()
                i = 0
                for ab in range(2):
                    Mx = Cs if ab == 0 else Ss
                    for st in range(16):
                        r, tt = st // 8, st % 8
                        mm(b, 512, sv[:, ab, r, tt * 128:(tt + 1) * 128], Mx[:, st, kb * 512:(kb + 1) * 512],
                           i == 0, i == 31,
                           reads=[("ring", s), ("Cs" if ab == 0 else "Ss", st // 8)])
                        i += 1
                act(YT[:, ck, kb * 512:(kb + 1) * 512], psum[b][:], AF.Copy, reads=[("ps", b)],
                    writes=[("YT", ck, kb)])
        for n in range(KC):
            s = next_slot()
            dma("pool", ring[s][:, 0:2048], wf_d[n], [], [("ring", s)])
            W = ring[s][:, 0:2048].rearrange("p (k n) -> p k n", k=KC)
            for half in range(2):
                b = next_bank()
                for kc in range(KC):
                    mm(b, 512, W[:, kc, :], YT[:, kc, half * 512:(half + 1) * 512], kc == 0, kc == KC - 1,
                       reads=[("ring", s), ("YT", kc, half)])
                dve_stt(xT[:, n, half * 512:(half + 1) * 512], psum[b][:], vcol(V_FNETB + n),
                        xT[:, n, half * 512:(half + 1) * 512], ALU.add, ALU.add,
                        reads=[("ps", b), "vec"], writes=[xkey(n, half)])
        ykeys = [("YT", ck, kb) for ck in range(16) for kb in range(2)] + [("Ss", 0), ("Ss", 1), ("Cs", 0), ("Cs", 1)]
        S.add("dve", lambda e: e.memset(epst[:, 3:4], 0.0), reads=[], writes=ykeys + ["Mreg"] + hkeys)
        ffn(1, wgu1_d, wd1_d)
        rmsnorm(V_FING, MAINB, xsrc, xsrc, xkey, xkey)
        okeys = []
        for q in range(4):
            dma("sp", out_d[:, 4 * q:4 * q + 4, :], xT[:, 4 * q:4 * q + 4, :],
                [("xT", kc, bk) for kc in range(4 * q, 4 * q + 4) for bk in (0, 1)], [("out", q)])
            okeys.append(("out", q))
        S.add("sp", None, reads=okeys, writes=[])
    else:
        okeys = [("x1out", q) for q in range(4)] + \
                [("abown", 0, tt, g, ab) for tt in range(8) for g in range(8) for ab in range(2)]
        okeys = [k for k in okeys if k in S.state]
        S.add("sp", None, reads=okeys, writes=[])

    with ExitStack() as stack:
        stack.enter_context(nc.allow_low_precision("bf16 matmul operands, fp32 accumulation"))
        S.finalize(nc, stack)
        with nc.Block() as block:
            @block.tensor
            def _(e):
                S.emit("pe", e)

            @block.scalar
            def _(e):
                S.emit("act", e)

            @block.vector
            def _(e):
                S.emit("dve", e)

            @block.gpsimd
            def _(e):
                S.emit("pool", e)

            @block.sync
            def _(e):
                S.emit("sp", e)
    return nc


def _pc(v):
    v = np.asarray(v, np.float32)
    return np.ascontiguousarray(v.reshape(-1, 128).T)


def _wtile(W, ncols_per_tile):
    K, N = W.shape
    kc = K // 128
    nt = N // ncols_per_tile
    a = W.reshape(kc, 128, nt, ncols_per_tile).transpose(2, 1, 0, 3)
    return np.ascontiguousarray(a).reshape(nt, 128, kc * ncols_per_tile)


_CONST_CACHE = {}


def _dft_consts():
    if "c" in _CONST_CACHE:
        return _CONST_CACHE["c"]
    bf = ml_dtypes.bfloat16
    c = np.arange(256, dtype=np.float64)
    th = 2 * np.pi * np.outer(c, c) / 256.0
    cc = (np.cos(th) / 16.0).reshape(2, 128, 256)
    sc = (np.sin(th) / 16.0).reshape(2, 128, 256)
    ccs = np.concatenate([cc, sc], axis=2).transpose(1, 0, 2)
    ccs = np.ascontiguousarray(ccs).astype(np.float32).astype(bf)
    csn = {}
    for half in range(2):
        for order in ("seq", "oo"):
            if order == "seq":
                s = np.arange(S_LEN, dtype=np.int64)
            else:
                oth = 1 - half
                s = np.concatenate([np.arange(T) + oth * T, np.arange(T) + half * T]).astype(np.int64)
            k = np.arange(T, dtype=np.int64) + half * T
            ph = (np.outer(s, k) % S_LEN).astype(np.float64) * (2 * np.pi / S_LEN)
            sc_ = 1.0 / math.sqrt(S_LEN)
            co = (np.cos(ph) * sc_).reshape(16, 128, T).transpose(1, 0, 2)
            si = (-np.sin(ph) * sc_).reshape(16, 128, T).transpose(1, 0, 2)
            a = np.stack([co, si], axis=1)
            csn[(half, order)] = np.ascontiguousarray(a).astype(np.float32).astype(bf)
    _CONST_CACHE["c"] = (ccs, csn)
    return ccs, csn


_NC_CACHE = {}


def _get_nc(mode):
    if mode not in _NC_CACHE:
        _NC_CACHE[mode] = build(mode)
    return _NC_CACHE[mode]


FUSED = True


def kernel(x, mix_norm_g, ffn_norm_g, final_norm_g, ab_w_in, conv_dw_w, conv_dw_b, conv_ln_g,
           conv_ln_b, sgu_ln_g, sgu_ln_b, sgu_w, sgu_b, ab_w_out, fnet_w_out, fnet_b_out,
           ffn_w_gate, ffn_w_up, ffn_w_down):
    f32 = np.float32
    x = np.asarray(x, f32)
    ccs, csn = _dft_consts()

    vec = np.zeros((128, NVEC), f32)
    vec[:, V_MIXG0:V_MIXG0 + 16] = _pc(mix_norm_g[0])
    vec[:, V_FFNG0:V_FFNG0 + 16] = _pc(ffn_norm_g[0])
    vec[:, V_MIXG1:V_MIXG1 + 16] = _pc(mix_norm_g[1])
    vec[:, V_FFNG1:V_FFNG1 + 16] = _pc(ffn_norm_g[1])
    vec[:, V_FING:V_FING + 16] = _pc(final_norm_g)
    vec[:, V_CONVB:V_CONVB + 8] = _pc(conv_dw_b[0])
    vec[:, V_CLNG:V_CLNG + 8] = _pc(conv_ln_g[0])
    vec[:, V_CLNB:V_CLNB + 8] = _pc(conv_ln_b[0])
    vec[:, V_FNETB:V_FNETB + 16] = _pc(fnet_b_out[0])
    cw = np.asarray(conv_dw_w[0], f32)
    vec[:, V_CONVW:V_CONVW + 248] = cw.reshape(31, 8, 128).transpose(2, 1, 0).reshape(128, 248)

    sgubc = np.concatenate([np.asarray(sgu_ln_g[0], f32), np.asarray(sgu_ln_b[0], f32),
                            np.asarray(sgu_b[0], f32).reshape(-1)])
    sgubc = np.ascontiguousarray(np.broadcast_to(sgubc[None, :], (128, 3072)))
    wsT = np.ascontiguousarray(np.asarray(sgu_w[0], f32).transpose(2, 0, 1)).reshape(128, 1024)

    w_in = np.asarray(ab_w_in[0], f32)
    ta = _wtile(w_in[:, 0:1024], 128).reshape(8, 128, 1, 2048)
    tg = _wtile(w_in[:, 1024:2048], 128).reshape(8, 128, 1, 2048)
    w_conv = np.ascontiguousarray(np.concatenate([ta, tg], axis=2)).reshape(8, 128, 4096)
    w_diag = np.zeros((8, 128, 31, 128), f32)
    pi = np.arange(128)
    w_diag[:, pi, :, pi] = cw.reshape(31, 8, 128).transpose(2, 1, 0)
    w_diag = w_diag.reshape(8, 128, 31 * 128)
    w_u = _wtile(w_in[:, 2048:3072], 128)
    w_v = _wtile(w_in[:, 3072:4096], 512)
    w_o = _wtile(np.asarray(ab_w_out[0], f32), 128)
    w_f = _wtile(np.asarray(fnet_w_out[0], f32), 128)

    def gu(l):
        a = _wtile(np.asarray(ffn_w_gate[l], f32), 128).reshape(FC, 128, 1, 2048)
        b = _wtile(np.asarray(ffn_w_up[l], f32), 128).reshape(FC, 128, 1, 2048)
        return np.ascontiguousarray(np.concatenate([a, b], axis=2)).reshape(FC, 128, 4096)

    def dn(l):
        W = np.asarray(ffn_w_down[l], f32)
        a = W.reshape(2, FH, 128, 16, 128).transpose(0, 3, 2, 1, 4)
        return np.ascontiguousarray(a).reshape(2, 16, 128, FH * 128)

    w_gu0, w_gu1, w_d0, w_d1 = gu(0), gu(1), dn(0), dn(1)

    xT_l, xh_l = [], []
    for c in range(8):
        b, half = c // 2, c % 2
        xs = x[b, half * T:(half + 1) * T, :]
        xT_l.append(np.ascontiguousarray(xs.T.reshape(KC, 128, T).transpose(1, 0, 2)))
        hal = np.zeros((32, D), f32)
        if half == 1:
            hal[0:HALO] = x[b, T - HALO:T, :]
        else:
            hal[HALO:2 * HALO] = x[b, T:T + HALO, :]
        xh_l.append(np.ascontiguousarray(hal.T.reshape(KC, 128, 32).transpose(1, 0, 2)))

    cores = list(range(8))
    if FUSED:
        nc = _get_nc("fused")
        in_maps = []
        for c in cores:
            o = c ^ 1
            in_maps.append({"xT": xT_l[c], "vec": vec, "xh": xh_l[c], "xTo": xT_l[o], "xho": xh_l[o],
                            "sgubc": sgubc, "wsT": wsT,
                            "w_conv": w_conv, "w_diag": w_diag, "w_u": w_u, "w_v": w_v, "w_o": w_o, "w_gu0": w_gu0,
                            "w_d0": w_d0, "ccs": ccs, "csn": csn[(c % 2, "oo")], "w_f": w_f, "w_gu1": w_gu1,
                            "w_d1": w_d1})
        res = run_bass_kernel_spmd(nc, in_maps, core_ids=cores)
        outs = [r["outT"] for r in res.results]
    else:
        nc1 = _get_nc("p1")
        in_maps = []
        for c in cores:
            in_maps.append({"xT": xT_l[c], "vec": vec, "xh": xh_l[c], "sgubc": sgubc, "wsT": wsT,
                            "w_conv": w_conv, "w_diag": w_diag, "w_u": w_u, "w_v": w_v, "w_o": w_o, "w_gu0": w_gu0,
                            "w_d0": w_d0, "ccs": ccs})
        r1 = run_bass_kernel_spmd(nc1, in_maps, core_ids=cores).results
        nc2 = _get_nc("p2")
        in_maps = []
        for c in cores:
            p = c - (c % 2)
            abf = np.ascontiguousarray(np.stack([r1[p]["ab_own"], r1[p + 1]["ab_own"]], axis=0))
            in_maps.append({"xT": r1[c]["x1T"], "vec": vec, "ab_full": abf, "csn": csn[(c % 2, "seq")],
                            "w_f": w_f, "w_gu1": w_gu1, "w_d1": w_d1})
        res = run_bass_kernel_spmd(nc2, in_maps, core_ids=cores)
        outs = [r["outT"] for r in res.results]

    out = np.empty((4, S_LEN, D), f32)
    for c in cores:
        b, half = c // 2, c % 2
        oT = np.asarray(outs[c], f32)
        out[b, half * T:(half + 1) * T, :] = oT.transpose(1, 0, 2).reshape(D, T).T
    return out
```

```python
import math
from contextlib import ExitStack

import numpy as np
import ml_dtypes

import concourse.bass as bass
import concourse.mybir as mybir
from concourse.bass_utils import run_bass_kernel_spmd

F32 = mybir.dt.float32
BF16 = mybir.dt.bfloat16
ALU = mybir.AluOpType
AF = mybir.ActivationFunctionType
AX = mybir.AxisListType

D = 2048
KC = 16
T = 1024
S_LEN = 2048
DFF = 5632
FC = 44
FH = 22
HALO = 15
TE = 1056
RMS_EPS = 1e-6
LN_EPS = 1e-5

V_MIXG0, V_FFNG0, V_MIXG1, V_FFNG1, V_FING = 0, 16, 32, 48, 64
V_CONVB, V_CLNG, V_CLNB = 80, 88, 96
V_FNETB = 104
V_SLNG, V_SLNB = 120, 128
V_CONVW = 136
NVEC = 136 + 248

SBUF_BASE = 16512
DEBUG_STOP = None
DBG_FC = None
DBG_DOWN = True
DBG_FH = 2
DBG_L1 = 3
MIX_ORDER = 0
MIX_POOLS = 0
SBUF_END = 229376 - 2048


class Op:
    __slots__ = ("eng", "idx", "fn", "deps", "dma", "need", "sig", "dsem", "dval", "uid")

    def __init__(self, eng, idx, fn, deps, dma, uid):
        self.eng, self.idx, self.fn, self.deps, self.dma = eng, idx, fn, deps, dma
        self.need = False
        self.sig = 0
        self.dsem = None
        self.dval = 0
        self.uid = uid


class Sched:
    ENGS = ("pe", "act", "dve", "pool", "sp")
    SEG = 2000
    ND = 6

    def __init__(self):
        self.ops = {e: [] for e in self.ENGS}
        self.state = {}
        self.uid = 0

    def add(self, eng, fn, reads=(), writes=(), dma=False):
        deps = {}
        wset = set(writes)
        for k in reads:
            if k in wset:
                continue
            st = self.state.get(k)
            if st is not None and st[0] is not None:
                deps[st[0].uid] = st[0]
        for k in wset:
            st = self.state.get(k)
            if st is not None:
                if st[0] is not None:
                    deps[st[0].uid] = st[0]
                for r in st[1].values():
                    deps[r.uid] = r
        self.uid += 1
        op = Op(eng, len(self.ops[eng]), fn, None, dma, self.uid)
        fd = []
        rset = set(reads)
        for d in deps.values():
            if d.dma or dma or d.eng != eng:
                fd.append(d)
            elif eng != "pe" and self._is_raw(d, rset):
                fd.append(d)
        op.deps = fd
        self.ops[eng].append(op)
        for k in reads:
            if k in wset:
                continue
            st = self.state.setdefault(k, [None, {}])
            st[1][("d", op.uid) if dma else eng] = op
        for k in wset:
            self.state[k] = [op, {}]
        return op

    def _is_raw(self, d, rset):
        for k in rset:
            st = self.state.get(k)
            if st is not None and st[0] is d:
                return True
        return False

    def finalize(self, nc, stack):
        for e in self.ENGS:
            for op in self.ops[e]:
                for d in op.deps:
                    d.need = True
        self.sems = {}
        for e in self.ENGS:
            n = 0
            nd = 0
            for op in self.ops[e]:
                if op.dma:
                    op.dsem = (e, nd % self.ND)
                    op.dval = 16 * (nd // self.ND + 1)
                    nd += 1
                elif op.need:
                    n += 1
                    op.sig = n
            nseg = (n + self.SEG - 1) // self.SEG
            for s in range(nseg):
                self.sems[(e, "c", s)] = stack.enter_context(nc.semaphore(f"s_{e}_{s}"))
            for s in range(min(nd, self.ND)):
                self.sems[(e, "d", s)] = stack.enter_context(nc.semaphore(f"d_{e}_{s}"))

    def emit(self, eng, e):
        w_sig = {x: 0 for x in self.ENGS}
        w_dma = {}
        for op in self.ops[eng]:
            waits = []
            if op.dma and op.dval > 16:
                key = (eng, "d", op.dsem[1])
                prev = op.dval - 16
                if w_dma.get(key, 0) < prev:
                    waits.append((self.sems[key], prev))
                    w_dma[key] = prev
            best = {}
            for d in op.deps:
                if d.dma:
                    key = (d.eng, "d", d.dsem[1])
                    if w_dma.get(key, 0) < d.dval:
                        waits.append((self.sems[key], d.dval))
                        w_dma[key] = d.dval
                elif d.sig > best.get(d.eng, 0):
                    best[d.eng] = d.sig
            for x, sg in best.items():
                if w_sig[x] < sg:
                    waits.append((self.sems[(x, "c", (sg - 1) // self.SEG)], (sg - 1) % self.SEG + 1))
                    w_sig[x] = sg
            for sem, val in waits:
                e.wait_ge(sem, val)
            if op.fn is None:
                continue
            ins = op.fn(e)
            if op.dma:
                ins.then_inc(self.sems[(eng, "d", op.dsem[1])], 16)
            elif op.need:
                ins.then_inc(self.sems[(eng, "c", (op.sig - 1) // self.SEG)], 1)


def build(mode):
    do1 = mode in ("p1", "fused")
    do2 = mode in ("p2", "fused")
    nc = bass.Bass("TRN2", target_bir_lowering=False)
    S = Sched()

    def din(name, shape, dt=F32):
        return nc.dram_tensor(name, list(shape), dt, kind="ExternalInput").ap()

    def dout(name, shape, dt=F32):
        return nc.dram_tensor(name, list(shape), dt, kind="ExternalOutput").ap()

    xT_d = din("xT", [128, KC, T])
    vec_d = din("vec", [128, NVEC])
    if do1:
        xh_d = din("xh", [128, KC, 32])
        sgubc_d = din("sgubc", [128, 3072])
        wsT_d = din("wsT", [128, 1024])
        wconv_d = din("w_conv", [8, 128, 4096])
        wdiag_d = din("w_diag", [8, 128, 31 * 128])
        wu_d = din("w_u", [8, 128, 2048])
        wv_d = din("w_v", [2, 128, 8192])
        wo_d = din("w_o", [16, 128, 2048])
        wgu0_d = din("w_gu0", [FC, 128, 4096])
        wd0_d = din("w_d0", [2, 16, 128, FH * 128])
        ccs_d = din("ccs", [128, 2, 512], BF16)
    if do2:
        csn_d = din("csn", [128, 2, 16, T], BF16)
        wf_d = din("w_f", [16, 128, 2048])
        wgu1_d = din("w_gu1", [FC, 128, 4096])
        wd1_d = din("w_d1", [2, 16, 128, FH * 128])
        out_d = dout("outT", [128, KC, T])
    if mode == "p1":
        x1_d = dout("x1T", [128, KC, T])
        abown_d = dout("ab_own", [16, 2, 128, 8, 128], BF16)
    elif mode == "p2":
        abfull_d = din("ab_full", [2, 16, 2, 128, 8, 128], BF16)
    else:
        xTo_d = din("xTo", [128, KC, T])
        xho_d = din("xho", [128, KC, 32])
        abfull_d = nc.dram_tensor("ab_full_i", [2, 16, 2, 128, 8, 128], BF16, kind="Internal").ap()

    cur = [SBUF_BASE]

    def region(nbytes):
        o = cur[0]
        cur[0] += (nbytes + 31) // 32 * 32
        assert cur[0] <= SBUF_END, ("sbuf overflow", cur[0])
        return o

    NRING = 3
    SLOT_B = 8192
    R_OFF = region(NRING * SLOT_B)
    VEC_OFF = region(NVEC * 4)
    ONES_OFF = region(3 * 256)
    CCS_OFF = region(2048)
    WST_OFF = region(2048)
    SQ_OFF = region(4 * 1024)
    RSTD_OFF = region(2 * 2048)
    STD_OFF = region(2 * 2048)
    EPS_OFF = region(64)
    X_OFF = region(KC * T * 4)
    H_OFF = region(KC * TE * 2)
    M_OFF = region(KC * T * 2)
    S_OFF = cur[0]
    S_SIZE = SBUF_END - S_OFF
    assert S_SIZE >= 32768 + 2048, S_SIZE

    cnt = [0]

    def sb(name, shape, dt, off):
        cnt[0] += 1
        return nc.alloc_sbuf_tensor_at(f"{name}{cnt[0]}", list(shape), dt, offset=off)

    xT = sb("xT", [128, KC, T], F32, X_OFF)
    hT = sb("hT", [128, KC, TE], BF16, H_OFF)
    mixT = sb("mixT", [128, KC, T], BF16, M_OFF)
    ring = [sb("ring", [128, 4096], BF16, R_OFF + i * SLOT_B) for i in range(NRING)]
    vec = sb("vec", [128, NVEC], F32, VEC_OFF)
    onesD = sb("onesD", [128, 128], BF16, ONES_OFF)
    onesG = sb("onesG", [128, 128], BF16, ONES_OFF + 256)
    ccs = sb("ccs", [128, 2, 512], BF16, CCS_OFF)
    wsT = sb("wsT", [128, 8, 128], BF16, WST_OFF)
    sqb = [sb("sq", [128, 512], BF16, SQ_OFF + i * 1024) for i in range(4)]
    rstdb = [sb("rstd", [128, 512], F32, RSTD_OFF + i * 2048) for i in range(2)]
    stdb = [sb("std", [128, 512], F32, STD_OFF + i * 2048) for i in range(2)]

    psum = [nc.alloc_psum_tensor(f"psb{i}", [128, 512], F32) for i in range(8)]
    bank_ctr = [0]

    def next_bank():
        b = bank_ctr[0] % 8
        bank_ctr[0] += 1
        return b

    ring_ctr = [0]

    def next_slot():
        s = ring_ctr[0] % NRING
        ring_ctr[0] += 1
        return s

    def vcol(c):
        return vec[:, c:c + 1]

    def mm(b, n, lhsT, rhs, start, stop, reads):
        out = psum[b][:, 0:n] if isinstance(n, int) else n
        S.add("pe", lambda e: e.matmul(out, lhsT, rhs, start=start, stop=stop),
              reads=reads, writes=[("ps", b)])

    def act(out, in_, func, reads, writes, bias=None, scale=None):
        kw = {}
        if bias is not None:
            kw["bias"] = bias
        if scale is not None:
            kw["scale"] = scale
        S.add("act", lambda e: e.activation(out, in_, func, **kw), reads=reads, writes=writes)

    def dve_tt(out, in0, in1, op, reads, writes):
        S.add("dve", lambda e: e.tensor_tensor(out, in0, in1, op), reads=reads, writes=writes)

    def dve_stt(out, in0, scalar, in1, op0, op1, reads, writes):
        S.add("dve", lambda e: e.scalar_tensor_tensor(out, in0, scalar, in1, op0, op1),
              reads=reads, writes=writes)

    def dve_ts(out, in0, s1, s2, op0, op1, reads, writes):
        S.add("dve", lambda e: e.tensor_scalar(out, in0, s1, s2, op0, op1), reads=reads, writes=writes)

    def dma(eng, out, in_, reads, writes):
        S.add(eng, lambda e: e.dma_start(out=out, in_=in_), reads=reads, writes=writes, dma=True)

    S.add("dve", lambda e: e.memset(onesD[:], 1.0 / D), writes=["onesD"])
    S.add("dve", lambda e: e.memset(onesG[:], 1.0 / 128.0), writes=["onesG"])
    dma("sp", vec[:], vec_d, [], ["vec"])

    norm_ctr = [0]

    def rmsnorm(gbase, blocks, src, dst, srckey, dstkey):
        for (c0, n, bk) in blocks:
            b = next_bank()
            i = norm_ctr[0] % 2
            norm_ctr[0] += 1
            for kc in range(KC):
                q = sqb[kc % 4]
                act(q[:, 0:n], src(kc, c0, n), AF.Square, reads=[srckey(kc, bk)], writes=[("sq", kc % 4)])
                mm(b, n, onesD[:], q[:, 0:n], kc == 0, kc == KC - 1, reads=[("sq", kc % 4), "onesD"])
            act(stdb[i][:, 0:n], psum[b][:, 0:n], AF.Sqrt, reads=[("ps", b)], writes=[("std", i)],
                bias=eps_rms[:, 0:1], scale=1.0)
            S.add("dve", lambda e, i=i, n=n: e.reciprocal(rstdb[i][:, 0:n], stdb[i][:, 0:n]),
                  reads=[("std", i)], writes=[("rstd", i)])
            for kc in range(KC):
                dve_stt(dst(kc, c0, n), src(kc, c0, n), vcol(gbase + kc), rstdb[i][:, 0:n],
                        ALU.mult, ALU.mult,
                        reads=[srckey(kc, bk), ("rstd", i), "vec"], writes=[dstkey(kc, bk)])

    epst = sb("eps", [128, 4], F32, EPS_OFF)
    eps_rms = epst[:, 0:1]
    eps_ln = epst[:, 1:2]
    S.add("dve", lambda e: e.memset(epst[:, 0:1], RMS_EPS), writes=["eps0"])
    S.add("dve", lambda e: e.memset(epst[:, 1:2], LN_EPS), writes=["eps1"])
    MAINB = [(0, 512, 0), (512, 512, 1)]

    def xsrc(kc, c0, n):
        return xT[:, kc, c0:c0 + n]

    def hdst(kc, c0, n):
        return hT[:, kc, c0:c0 + n]

    def xkey(kc, bk):
        return ("xT", kc, bk)

    def hkey(kc, bk):
        return ("hT", kc, bk)

    def load_xT(src_d):
        for q in range(4):
            dma("sp", xT[:, 4 * q:4 * q + 4, :], src_d[:, 4 * q:4 * q + 4, :], [],
                [("xT", kc, bk) for kc in range(4 * q, 4 * q + 4) for bk in (0, 1)])

    def wload(src_ap, ncols, extra_reads=()):
        s = next_slot()
        dma("pool", ring[s][:, 0:ncols], src_ap, list(extra_reads), [("ring", s)])
        return s

    def ffn(layer, wgu_d, wd_d):
        gb = V_FFNG0 if layer == 0 else V_FFNG1
        rmsnorm(gb, MAINB, xsrc, hdst, xkey, hkey)
        aT = sb("aT", [128, FH, T], BF16, M_OFF)
        sg = [sb("sg", [128, 512], F32, M_OFF + FH * T * 2 + i * 2048) for i in range(2)]
        sgc = 0
        for fh in range(DBG_FH):
            for fc in range(FH if DBG_FC is None else DBG_FC):
                f = fh * FH + fc
                s = wload(wgu_d[f], 4096)
                W = ring[s][:, :].rearrange("p (a k n) -> p a k n", a=2, k=KC)
                for half in range(2):
                    bg, bu = next_bank(), next_bank()
                    for kc in range(KC):
                        mm(bg, 512, W[:, 0, kc, :], hT[:, kc, half * 512:(half + 1) * 512], kc == 0, kc == KC - 1,
                           reads=[("ring", s), hkey(kc, half)])
                    for kc in range(KC):
                        mm(bu, 512, W[:, 1, kc, :], hT[:, kc, half * 512:(half + 1) * 512], kc == 0, kc == KC - 1,
                           reads=[("ring", s), hkey(kc, half)])
                    j = sgc % 2
                    sgc += 1
                    act(sg[j][:], psum[bg][:], AF.Silu, reads=[("ps", bg), "Mreg"], writes=[("sg", j)])
                    dve_tt(aT[:, fc, half * 512:(half + 1) * 512], sg[j][:], psum[bu][:], ALU.mult,
                           reads=[("sg", j), ("ps", bu)], writes=[("aT", fc, half)])
            for n in range(KC if DBG_DOWN else 0):
                s = wload(wd_d[fh, n], FH * 128)
                W = ring[s][:, 0:FH * 128].rearrange("p (k n) -> p k n", k=FH)
                for half in range(2):
                    b = next_bank()
                    for fc in range(FH):
                        mm(b, 512, W[:, fc, :], aT[:, fc, half * 512:(half + 1) * 512], fc == 0, fc == FH - 1,
                           reads=[("ring", s), ("aT", fc, half)])
                    dve_tt(xT[:, n, half * 512:(half + 1) * 512], psum[b][:], xT[:, n, half * 512:(half + 1) * 512],
                           ALU.add, reads=[("ps", b)], writes=[xkey(n, half)])

    def part1(xT_d, xh_d, abown_d, pss):
        load_xT(xT_d)
        check(1)
        xh = sb("xh", [128, KC, 32], F32, S_OFF)
        dma("sp", xh[:], xh_d, [], ["xh"])
        rmsnorm(V_MIXG0, MAINB, xsrc, hdst, xkey, hkey)
        rmsnorm(V_MIXG0, [(0, 32, 2)],
                lambda kc, c0, n: xh[:, kc, 0:32], lambda kc, c0, n: hT[:, kc, 1024:1056],
                lambda kc, bk: "xh", hkey)

        ax = [X_OFF]

        def aX(nbytes):
            o = ax[0]
            ax[0] += (nbytes + 31) // 32 * 32
            assert ax[0] <= X_OFF + KC * T * 4, "X arena overflow"
            return o

        as_ = [S_OFF]

        def aS(nbytes):
            o = as_[0]
            as_[0] += (nbytes + 31) // 32 * 32
            assert as_[0] <= SBUF_END, "S arena overflow"
            return o

        Wv = [sb("Wv", [128, KC, 512], BF16, aX(16384)) for _ in range(2)]
        glu = [sb("glu", [128, TE], BF16, aX(TE * 2)) for _ in range(2)]
        ycv = [sb("ycv", [128, T], F32, aX(T * 4)) for _ in range(2)]
        ybf = sb("ybf", [128, T], BF16, aX(T * 2))
        ysq = sb("ysq", [128, T], BF16, aX(T * 2))
        sig = [sb("sig", [128, 512], F32, aX(2048)) for _ in range(2)]
        m2 = sb("m2", [128, 512], F32, aX(2048))
        varb = sb("varb", [128, 512], F32, aX(2048))
        tcen = sb("tcen", [128, 512], F32, aX(2048))
        sgubc = sb("sgubc", [128, 3072], F32, aS(12288))
        vg = [sb("vg", [128, 1024], F32, aS(4096)) for _ in range(2)]
        vln = [sb("vln", [128, 1024], BF16, aS(2048)) for _ in range(2)]
        sptmp = sb("sptmp", [128, 512], F32, aS(2048))
        stats = sb("stats", [128, 2, 6], F32, aS(64))
        mv = sb("mv", [128, 2], F32, aS(32))
        sdv = sb("sdv", [128, 2], F32, aS(32))
        m2h = [m2, sb("m2b", [128, 512], F32, aS(2048))]
        varbh = [varb, sb("varbb", [128, 512], F32, aS(2048))]
        tcenh = [tcen, sb("tcenb", [128, 512], F32, aS(2048))]

        xdead = [xkey(kc, bk) for kc in range(KC) for bk in (0, 1)]
        S.add("dve", lambda e: e.memset(m2[:, 0:1], 0.0), reads=[], writes=xdead + ["arenaX"])

        dma("sp", sgubc[:], sgubc_d, [], ["sgubc", "xh"])
        dma("pool", wsT[:].rearrange("p a b -> p (a b)"), wsT_d, [], ["wsT"])

        check(2)
        if MIX_POOLS == 0:
            pools = {"A": [0, 1, 2, 3], "B": [4, 5], "Bs": [4, 5], "C": [6, 7]}
            pools["Bs"] = pools["B"]
        elif MIX_POOLS == 1:
            pools = {"A": [0, 1], "B": [2, 3], "Bs": [4, 5], "C": [6, 7]}
        else:
            pools = {"A": [0, 1, 2, 3], "B": [4], "Bs": [5, 6], "C": [7]}
        pctr = {"A": 0, "B": 0, "Bs": 0, "C": 0}
        if pools["Bs"] is pools["B"]:
            pctr_alias = {"Bs": "B"}
        else:
            pctr_alias = {}

        def pb(which):
            lst = pools[which]
            which = pctr_alias.get(which, which)
            b_ = lst[pctr[which] % len(lst)]
            pctr[which] += 1
            return b_

        for nb in range(2):
            dma("pool", Wv[nb][:].rearrange("p k n -> p (k n)"), wv_d[nb], ["arenaX"], [("Wv", nb)])
        lng_bc = sgubc[:, 0:1024]
        lnb_bc = sgubc[:, 1024:2048]
        bs_bc = sgubc[:, 2048:3072].rearrange("p (h q) -> p h q", h=8)

        def conv_A(c):
            s = wload(wconv_d[c], 4096)
            W = ring[s][:, :].rearrange("p (a k n) -> p a k n", a=2, k=KC)
            g = glu[c % 2]
            gk = ("glu", c % 2)
            for (c0, n, bk) in [(0, 512, 0), (512, 512, 1), (1024, 32, 2)]:
                ba, bg = pb("A"), pb("A")
                for kc in range(KC):
                    mm(ba, n, W[:, 0, kc, :], hT[:, kc, c0:c0 + n], kc == 0, kc == KC - 1,
                       reads=[("ring", s), hkey(kc, bk)])
                for kc in range(KC):
                    mm(bg, n, W[:, 1, kc, :], hT[:, kc, c0:c0 + n], kc == 0, kc == KC - 1,
                       reads=[("ring", s), hkey(kc, bk)])
                j = bk % 2
                act(sig[j][:, 0:n], psum[bg][:, 0:n], AF.Sigmoid, reads=[("ps", bg), "arenaX"], writes=[("sig", j)])
                if bk < 2:
                    dve_tt(g[:, HALO + c0:HALO + c0 + n], psum[ba][:, 0:n], sig[j][:, 0:n], ALU.mult,
                           reads=[("ps", ba), ("sig", j), "arenaX"], writes=[gk + (bk,)])
                else:
                    dve_tt(g[:, 0:HALO], psum[ba][:, 0:HALO], sig[j][:, 0:HALO], ALU.mult,
                           reads=[("ps", ba), ("sig", j), "arenaX"], writes=[gk + (2,)])
                    dve_tt(g[:, HALO + T:HALO + T + HALO], psum[ba][:, HALO:2 * HALO], sig[j][:, HALO:2 * HALO],
                           ALU.mult, reads=[("ps", ba), ("sig", j)], writes=[gk + (3,)])

        def conv_B(c):
            g = glu[c % 2]
            gk = ("glu", c % 2)
            sd = wload(wdiag_d[c], 31 * 128)
            Dg = ring[sd][:, 0:31 * 128].rearrange("p (j n) -> p j n", j=31)
            gkeys = [gk + (i,) for i in range(4)]
            y = ycv[c % 2]
            yk = ("ycv", c % 2)
            for half in range(2):
                hs = slice(half * 512, (half + 1) * 512)
                by = pb("B")
                for j in range(31):
                    mm(by, 512, Dg[:, j, :], g[:, half * 512 + j:half * 512 + j + 512], j == 0, j == 30,
                       reads=[("ring", sd)] + gkeys)
                act(y[:, hs], psum[by][:], AF.Identity, reads=[("ps", by), "vec"], writes=[yk + (half,)],
                    bias=vcol(V_CONVB + c), scale=1.0)
                act(ybf[:, hs], psum[by][:], AF.Identity, reads=[("ps", by), "vec"], writes=[("ybf", half)],
                    bias=vcol(V_CONVB + c), scale=1.0)
                act(ysq[:, hs], psum[by][:], AF.Square, reads=[("ps", by), "vec"], writes=[("ysq", half)],
                    bias=vcol(V_CONVB + c), scale=1.0)

        def conv_C(c):
            y = ycv[c % 2]
            yk = ("ycv", c % 2)
            for half in range(2):
                hs = slice(half * 512, (half + 1) * 512)
                m2_, varb_, tcen_ = m2h[half], varbh[half], tcenh[half]
                bm, bq = pb("Bs"), pb("Bs")
                mm(bm, 512, onesG[:], ybf[:, hs], True, True, reads=[("ybf", half), "onesG"])
                mm(bq, 512, onesG[:], ysq[:, hs], True, True, reads=[("ysq", half), "onesG"])
                act(m2_[:], psum[bm][:], AF.Square, reads=[("ps", bm)], writes=[("m2", half)])
                dve_tt(varb_[:], psum[bq][:], m2_[:], ALU.subtract, reads=[("ps", bq), ("m2", half)],
                       writes=[("varb", half)])
                act(varb_[:], varb_[:], AF.Sqrt, reads=[("varb", half)], writes=[("varb", half)],
                    bias=eps_ln, scale=1.0)
                S.add("dve", lambda e, v_=varb_: e.reciprocal(v_[:], v_[:]), reads=[("varb", half)],
                      writes=[("varb", half)])
                dve_tt(tcen_[:], y[:, hs], psum[bm][:], ALU.subtract, reads=[yk + (half,), ("ps", bm)],
                       writes=[("tcen", half)])
                dve_tt(tcen_[:], tcen_[:], varb_[:], ALU.mult, reads=[("tcen", half), ("varb", half)],
                       writes=[("tcen", half)])
                act(mixT[:, c, hs], tcen_[:], AF.Silu, reads=[("tcen", half), "vec"], writes=[("mixT", c, half)],
                    bias=vcol(V_CLNB + c), scale=vcol(V_CLNG + c))

        def sgu_u(hc):
            s = wload(wu_d[hc], 2048)
            W = ring[s][:, 0:2048].rearrange("p (k n) -> p k n", k=KC)
            for half in range(2):
                b = pb("C")
                for kc in range(KC):
                    mm(b, 512, W[:, kc, :], hT[:, kc, half * 512:(half + 1) * 512], kc == 0, kc == KC - 1,
                       reads=[("ring", s), hkey(kc, half)])
                act(mixT[:, 8 + hc, half * 512:(half + 1) * 512], psum[b][:], AF.Gelu,
                    reads=[("ps", b)], writes=[("mixT", 8 + hc, half)])

        def sgu_v_ln(tt):
            v = vg[tt % 2]
            vk = ("vg", tt % 2)
            for nb in range(2):
                b = pb("C")
                for kc in range(KC):
                    mm(b, 512, hT[:, kc, tt * 128:(tt + 1) * 128], Wv[nb][:, kc, :], kc == 0, kc == KC - 1,
                       reads=[("Wv", nb), hkey(kc, tt // 4)])
                act(v[:, nb * 512:(nb + 1) * 512], psum[b][:], AF.Gelu, reads=[("ps", b)],
                    writes=[vk + (nb,)])
            for nb in range(2):
                S.add("dve", lambda e, v=v, nb=nb: e.bn_stats(stats[:, nb, :], v[:, nb * 512:(nb + 1) * 512]),
                      reads=[vk + (nb,)], writes=[("stats", nb)])
            S.add("dve", lambda e: e.bn_aggr(mv[:], stats[:].rearrange("p a b -> p (a b)")),
                  reads=[("stats", 0), ("stats", 1)], writes=["mv"])
            act(sdv[:, 0:1], mv[:, 1:2], AF.Sqrt, reads=["mv"], writes=["sdv0"], bias=eps_ln, scale=1.0)
            S.add("dve", lambda e: e.reciprocal(sdv[:, 1:2], sdv[:, 0:1]), reads=["sdv0"], writes=["sdv1"])
            dve_ts(v[:], v[:], mv[:, 0:1], sdv[:, 1:2], ALU.subtract, ALU.mult,
                   reads=[vk + (0,), vk + (1,), "mv", "sdv1"], writes=[vk + (0,), vk + (1,)])
            dve_tt(v[:], v[:], lng_bc, ALU.mult, reads=[vk + (0,), vk + (1,), "sgubc"],
                   writes=[vk + (0,), vk + (1,)])
            dve_tt(vln[tt % 2][:], v[:], lnb_bc, ALU.add, reads=[vk + (0,), vk + (1,), "sgubc"],
                   writes=[("vln", tt % 2)])

        def sgu_spatial(tt):
            vl = vln[tt % 2]
            vlk = ("vln", tt % 2)
            for hg in range(2):
                b = pb("C")
                for hh in range(4):
                    hd = hg * 4 + hh
                    mm(b, psum[b][:, hh * 128:(hh + 1) * 128], vl[:, hd * 128:(hd + 1) * 128], wsT[:, hd, :],
                       True, True, reads=[vlk, "wsT"])
                dve_tt(sptmp[:].rearrange("p (h q) -> p h q", h=4), psum[b][:].rearrange("p (h q) -> p h q", h=4),
                       bs_bc[:, hg * 4:hg * 4 + 4, :], ALU.add, reads=[("ps", b), "sgubc"], writes=["sptmp"])
                mo = mixT[:, 8 + hg * 4:8 + hg * 4 + 4, tt * 128:(tt + 1) * 128]
                dve_tt(mo, sptmp[:].rearrange("p (h q) -> p h q", h=4), mo, ALU.mult,
                       reads=["sptmp"] + [("mixT", 8 + hg * 4 + hh, tt // 4) for hh in range(4)],
                       writes=[("mixT", 8 + hg * 4 + hh, tt // 4) for hh in range(4)])

        for hc in range(8):
            sgu_u(hc)
        if MIX_ORDER == 0:
            for c in range(8):
                conv_A(c)
                conv_B(c)
                conv_C(c)
                sgu_v_ln(c)
                if c >= 1:
                    sgu_spatial(c - 1)
            sgu_spatial(7)
        elif MIX_ORDER == 1:
            for i in range(10):
                if i < 8:
                    conv_A(i)
                    sgu_v_ln(i)
                if 2 <= i:
                    conv_C(i - 2)
                if 1 <= i <= 8:
                    conv_B(i - 1)
                    sgu_spatial(i - 1)
        else:
            for i in range(9):
                if i < 8:
                    conv_A(i)
                if 1 <= i:
                    conv_B(i - 1)
                if i < 8:
                    sgu_v_ln(i)
                if 1 <= i:
                    conv_C(i - 1)
                    sgu_spatial(i - 1)

        check(5)
        arena_keys = [k for k in S.state if isinstance(k, tuple) and k[0] in
                      ("glu", "ycv", "sig", "Wv", "ybf", "ysq", "m2", "varb", "tcen")] + ["arenaX"]
        for q in range(4):
            dma("sp", xT[:, 4 * q:4 * q + 4, :], xT_d[:, 4 * q:4 * q + 4, :], [],
                arena_keys + [("xT", kc, bk) for kc in range(4 * q, 4 * q + 4) for bk in (0, 1)])
        for n in range(KC):
            s = wload(wo_d[n], 2048)
            W = ring[s][:, 0:2048].rearrange("p (k n) -> p k n", k=KC)
            for half in range(2):
                b = next_bank()
                for kc in range(KC):
                    mm(b, 512, W[:, kc, :], mixT[:, kc, half * 512:(half + 1) * 512], kc == 0, kc == KC - 1,
                       reads=[("ring", s), ("mixT", kc, half)])
                dve_tt(xT[:, n, half * 512:(half + 1) * 512], psum[b][:], xT[:, n, half * 512:(half + 1) * 512],
                       ALU.add, reads=[("ps", b)], writes=[xkey(n, half)])

        check(6)
        skeys = [k for k in S.state if isinstance(k, tuple) and k[0] in ("vg", "vln", "stats", "mixT", "m2", "varb", "tcen")] + \
                ["sgubc", "sptmp", "mv", "sdv0", "sdv1", "xh"]
        S.add("dve", lambda e: e.memset(epst[:, 2:3], 0.0), reads=[], writes=skeys + ["Mreg"])
        ffn(0, wgu0_d, wd0_d)

        check(7)
        if mode == "p1":
            for q in range(4):
                dma("sp", x1_d[:, 4 * q:4 * q + 4, :], xT[:, 4 * q:4 * q + 4, :],
                    [("xT", kc, bk) for kc in range(4 * q, 4 * q + 4) for bk in (0, 1)], [("x1out", q)])
            x1done[0] = True

        dma("sp", ccs[:], ccs_d, [], ["ccs"])
        rmsnorm(V_MIXG1, MAINB, xsrc, hdst, xkey, hkey)
        stg = [sb("stg", [128, 2, 16, 128], BF16, S_OFF + 16384 + i * 8192) for i in range(2)]
        sc = 0
        for tt in range(8 if DBG_L1 >= 2 else 0):
            k = tt % 2
            for g in range(8):
                b = next_bank()
                for j in range(2):
                    mm(b, 512, hT[:, 2 * g + j, tt * 128:(tt + 1) * 128], ccs[:, j, :], j == 0, j == 1,
                       reads=[hkey(2 * g + j, tt // 4), "ccs"])
                so = stg[k][:, :, 2 * g:2 * g + 2, :]
                pi = psum[b][:].rearrange("p (a c j) -> p a c j", a=2, c=2)
                sc += 1
                if sc % 2 == 0:
                    act(so, pi, AF.Copy, reads=[("ps", b)], writes=[("stg", k, g)])
                else:
                    S.add("dve", lambda e, so=so, pi=pi: e.tensor_copy(so, pi),
                          reads=[("ps", b)], writes=[("stg", k, g)])
            for ab in range(2 if DBG_L1 >= 3 else 0):
                dma("sp", abown_d[:, ab, :, tt, :].rearrange("k p j -> p k j"), stg[k][:, ab, :, :],
                    [("stg", k, g) for g in range(8)], [("abown", pss, tt, ab)])

    x1done = [False]

    class _Stop(Exception):
        pass

    def check(k):
        if DEBUG_STOP is not None and k > DEBUG_STOP:
            raise _Stop()

    if mode == "p1":
        try:
            part1(xT_d, xh_d, abown_d, 0)
        except _Stop:
            pass
        if not x1done[0]:
            for q in range(4):
                dma("sp", x1_d[:, 4 * q:4 * q + 4, :], xT[:, 4 * q:4 * q + 4, :],
                    [("xT", kc, bk) for kc in range(4 * q, 4 * q + 4) for bk in (0, 1)], [("x1out", q)])
    elif mode == "fused":
        part1(xTo_d, xho_d, abfull_d[0], 0)
        part1(xT_d, xh_d, abfull_d[1], 1)
        ab_keys = [k for k in S.state if isinstance(k, tuple) and k[0] == "abown"]
        S.add("sp", None, reads=ab_keys, writes=[])

    if do2:
        if mode == "p2":
            load_xT(xT_d)
        Cs = sb("Cs", [128, 16, T], BF16, H_OFF)
        Ss = sb("Ss", [128, 16, T], BF16, S_OFF)
        hkeys = [hkey(kc, bk) for kc in range(KC) for bk in (0, 1, 2)]
        skeys2 = [k for k in S.state if isinstance(k, tuple) and k[0] in ("aT", "sg", "stg")]
        S.add("act", lambda e: e.activation(epst[:, 2:3], epst[:, 0:1], AF.Copy), reads=[],
              writes=[k for k in S.state if isinstance(k, tuple) and k[0] == "aT"] + ["YTfence"])
        for hh in range(2):
            dma("sp", Cs[:, 8 * hh:8 * hh + 8, :], csn_d[:, 0, 8 * hh:8 * hh + 8, :], [], hkeys + [("Cs", hh)])
            dma("sp", Ss[:, 8 * hh:8 * hh + 8, :], csn_d[:, 1, 8 * hh:8 * hh + 8, :], [], skeys2 + [("Ss", hh)])
        YT = mixT
        for ck in range(16):
            s = next_slot()
            sv = ring[s][:, :].rearrange("p (a r f) -> p a r f", a=2, r=2)
            for ab in range(2):
                dma("sp", sv[:, ab, :, :], abfull_d[:, ck, ab, :, :, :].rearrange("r p t j -> p r (t j)"),
                    [], [("ring", s)])
            for kb in range(2):
                b = next_bank()
                i = 0
                for ab in range(2):
                    Mx = Cs if ab == 0 else Ss
                    for st in range(16):
                        r, tt = st // 8, st % 8
                        mm(b, 512, sv[:, ab, r, tt * 128:(tt + 1) * 128], Mx[:, st, kb * 512:(kb + 1) * 512],
                           i == 0, i == 31,
                           reads=[("ring", s), ("Cs" if ab == 0 else "Ss", st // 8)])
                        i += 1
                act(YT[:, ck, kb * 512:(kb + 1) * 512], psum[b][:], AF.Copy, reads=[("ps", b)],
                    writes=[("YT", ck, kb)])
        for n in range(KC):
            s = next_slot()
            dma("pool", ring[s][:, 0:2048], wf_d[n], [], [("ring", s)])
            W = ring[s][:, 0:2048].rearrange("p (k n) -> p k n", k=KC)
            for half in range(2):
                b = next_bank()
                for kc in range(KC):
                    mm(b, 512, W[:, kc, :], YT[:, kc, half * 512:(half + 1) * 512], kc == 0, kc == KC - 1,
                       reads=[("ring", s), ("YT", kc, half)])
                dve_stt(xT[:, n, half * 512:(half + 1) * 512], psum[b][:], vcol(V_FNETB + n),
                        xT[:, n, half * 512:(half + 1) * 512], ALU.add, ALU.add,
                        reads=[("ps", b), "vec"], writes=[xkey(n, half)])
        ykeys = [("YT", ck, kb) for ck in range(16) for kb in range(2)] + [("Ss", 0), ("Ss", 1), ("Cs", 0), ("Cs", 1)]
        S.add("dve", lambda e: e.memset(epst[:, 3:4], 0.0), reads=[], writes=ykeys + ["Mreg"] + hkeys)
        ffn(1, wgu1_d, wd1_d)
        rmsnorm(V_FING, MAINB, xsrc, xsrc, xkey, xkey)
        okeys = []
        for bk in range(2):
            for q in range(2):
                dma("sp", out_d[:, 8 * q:8 * q + 8, bk * 512:(bk + 1) * 512],
                    xT[:, 8 * q:8 * q + 8, bk * 512:(bk + 1) * 512],
                    [("xT", kc, bk) for kc in range(8 * q, 8 * q + 8)], [("out", bk, q)])
                okeys.append(("out", bk, q))
        S.add("sp", None, reads=okeys, writes=[])
    else:
        okeys = [("x1out", q) for q in range(4)] + \
                [("abown", 0, tt, ab) for tt in range(8) for ab in range(2)]
        okeys = [k for k in okeys if k in S.state]
        S.add("sp", None, reads=okeys, writes=[])

    with ExitStack() as stack:
        stack.enter_context(nc.allow_low_precision("bf16 matmul operands, fp32 accumulation"))
        S.finalize(nc, stack)
        with nc.Block() as block:
            @block.tensor
            def _(e):
                S.emit("pe", e)

            @block.scalar
            def _(e):
                S.emit("act", e)

            @block.vector
            def _(e):
                S.emit("dve", e)

            @block.gpsimd
            def _(e):
                S.emit("pool", e)

            @block.sync
            def _(e):
                S.emit("sp", e)
    return nc


def _pc(v):
    v = np.asarray(v, np.float32)
    return np.ascontiguousarray(v.reshape(-1, 128).T)


def _wtile(W, ncols_per_tile):
    K, N = W.shape
    kc = K // 128
    nt = N // ncols_per_tile
    a = W.reshape(kc, 128, nt, ncols_per_tile).transpose(2, 1, 0, 3)
    return np.ascontiguousarray(a).reshape(nt, 128, kc * ncols_per_tile)


_CONST_CACHE = {}


def _dft_consts():
    if "c" in _CONST_CACHE:
        return _CONST_CACHE["c"]
    bf = ml_dtypes.bfloat16
    c = np.arange(256, dtype=np.float64)
    th = 2 * np.pi * np.outer(c, c) / 256.0
    cc = (np.cos(th) / 16.0).reshape(2, 128, 256)
    sc = (np.sin(th) / 16.0).reshape(2, 128, 256)
    ccs = np.concatenate([cc, sc], axis=2).transpose(1, 0, 2)
    ccs = np.ascontiguousarray(ccs).astype(np.float32).astype(bf)
    csn = {}
    for half in range(2):
        for order in ("seq", "oo"):
            if order == "seq":
                s = np.arange(S_LEN, dtype=np.int64)
            else:
                oth = 1 - half
                s = np.concatenate([np.arange(T) + oth * T, np.arange(T) + half * T]).astype(np.int64)
            k = np.arange(T, dtype=np.int64) + half * T
            ph = (np.outer(s, k) % S_LEN).astype(np.float64) * (2 * np.pi / S_LEN)
            sc_ = 1.0 / math.sqrt(S_LEN)
            co = (np.cos(ph) * sc_).reshape(16, 128, T).transpose(1, 0, 2)
            si = (-np.sin(ph) * sc_).reshape(16, 128, T).transpose(1, 0, 2)
            a = np.stack([co, si], axis=1)
            csn[(half, order)] = np.ascontiguousarray(a).astype(np.float32).astype(bf)
    _CONST_CACHE["c"] = (ccs, csn)
    return ccs, csn


_NC_CACHE = {}


def _get_nc(mode):
    if mode not in _NC_CACHE:
        _NC_CACHE[mode] = build(mode)
    return _NC_CACHE[mode]


FUSED = True


def kernel(x, mix_norm_g, ffn_norm_g, final_norm_g, ab_w_in, conv_dw_w, conv_dw_b, conv_ln_g,
           conv_ln_b, sgu_ln_g, sgu_ln_b, sgu_w, sgu_b, ab_w_out, fnet_w_out, fnet_b_out,
           ffn_w_gate, ffn_w_up, ffn_w_down):
    f32 = np.float32
    x = np.asarray(x, f32)
    ccs, csn = _dft_consts()

    vec = np.zeros((128, NVEC), f32)
    vec[:, V_MIXG0:V_MIXG0 + 16] = _pc(mix_norm_g[0])
    vec[:, V_FFNG0:V_FFNG0 + 16] = _pc(ffn_norm_g[0])
    vec[:, V_MIXG1:V_MIXG1 + 16] = _pc(mix_norm_g[1])
    vec[:, V_FFNG1:V_FFNG1 + 16] = _pc(ffn_norm_g[1])
    vec[:, V_FING:V_FING + 16] = _pc(final_norm_g)
    vec[:, V_CONVB:V_CONVB + 8] = _pc(conv_dw_b[0])
    vec[:, V_CLNG:V_CLNG + 8] = _pc(conv_ln_g[0])
    vec[:, V_CLNB:V_CLNB + 8] = _pc(conv_ln_b[0])
    vec[:, V_FNETB:V_FNETB + 16] = _pc(fnet_b_out[0])
    cw = np.asarray(conv_dw_w[0], f32)
    vec[:, V_CONVW:V_CONVW + 248] = cw.reshape(31, 8, 128).transpose(2, 1, 0).reshape(128, 248)

    sgubc = np.concatenate([np.asarray(sgu_ln_g[0], f32), np.asarray(sgu_ln_b[0], f32),
                            np.asarray(sgu_b[0], f32).reshape(-1)])
    sgubc = np.ascontiguousarray(np.broadcast_to(sgubc[None, :], (128, 3072)))
    wsT = np.ascontiguousarray(np.asarray(sgu_w[0], f32).transpose(2, 0, 1)).reshape(128, 1024)

    w_in = np.asarray(ab_w_in[0], f32)
    ta = _wtile(w_in[:, 0:1024], 128).reshape(8, 128, 1, 2048)
    tg = _wtile(w_in[:, 1024:2048], 128).reshape(8, 128, 1, 2048)
    w_conv = np.ascontiguousarray(np.concatenate([ta, tg], axis=2)).reshape(8, 128, 4096)
    w_diag = np.zeros((8, 128, 31, 128), f32)
    pi = np.arange(128)
    w_diag[:, pi, :, pi] = cw.reshape(31, 8, 128).transpose(2, 1, 0)
    w_diag = w_diag.reshape(8, 128, 31 * 128)
    w_u = _wtile(w_in[:, 2048:3072], 128)
    w_v = _wtile(w_in[:, 3072:4096], 512)
    w_o = _wtile(np.asarray(ab_w_out[0], f32), 128)
    w_f = _wtile(np.asarray(fnet_w_out[0], f32), 128)

    def gu(l):
        a = _wtile(np.asarray(ffn_w_gate[l], f32), 128).reshape(FC, 128, 1, 2048)
        b = _wtile(np.asarray(ffn_w_up[l], f32), 128).reshape(FC, 128, 1, 2048)
        return np.ascontiguousarray(np.concatenate([a, b], axis=2)).reshape(FC, 128, 4096)

    def dn(l):
        W = np.asarray(ffn_w_down[l], f32)
        a = W.reshape(2, FH, 128, 16, 128).transpose(0, 3, 2, 1, 4)
        return np.ascontiguousarray(a).reshape(2, 16, 128, FH * 128)

    w_gu0, w_gu1, w_d0, w_d1 = gu(0), gu(1), dn(0), dn(1)

    xT_l, xh_l = [], []
    for c in range(8):
        b, half = c // 2, c % 2
        xs = x[b, half * T:(half + 1) * T, :]
        xT_l.append(np.ascontiguousarray(xs.T.reshape(KC, 128, T).transpose(1, 0, 2)))
        hal = np.zeros((32, D), f32)
        if half == 1:
            hal[0:HALO] = x[b, T - HALO:T, :]
        else:
            hal[HALO:2 * HALO] = x[b, T:T + HALO, :]
        xh_l.append(np.ascontiguousarray(hal.T.reshape(KC, 128, 32).transpose(1, 0, 2)))

    cores = list(range(8))
    if FUSED:
        nc = _get_nc("fused")
        in_maps = []
        for c in cores:
            o = c ^ 1
            in_maps.append({"xT": xT_l[c], "vec": vec, "xh": xh_l[c], "xTo": xT_l[o], "xho": xh_l[o],
                            "sgubc": sgubc, "wsT": wsT,
                            "w_conv": w_conv, "w_diag": w_diag, "w_u": w_u, "w_v": w_v, "w_o": w_o, "w_gu0": w_gu0,
                            "w_d0": w_d0, "ccs": ccs, "csn": csn[(c % 2, "oo")], "w_f": w_f, "w_gu1": w_gu1,
                            "w_d1": w_d1})
        res = run_bass_kernel_spmd(nc, in_maps, core_ids=cores)
        outs = [r["outT"] for r in res.results]
    else:
        nc1 = _get_nc("p1")
        in_maps = []
        for c in cores:
            in_maps.append({"xT": xT_l[c], "vec": vec, "xh": xh_l[c], "sgubc": sgubc, "wsT": wsT,
                            "w_conv": w_conv, "w_diag": w_diag, "w_u": w_u, "w_v": w_v, "w_o": w_o, "w_gu0": w_gu0,
                            "w_d0": w_d0, "ccs": ccs})
        r1 = run_bass_kernel_spmd(nc1, in_maps, core_ids=cores).results
        nc2 = _get_nc("p2")
        in_maps = []
        for c in cores:
            p = c - (c % 2)
            abf = np.ascontiguousarray(np.stack([r1[p]["ab_own"], r1[p + 1]["ab_own"]], axis=0))
            in_maps.append({"xT": r1[c]["x1T"], "vec": vec, "ab_full": abf, "csn": csn[(c % 2, "seq")],
                            "w_f": w_f, "w_gu1": w_gu1, "w_d1": w_d1})
        res = run_bass_kernel_spmd(nc2, in_maps, core_ids=cores)
        outs = [r["outT"] for r in res.results]

    out = np.empty((4, S_LEN, D), f32)
    for c in cores:
        b, half = c // 2, c % 2
        oT = np.asarray(outs[c], f32)
        out[b, half * T:(half + 1) * T, :] = oT.transpose(1, 0, 2).reshape(D, T).T
    return out
```

```python
import math
from contextlib import ExitStack

import numpy as np
import ml_dtypes

import concourse.bass as bass
import concourse.mybir as mybir
from concourse.bass_utils import run_bass_kernel_spmd

F32 = mybir.dt.float32
BF16 = mybir.dt.bfloat16
ALU = mybir.AluOpType
AF = mybir.ActivationFunctionType
AX = mybir.AxisListType

D = 2048
KC = 16
T = 1024
S_LEN = 2048
DFF = 5632
FC = 44
FH = 22
HALO = 15
TE = 1056
RMS_EPS = 1e-6
LN_EPS = 1e-5

V_MIXG0, V_FFNG0, V_MIXG1, V_FFNG1, V_FING = 0, 16, 32, 48, 64
V_CONVB, V_CLNG, V_CLNB = 80, 88, 96
V_FNETB = 104
V_SLNG, V_SLNB = 120, 128
V_CONVW = 136
NVEC = 136 + 248

SBUF_BASE = 16512
DEBUG_STOP = None
DBG_FC = None
DBG_DOWN = True
DBG_FH = 2
DBG_L1 = 3
MIX_ORDER = 0
STRICT_SAME_ENGINE = True
MIX_POOLS = 0
SBUF_END = 229376 - 2048


class Op:
    __slots__ = ("eng", "idx", "fn", "deps", "dma", "need", "sig", "dsem", "dval", "uid")

    def __init__(self, eng, idx, fn, deps, dma, uid):
        self.eng, self.idx, self.fn, self.deps, self.dma = eng, idx, fn, deps, dma
        self.need = False
        self.sig = 0
        self.dsem = None
        self.dval = 0
        self.uid = uid


class Sched:
    ENGS = ("pe", "act", "dve", "pool", "sp")
    SEG = 2000
    ND = 6

    def __init__(self):
        self.ops = {e: [] for e in self.ENGS}
        self.state = {}
        self.uid = 0

    def add(self, eng, fn, reads=(), writes=(), dma=False):
        deps = {}
        wset = set(writes)
        for k in reads:
            if k in wset:
                continue
            st = self.state.get(k)
            if st is not None and st[0] is not None:
                deps[st[0].uid] = st[0]
        for k in wset:
            st = self.state.get(k)
            if st is not None:
                if st[0] is not None:
                    deps[st[0].uid] = st[0]
                for r in st[1].values():
                    deps[r.uid] = r
        self.uid += 1
        op = Op(eng, len(self.ops[eng]), fn, None, dma, self.uid)
        fd = []
        rset = set(reads)
        for d in deps.values():
            if d.dma or dma or d.eng != eng:
                fd.append(d)
            elif eng != "pe" and (STRICT_SAME_ENGINE or self._is_raw(d, rset)):
                fd.append(d)
        op.deps = fd
        self.ops[eng].append(op)
        for k in reads:
            if k in wset:
                continue
            st = self.state.setdefault(k, [None, {}])
            st[1][("d", op.uid) if dma else eng] = op
        for k in wset:
            self.state[k] = [op, {}]
        return op

    def _is_raw(self, d, rset):
        for k in rset:
            st = self.state.get(k)
            if st is not None and st[0] is d:
                return True
        return False

    def finalize(self, nc, stack):
        for e in self.ENGS:
            for op in self.ops[e]:
                for d in op.deps:
                    d.need = True
        self.sems = {}
        for e in self.ENGS:
            n = 0
            nd = 0
            for op in self.ops[e]:
                if op.dma:
                    op.dsem = (e, nd % self.ND)
                    op.dval = 16 * (nd // self.ND + 1)
                    nd += 1
                elif op.need:
                    n += 1
                    op.sig = n
            nseg = (n + self.SEG - 1) // self.SEG
            for s in range(nseg):
                self.sems[(e, "c", s)] = stack.enter_context(nc.semaphore(f"s_{e}_{s}"))
            for s in range(min(nd, self.ND)):
                self.sems[(e, "d", s)] = stack.enter_context(nc.semaphore(f"d_{e}_{s}"))

    def emit(self, eng, e):
        w_sig = {x: 0 for x in self.ENGS}
        w_dma = {}
        for op in self.ops[eng]:
            waits = []
            if op.dma and op.dval > 16:
                key = (eng, "d", op.dsem[1])
                prev = op.dval - 16
                if w_dma.get(key, 0) < prev:
                    waits.append((self.sems[key], prev))
                    w_dma[key] = prev
            best = {}
            for d in op.deps:
                if d.dma:
                    key = (d.eng, "d", d.dsem[1])
                    if w_dma.get(key, 0) < d.dval:
                        waits.append((self.sems[key], d.dval))
                        w_dma[key] = d.dval
                elif d.sig > best.get(d.eng, 0):
                    best[d.eng] = d.sig
            for x, sg in best.items():
                if w_sig[x] < sg:
                    waits.append((self.sems[(x, "c", (sg - 1) // self.SEG)], (sg - 1) % self.SEG + 1))
                    w_sig[x] = sg
            for sem, val in waits:
                e.wait_ge(sem, val)
            if op.fn is None:
                continue
            ins = op.fn(e)
            if op.dma:
                ins.then_inc(self.sems[(eng, "d", op.dsem[1])], 16)
            elif op.need:
                ins.then_inc(self.sems[(eng, "c", (op.sig - 1) // self.SEG)], 1)


def build(mode):
    do1 = mode in ("p1", "fused")
    do2 = mode in ("p2", "fused")
    nc = bass.Bass("TRN2", target_bir_lowering=False)
    S = Sched()

    def din(name, shape, dt=F32):
        return nc.dram_tensor(name, list(shape), dt, kind="ExternalInput").ap()

    def dout(name, shape, dt=F32):
        return nc.dram_tensor(name, list(shape), dt, kind="ExternalOutput").ap()

    xT_d = din("xT", [128, KC, T])
    vec_d = din("vec", [128, NVEC])
    if do1:
        xh_d = din("xh", [128, KC, 32])
        sgubc_d = din("sgubc", [128, 3072])
        wsT_d = din("wsT", [128, 1024])
        wconv_d = din("w_conv", [8, 128, 4096])
        wdiag_d = din("w_diag", [8, 128, 31 * 128])
        wu_d = din("w_u", [8, 128, 2048])
        wv_d = din("w_v", [2, 128, 8192])
        wo_d = din("w_o", [16, 128, 2048])
        wgu0_d = din("w_gu0", [FC, 128, 4096])
        wd0_d = din("w_d0", [2, 16, 128, FH * 128])
        ccs_d = din("ccs", [128, 2, 512], BF16)
    if do2:
        csn_d = din("csn", [128, 2, 16, T], BF16)
        wf_d = din("w_f", [16, 128, 2048])
        wgu1_d = din("w_gu1", [FC, 128, 4096])
        wd1_d = din("w_d1", [2, 16, 128, FH * 128])
        out_d = dout("outT", [128, KC, T])
    if mode == "p1":
        x1_d = dout("x1T", [128, KC, T])
        abown_d = dout("ab_own", [16, 2, 128, 8, 128], BF16)
    elif mode == "p2":
        abfull_d = din("ab_full", [2, 16, 2, 128, 8, 128], BF16)
    else:
        xTo_d = din("xTo", [128, KC, T])
        xho_d = din("xho", [128, KC, 32])
        abfull_d = nc.dram_tensor("ab_full_i", [2, 16, 2, 128, 8, 128], BF16, kind="Internal").ap()

    cur = [SBUF_BASE]

    def region(nbytes):
        o = cur[0]
        cur[0] += (nbytes + 31) // 32 * 32
        assert cur[0] <= SBUF_END, ("sbuf overflow", cur[0])
        return o

    NRING = 3
    SLOT_B = 8192
    R_OFF = region(NRING * SLOT_B)
    VEC_OFF = region(NVEC * 4)
    ONES_OFF = region(3 * 256)
    CCS_OFF = region(2048)
    WST_OFF = region(2048)
    SQ_OFF = region(4 * 1024)
    RSTD_OFF = region(2 * 2048)
    STD_OFF = region(2 * 2048)
    EPS_OFF = region(64)
    X_OFF = region(KC * T * 4)
    H_OFF = region(KC * TE * 2)
    M_OFF = region(KC * T * 2)
    S_OFF = cur[0]
    S_SIZE = SBUF_END - S_OFF
    assert S_SIZE >= 32768 + 2048, S_SIZE

    cnt = [0]

    def sb(name, shape, dt, off):
        cnt[0] += 1
        return nc.alloc_sbuf_tensor_at(f"{name}{cnt[0]}", list(shape), dt, offset=off)

    xT = sb("xT", [128, KC, T], F32, X_OFF)
    hT = sb("hT", [128, KC, TE], BF16, H_OFF)
    mixT = sb("mixT", [128, KC, T], BF16, M_OFF)
    ring = [sb("ring", [128, 4096], BF16, R_OFF + i * SLOT_B) for i in range(NRING)]
    vec = sb("vec", [128, NVEC], F32, VEC_OFF)
    onesD = sb("onesD", [128, 128], BF16, ONES_OFF)
    onesG = sb("onesG", [128, 128], BF16, ONES_OFF + 256)
    ccs = sb("ccs", [128, 2, 512], BF16, CCS_OFF)
    wsT = sb("wsT", [128, 8, 128], BF16, WST_OFF)
    sqb = [sb("sq", [128, 512], BF16, SQ_OFF + i * 1024) for i in range(4)]
    rstdb = [sb("rstd", [128, 512], F32, RSTD_OFF + i * 2048) for i in range(2)]
    stdb = [sb("std", [128, 512], F32, STD_OFF + i * 2048) for i in range(2)]

    psum = [nc.alloc_psum_tensor(f"psb{i}", [128, 512], F32) for i in range(8)]
    bank_ctr = [0]

    def next_bank():
        b = bank_ctr[0] % 8
        bank_ctr[0] += 1
        return b

    ring_ctr = [0]

    def next_slot():
        s = ring_ctr[0] % NRING
        ring_ctr[0] += 1
        return s

    def vcol(c):
        return vec[:, c:c + 1]

    def mm(b, n, lhsT, rhs, start, stop, reads):
        out = psum[b][:, 0:n] if isinstance(n, int) else n
        S.add("pe", lambda e: e.matmul(out, lhsT, rhs, start=start, stop=stop),
              reads=reads, writes=[("ps", b)])

    def act(out, in_, func, reads, writes, bias=None, scale=None):
        kw = {}
        if bias is not None:
            kw["bias"] = bias
        if scale is not None:
            kw["scale"] = scale
        S.add("act", lambda e: e.activation(out, in_, func, **kw), reads=reads, writes=writes)

    def dve_tt(out, in0, in1, op, reads, writes):
        S.add("dve", lambda e: e.tensor_tensor(out, in0, in1, op), reads=reads, writes=writes)

    def dve_stt(out, in0, scalar, in1, op0, op1, reads, writes):
        S.add("dve", lambda e: e.scalar_tensor_tensor(out, in0, scalar, in1, op0, op1),
              reads=reads, writes=writes)

    def dve_ts(out, in0, s1, s2, op0, op1, reads, writes):
        S.add("dve", lambda e: e.tensor_scalar(out, in0, s1, s2, op0, op1), reads=reads, writes=writes)

    def dma(eng, out, in_, reads, writes):
        S.add(eng, lambda e: e.dma_start(out=out, in_=in_), reads=reads, writes=writes, dma=True)

    S.add("dve", lambda e: e.memset(onesD[:], 1.0 / D), writes=["onesD"])
    S.add("dve", lambda e: e.memset(onesG[:], 1.0 / 128.0), writes=["onesG"])
    dma("sp", vec[:], vec_d, [], ["vec"])

    norm_ctr = [0]

    def rmsnorm(gbase, blocks, src, dst, srckey, dstkey):
        for (c0, n, bk) in blocks:
            b = next_bank()
            i = norm_ctr[0] % 2
            norm_ctr[0] += 1
            for kc in range(KC):
                q = sqb[kc % 4]
                act(q[:, 0:n], src(kc, c0, n), AF.Square, reads=[srckey(kc, bk)], writes=[("sq", kc % 4)])
                mm(b, n, onesD[:], q[:, 0:n], kc == 0, kc == KC - 1, reads=[("sq", kc % 4), "onesD"])
            act(stdb[i][:, 0:n], psum[b][:, 0:n], AF.Sqrt, reads=[("ps", b), "eps0"], writes=[("std", i)],
                bias=eps_rms[:, 0:1], scale=1.0)
            S.add("dve", lambda e, i=i, n=n: e.reciprocal(rstdb[i][:, 0:n], stdb[i][:, 0:n]),
                  reads=[("std", i)], writes=[("rstd", i)])
            for kc in range(KC):
                dve_stt(dst(kc, c0, n), src(kc, c0, n), vcol(gbase + kc), rstdb[i][:, 0:n],
                        ALU.mult, ALU.mult,
                        reads=[srckey(kc, bk), ("rstd", i), "vec"], writes=[dstkey(kc, bk)])

    epst = sb("eps", [128, 4], F32, EPS_OFF)
    eps_rms = epst[:, 0:1]
    eps_ln = epst[:, 1:2]
    S.add("dve", lambda e: e.memset(epst[:, 0:1], RMS_EPS), writes=["eps0"])
    S.add("dve", lambda e: e.memset(epst[:, 1:2], LN_EPS), writes=["eps1"])
    MAINB = [(0, 512, 0), (512, 512, 1)]

    def xsrc(kc, c0, n):
        return xT[:, kc, c0:c0 + n]

    def hdst(kc, c0, n):
        return hT[:, kc, c0:c0 + n]

    def xkey(kc, bk):
        return ("xT", kc, bk)

    def hkey(kc, bk):
        return ("hT", kc, bk)

    def load_xT(src_d):
        for q in range(4):
            dma("sp", xT[:, 4 * q:4 * q + 4, :], src_d[:, 4 * q:4 * q + 4, :], [],
                [("xT", kc, bk) for kc in range(4 * q, 4 * q + 4) for bk in (0, 1)])

    def wload(src_ap, ncols, extra_reads=()):
        s = next_slot()
        dma("pool", ring[s][:, 0:ncols], src_ap, list(extra_reads), [("ring", s)])
        return s

    def ffn(layer, wgu_d, wd_d):
        gb = V_FFNG0 if layer == 0 else V_FFNG1
        rmsnorm(gb, MAINB, xsrc, hdst, xkey, hkey)
        aT = sb("aT", [128, FH, T], BF16, M_OFF)
        sg = [sb("sg", [128, 512], F32, M_OFF + FH * T * 2 + i * 2048) for i in range(2)]
        sgc = 0
        for fh in range(DBG_FH):
            for fc in range(FH if DBG_FC is None else DBG_FC):
                f = fh * FH + fc
                s = wload(wgu_d[f], 4096)
                W = ring[s][:, :].rearrange("p (a k n) -> p a k n", a=2, k=KC)
                for half in range(2):
                    bg, bu = next_bank(), next_bank()
                    for kc in range(KC):
                        mm(bg, 512, W[:, 0, kc, :], hT[:, kc, half * 512:(half + 1) * 512], kc == 0, kc == KC - 1,
                           reads=[("ring", s), hkey(kc, half)])
                    for kc in range(KC):
                        mm(bu, 512, W[:, 1, kc, :], hT[:, kc, half * 512:(half + 1) * 512], kc == 0, kc == KC - 1,
                           reads=[("ring", s), hkey(kc, half)])
                    j = sgc % 2
                    sgc += 1
                    act(sg[j][:], psum[bg][:], AF.Silu, reads=[("ps", bg), "Mreg"], writes=[("sg", j)])
                    dve_tt(aT[:, fc, half * 512:(half + 1) * 512], sg[j][:], psum[bu][:], ALU.mult,
                           reads=[("sg", j), ("ps", bu)], writes=[("aT", fc, half)])
            for n in range(KC if DBG_DOWN else 0):
                s = wload(wd_d[fh, n], FH * 128)
                W = ring[s][:, 0:FH * 128].rearrange("p (k n) -> p k n", k=FH)
                for half in range(2):
                    b = next_bank()
                    for fc in range(FH):
                        mm(b, 512, W[:, fc, :], aT[:, fc, half * 512:(half + 1) * 512], fc == 0, fc == FH - 1,
                           reads=[("ring", s), ("aT", fc, half)])
                    dve_tt(xT[:, n, half * 512:(half + 1) * 512], psum[b][:], xT[:, n, half * 512:(half + 1) * 512],
                           ALU.add, reads=[("ps", b)], writes=[xkey(n, half)])

    def part1(xT_d, xh_d, abown_d, pss):
        load_xT(xT_d)
        check(1)
        xh = sb("xh", [128, KC, 32], F32, S_OFF)
        dma("sp", xh[:], xh_d, [], ["xh"])
        rmsnorm(V_MIXG0, MAINB, xsrc, hdst, xkey, hkey)
        rmsnorm(V_MIXG0, [(0, 32, 2)],
                lambda kc, c0, n: xh[:, kc, 0:32], lambda kc, c0, n: hT[:, kc, 1024:1056],
                lambda kc, bk: "xh", hkey)

        ax = [X_OFF]

        def aX(nbytes):
            o = ax[0]
            ax[0] += (nbytes + 31) // 32 * 32
            assert ax[0] <= X_OFF + KC * T * 4, "X arena overflow"
            return o

        as_ = [S_OFF]

        def aS(nbytes):
            o = as_[0]
            as_[0] += (nbytes + 31) // 32 * 32
            assert as_[0] <= SBUF_END, "S arena overflow"
            return o

        Wv = [sb("Wv", [128, KC, 512], BF16, aX(16384)) for _ in range(2)]
        glu = [sb("glu", [128, TE], BF16, aX(TE * 2)) for _ in range(2)]
        ycv = [sb("ycv", [128, T], F32, aX(T * 4)) for _ in range(2)]
        ybf = sb("ybf", [128, T], BF16, aX(T * 2))
        ysq = sb("ysq", [128, T], BF16, aX(T * 2))
        sig = [sb("sig", [128, 512], F32, aX(2048)) for _ in range(2)]
        m2 = sb("m2", [128, 512], F32, aX(2048))
        varb = sb("varb", [128, 512], F32, aX(2048))
        tcen = sb("tcen", [128, 512], F32, aX(2048))
        sgubc = sb("sgubc", [128, 3072], F32, aS(12288))
        vg = [sb("vg", [128, 1024], F32, aS(4096)) for _ in range(2)]
        vln = [sb("vln", [128, 1024], BF16, aS(2048)) for _ in range(2)]
        sptmp = sb("sptmp", [128, 512], F32, aS(2048))
        stats = sb("stats", [128, 2, 6], F32, aS(64))
        mv = sb("mv", [128, 2], F32, aS(32))
        sdv = sb("sdv", [128, 2], F32, aS(32))
        m2h = [m2, sb("m2b", [128, 512], F32, aS(2048))]
        varbh = [varb, sb("varbb", [128, 512], F32, aS(2048))]
        tcenh = [tcen, sb("tcenb", [128, 512], F32, aS(2048))]

        xdead = [xkey(kc, bk) for kc in range(KC) for bk in (0, 1)]
        stgk = [k for k in S.state if isinstance(k, tuple) and k[0] == "stg"]
        S.add("dve", lambda e: e.memset(m2[:, 0:1], 0.0), reads=[], writes=xdead + stgk + ["arenaX"])

        dma("sp", sgubc[:], sgubc_d, [], ["sgubc", "xh"])
        dma("pool", wsT[:].rearrange("p a b -> p (a b)"), wsT_d, [], ["wsT"])

        check(2)
        if MIX_POOLS == 0:
            pools = {"A": [0, 1, 2, 3], "B": [4, 5], "Bs": [4, 5], "C": [6, 7]}
            pools["Bs"] = pools["B"]
        elif MIX_POOLS == 1:
            pools = {"A": [0, 1], "B": [2, 3], "Bs": [4, 5], "C": [6, 7]}
        else:
            pools = {"A": [0, 1, 2, 3], "B": [4], "Bs": [5, 6], "C": [7]}
        pctr = {"A": 0, "B": 0, "Bs": 0, "C": 0}
        if pools["Bs"] is pools["B"]:
            pctr_alias = {"Bs": "B"}
        else:
            pctr_alias = {}

        def pb(which):
            lst = pools[which]
            which = pctr_alias.get(which, which)
            b_ = lst[pctr[which] % len(lst)]
            pctr[which] += 1
            return b_

        for nb in range(2):
            dma("pool", Wv[nb][:].rearrange("p k n -> p (k n)"), wv_d[nb], ["arenaX"], [("Wv", nb)])
        lng_bc = sgubc[:, 0:1024]
        lnb_bc = sgubc[:, 1024:2048]
        bs_bc = sgubc[:, 2048:3072].rearrange("p (h q) -> p h q", h=8)

        def conv_A(c):
            s = wload(wconv_d[c], 4096)
            W = ring[s][:, :].rearrange("p (a k n) -> p a k n", a=2, k=KC)
            g = glu[c % 2]
            gk = ("glu", c % 2)
            for (c0, n, bk) in [(0, 512, 0), (512, 512, 1), (1024, 32, 2)]:
                ba, bg = pb("A"), pb("A")
                for kc in range(KC):
                    mm(ba, n, W[:, 0, kc, :], hT[:, kc, c0:c0 + n], kc == 0, kc == KC - 1,
                       reads=[("ring", s), hkey(kc, bk)])
                for kc in range(KC):
                    mm(bg, n, W[:, 1, kc, :], hT[:, kc, c0:c0 + n], kc == 0, kc == KC - 1,
                       reads=[("ring", s), hkey(kc, bk)])
                j = bk % 2
                act(sig[j][:, 0:n], psum[bg][:, 0:n], AF.Sigmoid, reads=[("ps", bg), "arenaX"], writes=[("sig", j)])
                if bk < 2:
                    dve_tt(g[:, HALO + c0:HALO + c0 + n], psum[ba][:, 0:n], sig[j][:, 0:n], ALU.mult,
                           reads=[("ps", ba), ("sig", j), "arenaX"], writes=[gk + (bk,)])
                else:
                    dve_tt(g[:, 0:HALO], psum[ba][:, 0:HALO], sig[j][:, 0:HALO], ALU.mult,
                           reads=[("ps", ba), ("sig", j), "arenaX"], writes=[gk + (2,)])
                    dve_tt(g[:, HALO + T:HALO + T + HALO], psum[ba][:, HALO:2 * HALO], sig[j][:, HALO:2 * HALO],
                           ALU.mult, reads=[("ps", ba), ("sig", j)], writes=[gk + (3,)])

        def conv_B(c):
            g = glu[c % 2]
            gk = ("glu", c % 2)
            sd = wload(wdiag_d[c], 31 * 128)
            Dg = ring[sd][:, 0:31 * 128].rearrange("p (j n) -> p j n", j=31)
            gkeys = [gk + (i,) for i in range(4)]
            y = ycv[c % 2]
            yk = ("ycv", c % 2)
            for half in range(2):
                hs = slice(half * 512, (half + 1) * 512)
                by = pb("B")
                for j in range(31):
                    mm(by, 512, Dg[:, j, :], g[:, half * 512 + j:half * 512 + j + 512], j == 0, j == 30,
                       reads=[("ring", sd)] + gkeys)
                act(y[:, hs], psum[by][:], AF.Identity, reads=[("ps", by), "vec"], writes=[yk + (half,)],
                    bias=vcol(V_CONVB + c), scale=1.0)
                act(ybf[:, hs], psum[by][:], AF.Identity, reads=[("ps", by), "vec"], writes=[("ybf", half)],
                    bias=vcol(V_CONVB + c), scale=1.0)
                act(ysq[:, hs], psum[by][:], AF.Square, reads=[("ps", by), "vec"], writes=[("ysq", half)],
                    bias=vcol(V_CONVB + c), scale=1.0)

        def conv_C(c):
            y = ycv[c % 2]
            yk = ("ycv", c % 2)
            for half in range(2):
                hs = slice(half * 512, (half + 1) * 512)
                m2_, varb_, tcen_ = m2h[half], varbh[half], tcenh[half]
                bm, bq = pb("Bs"), pb("Bs")
                mm(bm, 512, onesG[:], ybf[:, hs], True, True, reads=[("ybf", half), "onesG"])
                mm(bq, 512, onesG[:], ysq[:, hs], True, True, reads=[("ysq", half), "onesG"])
                act(m2_[:], psum[bm][:], AF.Square, reads=[("ps", bm), "arenaX"], writes=[("m2", half)])
                dve_tt(varb_[:], psum[bq][:], m2_[:], ALU.subtract, reads=[("ps", bq), ("m2", half)],
                       writes=[("varb", half)])
                act(varb_[:], varb_[:], AF.Sqrt, reads=[("varb", half), "eps1"], writes=[("varb", half)],
                    bias=eps_ln, scale=1.0)
                S.add("dve", lambda e, v_=varb_: e.reciprocal(v_[:], v_[:]), reads=[("varb", half)],
                      writes=[("varb", half)])
                dve_tt(tcen_[:], y[:, hs], psum[bm][:], ALU.subtract, reads=[yk + (half,), ("ps", bm)],
                       writes=[("tcen", half)])
                dve_tt(tcen_[:], tcen_[:], varb_[:], ALU.mult, reads=[("tcen", half), ("varb", half)],
                       writes=[("tcen", half)])
                act(mixT[:, c, hs], tcen_[:], AF.Silu, reads=[("tcen", half), "vec"], writes=[("mixT", c, half)],
                    bias=vcol(V_CLNB + c), scale=vcol(V_CLNG + c))

        def sgu_u(hc):
            s = wload(wu_d[hc], 2048)
            W = ring[s][:, 0:2048].rearrange("p (k n) -> p k n", k=KC)
            for half in range(2):
                b = pb("C")
                for kc in range(KC):
                    mm(b, 512, W[:, kc, :], hT[:, kc, half * 512:(half + 1) * 512], kc == 0, kc == KC - 1,
                       reads=[("ring", s), hkey(kc, half)])
                act(mixT[:, 8 + hc, half * 512:(half + 1) * 512], psum[b][:], AF.Gelu,
                    reads=[("ps", b)], writes=[("mixT", 8 + hc, half)])

        def sgu_v_ln(tt):
            v = vg[tt % 2]
            vk = ("vg", tt % 2)
            for nb in range(2):
                b = pb("C")
                for kc in range(KC):
                    mm(b, 512, hT[:, kc, tt * 128:(tt + 1) * 128], Wv[nb][:, kc, :], kc == 0, kc == KC - 1,
                       reads=[("Wv", nb), hkey(kc, tt // 4)])
                act(v[:, nb * 512:(nb + 1) * 512], psum[b][:], AF.Gelu, reads=[("ps", b), "arenaX"],
                    writes=[vk + (nb,)])
            for nb in range(2):
                S.add("dve", lambda e, v=v, nb=nb: e.bn_stats(stats[:, nb, :], v[:, nb * 512:(nb + 1) * 512]),
                      reads=[vk + (nb,)], writes=[("stats", nb)])
            S.add("dve", lambda e: e.bn_aggr(mv[:], stats[:].rearrange("p a b -> p (a b)")),
                  reads=[("stats", 0), ("stats", 1)], writes=["mv"])
            act(sdv[:, 0:1], mv[:, 1:2], AF.Sqrt, reads=["mv", "eps1", "arenaX"], writes=["sdv0"], bias=eps_ln, scale=1.0)
            S.add("dve", lambda e: e.reciprocal(sdv[:, 1:2], sdv[:, 0:1]), reads=["sdv0"], writes=["sdv1"])
            dve_ts(v[:], v[:], mv[:, 0:1], sdv[:, 1:2], ALU.subtract, ALU.mult,
                   reads=[vk + (0,), vk + (1,), "mv", "sdv1"], writes=[vk + (0,), vk + (1,)])
            dve_tt(v[:], v[:], lng_bc, ALU.mult, reads=[vk + (0,), vk + (1,), "sgubc"],
                   writes=[vk + (0,), vk + (1,)])
            dve_tt(vln[tt % 2][:], v[:], lnb_bc, ALU.add, reads=[vk + (0,), vk + (1,), "sgubc"],
                   writes=[("vln", tt % 2)])

        def sgu_spatial(tt):
            vl = vln[tt % 2]
            vlk = ("vln", tt % 2)
            for hg in range(2):
                b = pb("C")
                for hh in range(4):
                    hd = hg * 4 + hh
                    mm(b, psum[b][:, hh * 128:(hh + 1) * 128], vl[:, hd * 128:(hd + 1) * 128], wsT[:, hd, :],
                       True, True, reads=[vlk, "wsT"])
                dve_tt(sptmp[:].rearrange("p (h q) -> p h q", h=4), psum[b][:].rearrange("p (h q) -> p h q", h=4),
                       bs_bc[:, hg * 4:hg * 4 + 4, :], ALU.add, reads=[("ps", b), "sgubc"], writes=["sptmp"])
                mo = mixT[:, 8 + hg * 4:8 + hg * 4 + 4, tt * 128:(tt + 1) * 128]
                dve_tt(mo, sptmp[:].rearrange("p (h q) -> p h q", h=4), mo, ALU.mult,
                       reads=["sptmp"] + [("mixT", 8 + hg * 4 + hh, tt // 4) for hh in range(4)],
                       writes=[("mixT", 8 + hg * 4 + hh, tt // 4) for hh in range(4)])

        for hc in range(8):
            sgu_u(hc)
        if MIX_ORDER == 0:
            for c in range(8):
                conv_A(c)
                conv_B(c)
                conv_C(c)
                sgu_v_ln(c)
                if c >= 1:
                    sgu_spatial(c - 1)
            sgu_spatial(7)
        elif MIX_ORDER == 1:
            for i in range(10):
                if i < 8:
                    conv_A(i)
                    sgu_v_ln(i)
                if 2 <= i:
                    conv_C(i - 2)
                if 1 <= i <= 8:
                    conv_B(i - 1)
                    sgu_spatial(i - 1)
        else:
            for i in range(9):
                if i < 8:
                    conv_A(i)
                if 1 <= i:
                    conv_B(i - 1)
                if i < 8:
                    sgu_v_ln(i)
                if 1 <= i:
                    conv_C(i - 1)
                    sgu_spatial(i - 1)

        check(5)
        arena_keys = [k for k in S.state if isinstance(k, tuple) and k[0] in
                      ("glu", "ycv", "sig", "Wv", "ybf", "ysq", "m2", "varb", "tcen")] + ["arenaX"]
        for q in range(4):
            dma("sp", xT[:, 4 * q:4 * q + 4, :], xT_d[:, 4 * q:4 * q + 4, :], [],
                arena_keys + [("xT", kc, bk) for kc in range(4 * q, 4 * q + 4) for bk in (0, 1)])
        for n in range(KC):
            s = wload(wo_d[n], 2048)
            W = ring[s][:, 0:2048].rearrange("p (k n) -> p k n", k=KC)
            for half in range(2):
                b = next_bank()
                for kc in range(KC):
                    mm(b, 512, W[:, kc, :], mixT[:, kc, half * 512:(half + 1) * 512], kc == 0, kc == KC - 1,
                       reads=[("ring", s), ("mixT", kc, half)])
                dve_tt(xT[:, n, half * 512:(half + 1) * 512], psum[b][:], xT[:, n, half * 512:(half + 1) * 512],
                       ALU.add, reads=[("ps", b)], writes=[xkey(n, half)])

        check(6)
        skeys = [k for k in S.state if isinstance(k, tuple) and k[0] in ("vg", "vln", "stats", "mixT", "m2", "varb", "tcen")] + \
                ["sgubc", "sptmp", "mv", "sdv0", "sdv1", "xh"]
        S.add("dve", lambda e: e.memset(epst[:, 2:3], 0.0), reads=[], writes=skeys + ["Mreg", "epsm2"])
        ffn(0, wgu0_d, wd0_d)

        check(7)
        if mode == "p1":
            for q in range(4):
                dma("sp", x1_d[:, 4 * q:4 * q + 4, :], xT[:, 4 * q:4 * q + 4, :],
                    [("xT", kc, bk) for kc in range(4 * q, 4 * q + 4) for bk in (0, 1)], [("x1out", q)])
            x1done[0] = True

        dma("sp", ccs[:], ccs_d, [], ["ccs"])
        rmsnorm(V_MIXG1, MAINB, xsrc, hdst, xkey, hkey)
        stg = [sb("stg", [128, 2, 16, 128], BF16, S_OFF + 16384 + i * 8192) for i in range(2)]
        sc = 0
        for tt in range(8 if DBG_L1 >= 2 else 0):
            k = tt % 2
            for g in range(8):
                b = next_bank()
                for j in range(2):
                    mm(b, 512, hT[:, 2 * g + j, tt * 128:(tt + 1) * 128], ccs[:, j, :], j == 0, j == 1,
                       reads=[hkey(2 * g + j, tt // 4), "ccs"])
                so = stg[k][:, :, 2 * g:2 * g + 2, :]
                pi = psum[b][:].rearrange("p (a c j) -> p a c j", a=2, c=2)
                sc += 1
                if sc % 2 == 0:
                    act(so, pi, AF.Copy, reads=[("ps", b)], writes=[("stg", k, g)])
                else:
                    S.add("dve", lambda e, so=so, pi=pi: e.tensor_copy(so, pi),
                          reads=[("ps", b)], writes=[("stg", k, g)])
            for ab in range(2 if DBG_L1 >= 3 else 0):
                dma("sp", abown_d[:, ab, :, tt, :].rearrange("k p j -> p k j"), stg[k][:, ab, :, :],
                    [("stg", k, g) for g in range(8)], [("abown", pss, tt, ab)])

    x1done = [False]

    class _Stop(Exception):
        pass

    def check(k):
        if DEBUG_STOP is not None and k > DEBUG_STOP:
            raise _Stop()

    if mode == "p1":
        try:
            part1(xT_d, xh_d, abown_d, 0)
        except _Stop:
            pass
        if not x1done[0]:
            for q in range(4):
                dma("sp", x1_d[:, 4 * q:4 * q + 4, :], xT[:, 4 * q:4 * q + 4, :],
                    [("xT", kc, bk) for kc in range(4 * q, 4 * q + 4) for bk in (0, 1)], [("x1out", q)])
    elif mode == "fused":
        part1(xTo_d, xho_d, abfull_d[0], 0)
        part1(xT_d, xh_d, abfull_d[1], 1)
        ab_keys = [k for k in S.state if isinstance(k, tuple) and k[0] == "abown"]
        S.add("sp", None, reads=ab_keys, writes=[])

    if do2:
        if mode == "p2":
            load_xT(xT_d)
        Cs = sb("Cs", [128, 16, T], BF16, H_OFF)
        Ss = sb("Ss", [128, 16, T], BF16, S_OFF)
        hkeys = [hkey(kc, bk) for kc in range(KC) for bk in (0, 1, 2)]
        skeys2 = [k for k in S.state if isinstance(k, tuple) and k[0] in ("aT", "sg", "stg")]
        S.add("act", lambda e: e.activation(epst[:, 2:3], epst[:, 0:1], AF.Copy), reads=["eps0"],
              writes=[k for k in S.state if isinstance(k, tuple) and k[0] == "aT"] + ["YTfence", "epsm2"])
        for hh in range(2):
            dma("sp", Cs[:, 8 * hh:8 * hh + 8, :], csn_d[:, 0, 8 * hh:8 * hh + 8, :], [], hkeys + [("Cs", hh)])
            dma("sp", Ss[:, 8 * hh:8 * hh + 8, :], csn_d[:, 1, 8 * hh:8 * hh + 8, :], [], skeys2 + [("Ss", hh)])
        YT = mixT
        for ck in range(16):
            s = next_slot()
            sv = ring[s][:, :].rearrange("p (a r f) -> p a r f", a=2, r=2)
            for ab in range(2):
                dma("sp", sv[:, ab, :, :], abfull_d[:, ck, ab, :, :, :].rearrange("r p t j -> p r (t j)"),
                    [], [("ring", s)])
            for kb in range(2):
                b = next_bank()
                i = 0
                for ab in range(2):
                    Mx = Cs if ab == 0 else Ss
                    for st in range(16):
                        r, tt = st // 8, st % 8
                        mm(b, 512, sv[:, ab, r, tt * 128:(tt + 1) * 128], Mx[:, st, kb * 512:(kb + 1) * 512],
                           i == 0, i == 31,
                           reads=[("ring", s), ("Cs" if ab == 0 else "Ss", st // 8)])
                        i += 1
                act(YT[:, ck, kb * 512:(kb + 1) * 512], psum[b][:], AF.Copy, reads=[("ps", b)],
                    writes=[("YT", ck, kb)])
        for n in range(KC):
            s = next_slot()
            dma("pool", ring[s][:, 0:2048], wf_d[n], [], [("ring", s)])
            W = ring[s][:, 0:2048].rearrange("p (k n) -> p k n", k=KC)
            for half in range(2):
                b = next_bank()
                for kc in range(KC):
                    mm(b, 512, W[:, kc, :], YT[:, kc, half * 512:(half + 1) * 512], kc == 0, kc == KC - 1,
                       reads=[("ring", s), ("YT", kc, half)])
                dve_stt(xT[:, n, half * 512:(half + 1) * 512], psum[b][:], vcol(V_FNETB + n),
                        xT[:, n, half * 512:(half + 1) * 512], ALU.add, ALU.add,
                        reads=[("ps", b), "vec"], writes=[xkey(n, half)])
        ykeys = [("YT", ck, kb) for ck in range(16) for kb in range(2)] + [("Ss", 0), ("Ss", 1), ("Cs", 0), ("Cs", 1)]
        S.add("dve", lambda e: e.memset(epst[:, 3:4], 0.0), reads=[], writes=ykeys + ["Mreg", "epsm3"] + hkeys)
        ffn(1, wgu1_d, wd1_d)
        rmsnorm(V_FING, MAINB, xsrc, xsrc, xkey, xkey)
        okeys = []
        for bk in range(2):
            for q in range(2):
                dma("sp", out_d[:, 8 * q:8 * q + 8, bk * 512:(bk + 1) * 512],
                    xT[:, 8 * q:8 * q + 8, bk * 512:(bk + 1) * 512],
                    [("xT", kc, bk) for kc in range(8 * q, 8 * q + 8)], [("out", bk, q)])
                okeys.append(("out", bk, q))
        S.add("sp", None, reads=okeys, writes=[])
    else:
        okeys = [("x1out", q) for q in range(4)] + \
                [("abown", 0, tt, ab) for tt in range(8) for ab in range(2)]
        okeys = [k for k in okeys if k in S.state]
        S.add("sp", None, reads=okeys, writes=[])

    with ExitStack() as stack:
        stack.enter_context(nc.allow_low_precision("bf16 matmul operands, fp32 accumulation"))
        S.finalize(nc, stack)
        with nc.Block() as block:
            @block.tensor
            def _(e):
                S.emit("pe", e)

            @block.scalar
            def _(e):
                S.emit("act", e)

            @block.vector
            def _(e):
                S.emit("dve", e)

            @block.gpsimd
            def _(e):
                S.emit("pool", e)

            @block.sync
            def _(e):
                S.emit("sp", e)
    return nc


def _pc(v):
    v = np.asarray(v, np.float32)
    return np.ascontiguousarray(v.reshape(-1, 128).T)


def _wtile(W, ncols_per_tile):
    K, N = W.shape
    kc = K // 128
    nt = N // ncols_per_tile
    a = W.reshape(kc, 128, nt, ncols_per_tile).transpose(2, 1, 0, 3)
    return np.ascontiguousarray(a).reshape(nt, 128, kc * ncols_per_tile)


_CONST_CACHE = {}


def _dft_consts():
    if "c" in _CONST_CACHE:
        return _CONST_CACHE["c"]
    bf = ml_dtypes.bfloat16
    c = np.arange(256, dtype=np.float64)
    th = 2 * np.pi * np.outer(c, c) / 256.0
    cc = (np.cos(th) / 16.0).reshape(2, 128, 256)
    sc = (np.sin(th) / 16.0).reshape(2, 128, 256)
    ccs = np.concatenate([cc, sc], axis=2).transpose(1, 0, 2)
    ccs = np.ascontiguousarray(ccs).astype(np.float32).astype(bf)
    csn = {}
    for half in range(2):
        for order in ("seq", "oo"):
            if order == "seq":
                s = np.arange(S_LEN, dtype=np.int64)
            else:
                oth = 1 - half
                s = np.concatenate([np.arange(T) + oth * T, np.arange(T) + half * T]).astype(np.int64)
            k = np.arange(T, dtype=np.int64) + half * T
            ph = (np.outer(s, k) % S_LEN).astype(np.float64) * (2 * np.pi / S_LEN)
            sc_ = 1.0 / math.sqrt(S_LEN)
            co = (np.cos(ph) * sc_).reshape(16, 128, T).transpose(1, 0, 2)
            si = (-np.sin(ph) * sc_).reshape(16, 128, T).transpose(1, 0, 2)
            a = np.stack([co, si], axis=1)
            csn[(half, order)] = np.ascontiguousarray(a).astype(np.float32).astype(bf)
    _CONST_CACHE["c"] = (ccs, csn)
    return ccs, csn


_NC_CACHE = {}


def _get_nc(mode):
    if mode not in _NC_CACHE:
        _NC_CACHE[mode] = build(mode)
    return _NC_CACHE[mode]


FUSED = True


def kernel(x, mix_norm_g, ffn_norm_g, final_norm_g, ab_w_in, conv_dw_w, conv_dw_b, conv_ln_g,
           conv_ln_b, sgu_ln_g, sgu_ln_b, sgu_w, sgu_b, ab_w_out, fnet_w_out, fnet_b_out,
           ffn_w_gate, ffn_w_up, ffn_w_down):
    f32 = np.float32
    x = np.asarray(x, f32)
    ccs, csn = _dft_consts()

    vec = np.zeros((128, NVEC), f32)
    vec[:, V_MIXG0:V_MIXG0 + 16] = _pc(mix_norm_g[0])
    vec[:, V_FFNG0:V_FFNG0 + 16] = _pc(ffn_norm_g[0])
    vec[:, V_MIXG1:V_MIXG1 + 16] = _pc(mix_norm_g[1])
    vec[:, V_FFNG1:V_FFNG1 + 16] = _pc(ffn_norm_g[1])
    vec[:, V_FING:V_FING + 16] = _pc(final_norm_g)
    vec[:, V_CONVB:V_CONVB + 8] = _pc(conv_dw_b[0])
    vec[:, V_CLNG:V_CLNG + 8] = _pc(conv_ln_g[0])
    vec[:, V_CLNB:V_CLNB + 8] = _pc(conv_ln_b[0])
    vec[:, V_FNETB:V_FNETB + 16] = _pc(fnet_b_out[0])
    cw = np.asarray(conv_dw_w[0], f32)
    vec[:, V_CONVW:V_CONVW + 248] = cw.reshape(31, 8, 128).transpose(2, 1, 0).reshape(128, 248)

    sgubc = np.concatenate([np.asarray(sgu_ln_g[0], f32), np.asarray(sgu_ln_b[0], f32),
                            np.asarray(sgu_b[0], f32).reshape(-1)])
    sgubc = np.ascontiguousarray(np.broadcast_to(sgubc[None, :], (128, 3072)))
    wsT = np.ascontiguousarray(np.asarray(sgu_w[0], f32).transpose(2, 0, 1)).reshape(128, 1024)

    w_in = np.asarray(ab_w_in[0], f32)
    ta = _wtile(w_in[:, 0:1024], 128).reshape(8, 128, 1, 2048)
    tg = _wtile(w_in[:, 1024:2048], 128).reshape(8, 128, 1, 2048)
    w_conv = np.ascontiguousarray(np.concatenate([ta, tg], axis=2)).reshape(8, 128, 4096)
    w_diag = np.zeros((8, 128, 31, 128), f32)
    pi = np.arange(128)
    w_diag[:, pi, :, pi] = cw.reshape(31, 8, 128).transpose(2, 1, 0)
    w_diag = w_diag.reshape(8, 128, 31 * 128)
    w_u = _wtile(w_in[:, 2048:3072], 128)
    w_v = _wtile(w_in[:, 3072:4096], 512)
    w_o = _wtile(np.asarray(ab_w_out[0], f32), 128)
    w_f = _wtile(np.asarray(fnet_w_out[0], f32), 128)

    def gu(l):
        a = _wtile(np.asarray(ffn_w_gate[l], f32), 128).reshape(FC, 128, 1, 2048)
        b = _wtile(np.asarray(ffn_w_up[l], f32), 128).reshape(FC, 128, 1, 2048)
        return np.ascontiguousarray(np.concatenate([a, b], axis=2)).reshape(FC, 128, 4096)

    def dn(l):
        W = np.asarray(ffn_w_down[l], f32)
        a = W.reshape(2, FH, 128, 16, 128).transpose(0, 3, 2, 1, 4)
        return np.ascontiguousarray(a).reshape(2, 16, 128, FH * 128)

    w_gu0, w_gu1, w_d0, w_d1 = gu(0), gu(1), dn(0), dn(1)

    xT_l, xh_l = [], []
    for c in range(8):
        b, half = c // 2, c % 2
        xs = x[b, half * T:(half + 1) * T, :]
        xT_l.append(np.ascontiguousarray(xs.T.reshape(KC, 128, T).transpose(1, 0, 2)))
        hal = np.zeros((32, D), f32)
        if half == 1:
            hal[0:HALO] = x[b, T - HALO:T, :]
        else:
            hal[HALO:2 * HALO] = x[b, T:T + HALO, :]
        xh_l.append(np.ascontiguousarray(hal.T.reshape(KC, 128, 32).transpose(1, 0, 2)))

    cores = list(range(8))
    if FUSED:
        nc = _get_nc("fused")
        in_maps = []
        for c in cores:
            o = c ^ 1
            in_maps.append({"xT": xT_l[c], "vec": vec, "xh": xh_l[c], "xTo": xT_l[o], "xho": xh_l[o],
                            "sgubc": sgubc, "wsT": wsT,
                            "w_conv": w_conv, "w_diag": w_diag, "w_u": w_u, "w_v": w_v, "w_o": w_o, "w_gu0": w_gu0,
                            "w_d0": w_d0, "ccs": ccs, "csn": csn[(c % 2, "oo")], "w_f": w_f, "w_gu1": w_gu1,
                            "w_d1": w_d1})
        res = run_bass_kernel_spmd(nc, in_maps, core_ids=cores)
        outs = [r["outT"] for r in res.results]
    else:
        nc1 = _get_nc("p1")
        in_maps = []
        for c in cores:
            in_maps.append({"xT": xT_l[c], "vec": vec, "xh": xh_l[c], "sgubc": sgubc, "wsT": wsT,
                            "w_conv": w_conv, "w_diag": w_diag, "w_u": w_u, "w_v": w_v, "w_o": w_o, "w_gu0": w_gu0,
                            "w_d0": w_d0, "ccs": ccs})
        r1 = run_bass_kernel_spmd(nc1, in_maps, core_ids=cores).results
        nc2 = _get_nc("p2")
        in_maps = []
        for c in cores:
            p = c - (c % 2)
            abf = np.ascontiguousarray(np.stack([r1[p]["ab_own"], r1[p + 1]["ab_own"]], axis=0))
            in_maps.append({"xT": r1[c]["x1T"], "vec": vec, "ab_full": abf, "csn": csn[(c % 2, "seq")],
                            "w_f": w_f, "w_gu1": w_gu1, "w_d1": w_d1})
        res = run_bass_kernel_spmd(nc2, in_maps, core_ids=cores)
        outs = [r["outT"] for r in res.results]

    out = np.empty((4, S_LEN, D), f32)
    for c in cores:
        b, half = c // 2, c % 2
        oT = np.asarray(outs[c], f32)
        out[b, half * T:(half + 1) * T, :] = oT.transpose(1, 0, 2).reshape(D, T).T
    return out
```

```python
import math
from contextlib import ExitStack

import numpy as np
import ml_dtypes

import concourse.bass as bass
import concourse.mybir as mybir
from concourse.bass_utils import run_bass_kernel_spmd

F32 = mybir.dt.float32
BF16 = mybir.dt.bfloat16
ALU = mybir.AluOpType
AF = mybir.ActivationFunctionType
AX = mybir.AxisListType

D = 2048
KC = 16
T = 1024
S_LEN = 2048
DFF = 5632
FC = 44
FH = 22
HALO = 15
TE = 1056
RMS_EPS = 1e-6
LN_EPS = 1e-5

V_MIXG0, V_FFNG0, V_MIXG1, V_FFNG1, V_FING = 0, 16, 32, 48, 64
V_CONVB, V_CLNG, V_CLNB = 80, 88, 96
V_FNETB = 104
V_SLNG, V_SLNB = 120, 128
V_CONVW = 136
NVEC = 136 + 248

SBUF_BASE = 16512
DEBUG_STOP = None
DBG_FC = None
DBG_DOWN = True
DBG_FH = 2
DBG_L1 = 3
MIX_ORDER = 0
STRICT_SAME_ENGINE = True
MIX_POOLS = 0
SBUF_END = 229376 - 2048


class Op:
    __slots__ = ("eng", "idx", "fn", "deps", "dma", "need", "sig", "dsem", "dval", "uid")

    def __init__(self, eng, idx, fn, deps, dma, uid):
        self.eng, self.idx, self.fn, self.deps, self.dma = eng, idx, fn, deps, dma
        self.need = False
        self.sig = 0
        self.dsem = None
        self.dval = 0
        self.uid = uid


class Sched:
    ENGS = ("pe", "act", "dve", "pool", "sp")
    SEG = 2000
    ND = 6

    def __init__(self):
        self.ops = {e: [] for e in self.ENGS}
        self.state = {}
        self.uid = 0

    def add(self, eng, fn, reads=(), writes=(), dma=False):
        deps = {}
        wset = set(writes)
        for k in reads:
            if k in wset:
                continue
            st = self.state.get(k)
            if st is not None and st[0] is not None:
                deps[st[0].uid] = st[0]
        for k in wset:
            st = self.state.get(k)
            if st is not None:
                if st[0] is not None:
                    deps[st[0].uid] = st[0]
                for r in st[1].values():
                    deps[r.uid] = r
        self.uid += 1
        op = Op(eng, len(self.ops[eng]), fn, None, dma, self.uid)
        fd = []
        rset = set(reads)
        for d in deps.values():
            if d.dma or dma or d.eng != eng:
                fd.append(d)
            elif eng != "pe" and (STRICT_SAME_ENGINE or self._is_raw(d, rset)):
                fd.append(d)
        op.deps = fd
        self.ops[eng].append(op)
        for k in reads:
            if k in wset:
                continue
            st = self.state.setdefault(k, [None, {}])
            st[1][("d", op.uid) if dma else eng] = op
        for k in wset:
            self.state[k] = [op, {}]
        return op

    def _is_raw(self, d, rset):
        for k in rset:
            st = self.state.get(k)
            if st is not None and st[0] is d:
                return True
        return False

    def finalize(self, nc, stack):
        for e in self.ENGS:
            for op in self.ops[e]:
                for d in op.deps:
                    d.need = True
        self.sems = {}
        for e in self.ENGS:
            n = 0
            nd = 0
            for op in self.ops[e]:
                if op.dma:
                    op.dsem = (e, nd % self.ND)
                    op.dval = 16 * (nd // self.ND + 1)
                    nd += 1
                elif op.need:
                    n += 1
                    op.sig = n
            nseg = (n + self.SEG - 1) // self.SEG
            for s in range(nseg):
                self.sems[(e, "c", s)] = stack.enter_context(nc.semaphore(f"s_{e}_{s}"))
            for s in range(min(nd, self.ND)):
                self.sems[(e, "d", s)] = stack.enter_context(nc.semaphore(f"d_{e}_{s}"))

    def emit(self, eng, e):
        w_sig = {x: 0 for x in self.ENGS}
        w_dma = {}
        for op in self.ops[eng]:
            waits = []
            if op.dma and op.dval > 16:
                key = (eng, "d", op.dsem[1])
                prev = op.dval - 16
                if w_dma.get(key, 0) < prev:
                    waits.append((self.sems[key], prev))
                    w_dma[key] = prev
            best = {}
            for d in op.deps:
                if d.dma:
                    key = (d.eng, "d", d.dsem[1])
                    if w_dma.get(key, 0) < d.dval:
                        waits.append((self.sems[key], d.dval))
                        w_dma[key] = d.dval
                elif d.sig > best.get(d.eng, 0):
                    best[d.eng] = d.sig
            for x, sg in best.items():
                if w_sig[x] < sg:
                    waits.append((self.sems[(x, "c", (sg - 1) // self.SEG)], (sg - 1) % self.SEG + 1))
                    w_sig[x] = sg
            for sem, val in waits:
                e.wait_ge(sem, val)
            if op.fn is None:
                continue
            ins = op.fn(e)
            if op.dma:
                ins.then_inc(self.sems[(eng, "d", op.dsem[1])], 16)
            elif op.need:
                ins.then_inc(self.sems[(eng, "c", (op.sig - 1) // self.SEG)], 1)


def build(mode):
    do1 = mode in ("p1", "fused")
    do2 = mode in ("p2", "fused")
    nc = bass.Bass("TRN2", target_bir_lowering=False)
    S = Sched()

    def din(name, shape, dt=F32):
        return nc.dram_tensor(name, list(shape), dt, kind="ExternalInput").ap()

    def dout(name, shape, dt=F32):
        return nc.dram_tensor(name, list(shape), dt, kind="ExternalOutput").ap()

    xT_d = din("xT", [128, KC, T])
    vec_d = din("vec", [128, NVEC])
    if do1:
        xh_d = din("xh", [128, KC, 32])
        sgubc_d = din("sgubc", [128, 3072])
        wsT_d = din("wsT", [128, 1024])
        wconv_d = din("w_conv", [8, 128, 4096])
        wdiag_d = din("w_diag", [8, 128, 31 * 128])
        wu_d = din("w_u", [8, 128, 2048])
        wv_d = din("w_v", [2, 128, 8192])
        wo_d = din("w_o", [16, 128, 2048])
        wgu0_d = din("w_gu0", [FC, 128, 4096])
        wd0_d = din("w_d0", [2, 16, 128, FH * 128])
        ccs_d = din("ccs", [128, 2, 512], BF16)
    if do2:
        csn_d = din("csn", [128, 2, 16, T], BF16)
        wf_d = din("w_f", [16, 128, 2048])
        wgu1_d = din("w_gu1", [FC, 128, 4096])
        wd1_d = din("w_d1", [2, 16, 128, FH * 128])
        out_d = dout("outT", [128, KC, T])
    if mode == "p1":
        x1_d = dout("x1T", [128, KC, T])
        abown_d = dout("ab_own", [16, 2, 128, 8, 128], BF16)
    elif mode == "p2":
        abfull_d = din("ab_full", [2, 16, 2, 128, 8, 128], BF16)
    else:
        xTo_d = din("xTo", [128, KC, T])
        xho_d = din("xho", [128, KC, 32])
        abfull_d = nc.dram_tensor("ab_full_i", [2, 16, 2, 128, 8, 128], BF16, kind="Internal").ap()

    cur = [SBUF_BASE]

    def region(nbytes):
        o = cur[0]
        cur[0] += (nbytes + 31) // 32 * 32
        assert cur[0] <= SBUF_END, ("sbuf overflow", cur[0])
        return o

    NRING = 3
    SLOT_B = 8192
    R_OFF = region(NRING * SLOT_B)
    VEC_OFF = region(NVEC * 4)
    ONES_OFF = region(3 * 256)
    CCS_OFF = region(2048)
    WST_OFF = region(2048)
    SQ_OFF = region(4 * 1024)
    RSTD_OFF = region(2 * 2048)
    STD_OFF = region(2 * 2048)
    EPS_OFF = region(64)
    X_OFF = region(KC * T * 4)
    H_OFF = region(KC * TE * 2)
    M_OFF = region(KC * T * 2)
    S_OFF = cur[0]
    S_SIZE = SBUF_END - S_OFF
    assert S_SIZE >= 32768 + 2048, S_SIZE

    cnt = [0]

    def sb(name, shape, dt, off):
        cnt[0] += 1
        return nc.alloc_sbuf_tensor_at(f"{name}{cnt[0]}", list(shape), dt, offset=off)

    xT = sb("xT", [128, KC, T], F32, X_OFF)
    hT = sb("hT", [128, KC, TE], BF16, H_OFF)
    mixT = sb("mixT", [128, KC, T], BF16, M_OFF)
    ring = [sb("ring", [128, 4096], BF16, R_OFF + i * SLOT_B) for i in range(NRING)]
    vec = sb("vec", [128, NVEC], F32, VEC_OFF)
    onesD = sb("onesD", [128, 128], BF16, ONES_OFF)
    onesG = sb("onesG", [128, 128], BF16, ONES_OFF + 256)
    ccs = sb("ccs", [128, 2, 512], BF16, CCS_OFF)
    wsT = sb("wsT", [128, 8, 128], BF16, WST_OFF)
    sqb = [sb("sq", [128, 512], BF16, SQ_OFF + i * 1024) for i in range(4)]
    rstdb = [sb("rstd", [128, 512], F32, RSTD_OFF + i * 2048) for i in range(2)]
    stdb = [sb("std", [128, 512], F32, STD_OFF + i * 2048) for i in range(2)]

    psum = [nc.alloc_psum_tensor(f"psb{i}", [128, 512], F32) for i in range(8)]
    bank_ctr = [0]

    def next_bank():
        b = bank_ctr[0] % 8
        bank_ctr[0] += 1
        return b

    ring_ctr = [0]

    def next_slot():
        s = ring_ctr[0] % NRING
        ring_ctr[0] += 1
        return s

    def vcol(c):
        return vec[:, c:c + 1]

    def mm(b, n, lhsT, rhs, start, stop, reads):
        out = psum[b][:, 0:n] if isinstance(n, int) else n
        S.add("pe", lambda e: e.matmul(out, lhsT, rhs, start=start, stop=stop),
              reads=reads, writes=[("ps", b)])

    def act(out, in_, func, reads, writes, bias=None, scale=None):
        kw = {}
        if bias is not None:
            kw["bias"] = bias
        if scale is not None:
            kw["scale"] = scale
        S.add("act", lambda e: e.activation(out, in_, func, **kw), reads=reads, writes=writes)

    def dve_tt(out, in0, in1, op, reads, writes):
        S.add("dve", lambda e: e.tensor_tensor(out, in0, in1, op), reads=reads, writes=writes)

    def dve_stt(out, in0, scalar, in1, op0, op1, reads, writes):
        S.add("dve", lambda e: e.scalar_tensor_tensor(out, in0, scalar, in1, op0, op1),
              reads=reads, writes=writes)

    def dve_ts(out, in0, s1, s2, op0, op1, reads, writes):
        S.add("dve", lambda e: e.tensor_scalar(out, in0, s1, s2, op0, op1), reads=reads, writes=writes)

    def dma(eng, out, in_, reads, writes):
        S.add(eng, lambda e: e.dma_start(out=out, in_=in_), reads=reads, writes=writes, dma=True)

    S.add("dve", lambda e: e.memset(onesD[:], 1.0 / D), writes=["onesD"])
    S.add("dve", lambda e: e.memset(onesG[:], 1.0 / 128.0), writes=["onesG"])
    dma("sp", vec[:], vec_d, [], ["vec"])

    norm_ctr = [0]

    def rmsnorm(gbase, blocks, src, dst, srckey, dstkey):
        for (c0, n, bk) in blocks:
            b = next_bank()
            i = norm_ctr[0] % 2
            norm_ctr[0] += 1
            for kc in range(KC):
                q = sqb[kc % 4]
                act(q[:, 0:n], src(kc, c0, n), AF.Square, reads=[srckey(kc, bk)], writes=[("sq", kc % 4)])
                mm(b, n, onesD[:], q[:, 0:n], kc == 0, kc == KC - 1, reads=[("sq", kc % 4), "onesD"])
            act(stdb[i][:, 0:n], psum[b][:, 0:n], AF.Sqrt, reads=[("ps", b), "eps0"], writes=[("std", i)],
                bias=eps_rms[:, 0:1], scale=1.0)
            S.add("dve", lambda e, i=i, n=n: e.reciprocal(rstdb[i][:, 0:n], stdb[i][:, 0:n]),
                  reads=[("std", i)], writes=[("rstd", i)])
            for kc in range(KC):
                dve_stt(dst(kc, c0, n), src(kc, c0, n), vcol(gbase + kc), rstdb[i][:, 0:n],
                        ALU.mult, ALU.mult,
                        reads=[srckey(kc, bk), ("rstd", i), "vec"], writes=[dstkey(kc, bk)])

    epst = sb("eps", [128, 4], F32, EPS_OFF)
    eps_rms = epst[:, 0:1]
    eps_ln = epst[:, 1:2]
    S.add("dve", lambda e: e.memset(epst[:, 0:1], RMS_EPS), writes=["eps0"])
    S.add("dve", lambda e: e.memset(epst[:, 1:2], LN_EPS), writes=["eps1"])
    MAINB = [(0, 512, 0), (512, 512, 1)]

    def xsrc(kc, c0, n):
        return xT[:, kc, c0:c0 + n]

    def hdst(kc, c0, n):
        return hT[:, kc, c0:c0 + n]

    def xkey(kc, bk):
        return ("xT", kc, bk)

    def hkey(kc, bk):
        return ("hT", kc, bk)

    def load_xT(src_d):
        for q in range(4):
            dma("sp", xT[:, 4 * q:4 * q + 4, :], src_d[:, 4 * q:4 * q + 4, :], [],
                [("xT", kc, bk) for kc in range(4 * q, 4 * q + 4) for bk in (0, 1)])

    def wload(src_ap, ncols, extra_reads=()):
        s = next_slot()
        dma("pool", ring[s][:, 0:ncols], src_ap, list(extra_reads), [("ring", s)])
        return s

    def ffn(layer, wgu_d, wd_d):
        gb = V_FFNG0 if layer == 0 else V_FFNG1
        rmsnorm(gb, MAINB, xsrc, hdst, xkey, hkey)
        aT = sb("aT", [128, FH, T], BF16, M_OFF)
        sg = [sb("sg", [128, 512], F32, M_OFF + FH * T * 2 + i * 2048) for i in range(2)]
        sgc = 0
        for fh in range(DBG_FH):
            for fc in range(FH if DBG_FC is None else DBG_FC):
                f = fh * FH + fc
                s = wload(wgu_d[f], 4096)
                W = ring[s][:, :].rearrange("p (a k n) -> p a k n", a=2, k=KC)
                for half in range(2):
                    bg, bu = next_bank(), next_bank()
                    for kc in range(KC):
                        mm(bg, 512, W[:, 0, kc, :], hT[:, kc, half * 512:(half + 1) * 512], kc == 0, kc == KC - 1,
                           reads=[("ring", s), hkey(kc, half)])
                    for kc in range(KC):
                        mm(bu, 512, W[:, 1, kc, :], hT[:, kc, half * 512:(half + 1) * 512], kc == 0, kc == KC - 1,
                           reads=[("ring", s), hkey(kc, half)])
                    j = sgc % 2
                    sgc += 1
                    act(sg[j][:], psum[bg][:], AF.Silu, reads=[("ps", bg), "Mreg"], writes=[("sg", j)])
                    dve_tt(aT[:, fc, half * 512:(half + 1) * 512], sg[j][:], psum[bu][:], ALU.mult,
                           reads=[("sg", j), ("ps", bu)], writes=[("aT", fc, half)])
            for n in range(KC if DBG_DOWN else 0):
                s = wload(wd_d[fh, n], FH * 128)
                W = ring[s][:, 0:FH * 128].rearrange("p (k n) -> p k n", k=FH)
                for half in range(2):
                    b = next_bank()
                    for fc in range(FH):
                        mm(b, 512, W[:, fc, :], aT[:, fc, half * 512:(half + 1) * 512], fc == 0, fc == FH - 1,
                           reads=[("ring", s), ("aT", fc, half)])
                    dve_tt(xT[:, n, half * 512:(half + 1) * 512], psum[b][:], xT[:, n, half * 512:(half + 1) * 512],
                           ALU.add, reads=[("ps", b)], writes=[xkey(n, half)])

    def part1(xT_d, xh_d, abown_d, pss):
        load_xT(xT_d)
        check(1)
        xh = sb("xh", [128, KC, 32], F32, S_OFF)
        dma("sp", xh[:], xh_d, [], ["xh"])
        rmsnorm(V_MIXG0, MAINB, xsrc, hdst, xkey, hkey)
        rmsnorm(V_MIXG0, [(0, 32, 2)],
                lambda kc, c0, n: xh[:, kc, 0:32], lambda kc, c0, n: hT[:, kc, 1024:1056],
                lambda kc, bk: "xh", hkey)

        ax = [X_OFF]

        def aX(nbytes):
            o = ax[0]
            ax[0] += (nbytes + 31) // 32 * 32
            assert ax[0] <= X_OFF + KC * T * 4, "X arena overflow"
            return o

        as_ = [S_OFF]

        def aS(nbytes):
            o = as_[0]
            as_[0] += (nbytes + 31) // 32 * 32
            assert as_[0] <= SBUF_END, "S arena overflow"
            return o

        Wv = [sb("Wv", [128, KC, 512], BF16, aX(16384)) for _ in range(2)]
        glu = [sb("glu", [128, TE], BF16, aX(TE * 2)) for _ in range(2)]
        ycv = [sb("ycv", [128, T], F32, aX(T * 4)) for _ in range(2)]
        ybf = sb("ybf", [128, T], BF16, aX(T * 2))
        ysq = sb("ysq", [128, T], BF16, aX(T * 2))
        sig = [sb("sig", [128, 512], F32, aX(2048)) for _ in range(2)]
        m2 = sb("m2", [128, 512], F32, aX(2048))
        varb = sb("varb", [128, 512], F32, aX(2048))
        tcen = sb("tcen", [128, 512], F32, aX(2048))
        sgubc = sb("sgubc", [128, 3072], F32, aS(12288))
        vg = [sb("vg", [128, 1024], F32, aS(4096)) for _ in range(2)]
        vln = [sb("vln", [128, 1024], BF16, aS(2048)) for _ in range(2)]
        sptmp = sb("sptmp", [128, 512], F32, aS(2048))
        stats = sb("stats", [128, 2, 6], F32, aS(64))
        mv = sb("mv", [128, 2], F32, aS(32))
        sdv = sb("sdv", [128, 2], F32, aS(32))
        m2h = [m2, sb("m2b", [128, 512], F32, aS(2048))]
        varbh = [varb, sb("varbb", [128, 512], F32, aS(2048))]
        tcenh = [tcen, sb("tcenb", [128, 512], F32, aS(2048))]

        xdead = [xkey(kc, bk) for kc in range(KC) for bk in (0, 1)]
        stgk = [k for k in S.state if isinstance(k, tuple) and k[0] == "stg"]
        S.add("dve", lambda e: e.memset(m2[:, 0:1], 0.0), reads=[], writes=xdead + stgk + ["arenaX"])

        dma("sp", sgubc[:], sgubc_d, [], ["sgubc", "xh"])
        dma("pool", wsT[:].rearrange("p a b -> p (a b)"), wsT_d, [], ["wsT"])

        check(2)
        if MIX_POOLS == 0:
            pools = {"A": [0, 1, 2, 3], "B": [4, 5], "Bs": [4, 5], "C": [6, 7]}
            pools["Bs"] = pools["B"]
        elif MIX_POOLS == 1:
            pools = {"A": [0, 1], "B": [2, 3], "Bs": [4, 5], "C": [6, 7]}
        else:
            pools = {"A": [0, 1, 2, 3], "B": [4], "Bs": [5, 6], "C": [7]}
        pctr = {"A": 0, "B": 0, "Bs": 0, "C": 0}
        if pools["Bs"] is pools["B"]:
            pctr_alias = {"Bs": "B"}
        else:
            pctr_alias = {}

        def pb(which):
            lst = pools[which]
            which = pctr_alias.get(which, which)
            b_ = lst[pctr[which] % len(lst)]
            pctr[which] += 1
            return b_

        for nb in range(2):
            dma("pool", Wv[nb][:].rearrange("p k n -> p (k n)"), wv_d[nb], ["arenaX"], [("Wv", nb)])
        lng_bc = sgubc[:, 0:1024]
        lnb_bc = sgubc[:, 1024:2048]
        bs_bc = sgubc[:, 2048:3072].rearrange("p (h q) -> p h q", h=8)

        def conv_A(c):
            s = wload(wconv_d[c], 4096)
            W = ring[s][:, :].rearrange("p (a k n) -> p a k n", a=2, k=KC)
            g = glu[c % 2]
            gk = ("glu", c % 2)
            for (c0, n, bk) in [(0, 512, 0), (512, 512, 1), (1024, 32, 2)]:
                ba, bg = pb("A"), pb("A")
                for kc in range(KC):
                    mm(ba, n, W[:, 0, kc, :], hT[:, kc, c0:c0 + n], kc == 0, kc == KC - 1,
                       reads=[("ring", s), hkey(kc, bk)])
                for kc in range(KC):
                    mm(bg, n, W[:, 1, kc, :], hT[:, kc, c0:c0 + n], kc == 0, kc == KC - 1,
                       reads=[("ring", s), hkey(kc, bk)])
                j = bk % 2
                act(sig[j][:, 0:n], psum[bg][:, 0:n], AF.Sigmoid, reads=[("ps", bg), "arenaX"], writes=[("sig", j)])
                if bk < 2:
                    dve_tt(g[:, HALO + c0:HALO + c0 + n], psum[ba][:, 0:n], sig[j][:, 0:n], ALU.mult,
                           reads=[("ps", ba), ("sig", j), "arenaX"], writes=[gk + (bk,)])
                else:
                    dve_tt(g[:, 0:HALO], psum[ba][:, 0:HALO], sig[j][:, 0:HALO], ALU.mult,
                           reads=[("ps", ba), ("sig", j), "arenaX"], writes=[gk + (2,)])
                    dve_tt(g[:, HALO + T:HALO + T + HALO], psum[ba][:, HALO:2 * HALO], sig[j][:, HALO:2 * HALO],
                           ALU.mult, reads=[("ps", ba), ("sig", j)], writes=[gk + (3,)])

        def conv_B(c):
            g = glu[c % 2]
            gk = ("glu", c % 2)
            sd = wload(wdiag_d[c], 31 * 128)
            Dg = ring[sd][:, 0:31 * 128].rearrange("p (j n) -> p j n", j=31)
            gkeys = [gk + (i,) for i in range(4)]
            y = ycv[c % 2]
            yk = ("ycv", c % 2)
            for half in range(2):
                hs = slice(half * 512, (half + 1) * 512)
                by = pb("B")
                for j in range(31):
                    mm(by, 512, Dg[:, j, :], g[:, half * 512 + j:half * 512 + j + 512], j == 0, j == 30,
                       reads=[("ring", sd)] + gkeys)
                act(y[:, hs], psum[by][:], AF.Identity, reads=[("ps", by), "vec"], writes=[yk + (half,)],
                    bias=vcol(V_CONVB + c), scale=1.0)
                act(ybf[:, hs], psum[by][:], AF.Identity, reads=[("ps", by), "vec"], writes=[("ybf", half)],
                    bias=vcol(V_CONVB + c), scale=1.0)
                act(ysq[:, hs], psum[by][:], AF.Square, reads=[("ps", by), "vec"], writes=[("ysq", half)],
                    bias=vcol(V_CONVB + c), scale=1.0)

        def conv_C(c):
            y = ycv[c % 2]
            yk = ("ycv", c % 2)
            for half in range(2):
                hs = slice(half * 512, (half + 1) * 512)
                m2_, varb_, tcen_ = m2h[half], varbh[half], tcenh[half]
                bm, bq = pb("Bs"), pb("Bs")
                mm(bm, 512, onesG[:], ybf[:, hs], True, True, reads=[("ybf", half), "onesG"])
                mm(bq, 512, onesG[:], ysq[:, hs], True, True, reads=[("ysq", half), "onesG"])
                act(m2_[:], psum[bm][:], AF.Square, reads=[("ps", bm), "arenaX"], writes=[("m2", half)])
                dve_tt(varb_[:], psum[bq][:], m2_[:], ALU.subtract, reads=[("ps", bq), ("m2", half)],
                       writes=[("varb", half)])
                act(varb_[:], varb_[:], AF.Sqrt, reads=[("varb", half), "eps1"], writes=[("varb", half)],
                    bias=eps_ln, scale=1.0)
                S.add("dve", lambda e, v_=varb_: e.reciprocal(v_[:], v_[:]), reads=[("varb", half)],
                      writes=[("varb", half)])
                dve_tt(tcen_[:], y[:, hs], psum[bm][:], ALU.subtract, reads=[yk + (half,), ("ps", bm)],
                       writes=[("tcen", half)])
                dve_tt(tcen_[:], tcen_[:], varb_[:], ALU.mult, reads=[("tcen", half), ("varb", half)],
                       writes=[("tcen", half)])
                act(mixT[:, c, hs], tcen_[:], AF.Silu, reads=[("tcen", half), "vec"], writes=[("mixT", c, half)],
                    bias=vcol(V_CLNB + c), scale=vcol(V_CLNG + c))

        def sgu_u(hc):
            s = wload(wu_d[hc], 2048)
            W = ring[s][:, 0:2048].rearrange("p (k n) -> p k n", k=KC)
            for half in range(2):
                b = pb("C")
                for kc in range(KC):
                    mm(b, 512, W[:, kc, :], hT[:, kc, half * 512:(half + 1) * 512], kc == 0, kc == KC - 1,
                       reads=[("ring", s), hkey(kc, half)])
                act(mixT[:, 8 + hc, half * 512:(half + 1) * 512], psum[b][:], AF.Gelu,
                    reads=[("ps", b)], writes=[("mixT", 8 + hc, half)])

        def sgu_v_ln(tt):
            v = vg[tt % 2]
            vk = ("vg", tt % 2)
            for nb in range(2):
                b = pb("C")
                for kc in range(KC):
                    mm(b, 512, hT[:, kc, tt * 128:(tt + 1) * 128], Wv[nb][:, kc, :], kc == 0, kc == KC - 1,
                       reads=[("Wv", nb), hkey(kc, tt // 4)])
                act(v[:, nb * 512:(nb + 1) * 512], psum[b][:], AF.Gelu, reads=[("ps", b), "arenaX"],
                    writes=[vk + (nb,)])
            for nb in range(2):
                S.add("dve", lambda e, v=v, nb=nb: e.bn_stats(stats[:, nb, :], v[:, nb * 512:(nb + 1) * 512]),
                      reads=[vk + (nb,)], writes=[("stats", nb)])
            S.add("dve", lambda e: e.bn_aggr(mv[:], stats[:].rearrange("p a b -> p (a b)")),
                  reads=[("stats", 0), ("stats", 1)], writes=["mv"])
            act(sdv[:, 0:1], mv[:, 1:2], AF.Sqrt, reads=["mv", "eps1", "arenaX"], writes=["sdv0"], bias=eps_ln, scale=1.0)
            S.add("dve", lambda e: e.reciprocal(sdv[:, 1:2], sdv[:, 0:1]), reads=["sdv0"], writes=["sdv1"])
            dve_ts(v[:], v[:], mv[:, 0:1], sdv[:, 1:2], ALU.subtract, ALU.mult,
                   reads=[vk + (0,), vk + (1,), "mv", "sdv1"], writes=[vk + (0,), vk + (1,)])
            dve_tt(v[:], v[:], lng_bc, ALU.mult, reads=[vk + (0,), vk + (1,), "sgubc"],
                   writes=[vk + (0,), vk + (1,)])
            dve_tt(vln[tt % 2][:], v[:], lnb_bc, ALU.add, reads=[vk + (0,), vk + (1,), "sgubc"],
                   writes=[("vln", tt % 2)])

        def sgu_spatial(tt):
            vl = vln[tt % 2]
            vlk = ("vln", tt % 2)
            for hg in range(2):
                b = pb("C")
                for hh in range(4):
                    hd = hg * 4 + hh
                    mm(b, psum[b][:, hh * 128:(hh + 1) * 128], vl[:, hd * 128:(hd + 1) * 128], wsT[:, hd, :],
                       True, True, reads=[vlk, "wsT"])
                dve_tt(sptmp[:].rearrange("p (h q) -> p h q", h=4), psum[b][:].rearrange("p (h q) -> p h q", h=4),
                       bs_bc[:, hg * 4:hg * 4 + 4, :], ALU.add, reads=[("ps", b), "sgubc"], writes=["sptmp"])
                mo = mixT[:, 8 + hg * 4:8 + hg * 4 + 4, tt * 128:(tt + 1) * 128]
                dve_tt(mo, sptmp[:].rearrange("p (h q) -> p h q", h=4), mo, ALU.mult,
                       reads=["sptmp"] + [("mixT", 8 + hg * 4 + hh, tt // 4) for hh in range(4)],
                       writes=[("mixT", 8 + hg * 4 + hh, tt // 4) for hh in range(4)])

        for hc in range(8):
            sgu_u(hc)
        if MIX_ORDER == 0:
            for c in range(8):
                conv_A(c)
                conv_B(c)
                conv_C(c)
                sgu_v_ln(c)
                if c >= 1:
                    sgu_spatial(c - 1)
            sgu_spatial(7)
        elif MIX_ORDER == 1:
            for i in range(10):
                if i < 8:
                    conv_A(i)
                    sgu_v_ln(i)
                if 2 <= i:
                    conv_C(i - 2)
                if 1 <= i <= 8:
                    conv_B(i - 1)
                    sgu_spatial(i - 1)
        else:
            for i in range(9):
                if i < 8:
                    conv_A(i)
                if 1 <= i:
                    conv_B(i - 1)
                if i < 8:
                    sgu_v_ln(i)
                if 1 <= i:
                    conv_C(i - 1)
                    sgu_spatial(i - 1)

        check(5)
        arena_keys = [k for k in S.state if isinstance(k, tuple) and k[0] in
                      ("glu", "ycv", "sig", "Wv", "ybf", "ysq", "m2", "varb", "tcen")] + ["arenaX"]
        for q in range(4):
            dma("sp", xT[:, 4 * q:4 * q + 4, :], xT_d[:, 4 * q:4 * q + 4, :], [],
                arena_keys + [("xT", kc, bk) for kc in range(4 * q, 4 * q + 4) for bk in (0, 1)])
        for n in range(KC):
            s = wload(wo_d[n], 2048)
            W = ring[s][:, 0:2048].rearrange("p (k n) -> p k n", k=KC)
            for half in range(2):
                b = next_bank()
                for kc in range(KC):
                    mm(b, 512, W[:, kc, :], mixT[:, kc, half * 512:(half + 1) * 512], kc == 0, kc == KC - 1,
                       reads=[("ring", s), ("mixT", kc, half)])
                dve_tt(xT[:, n, half * 512:(half + 1) * 512], psum[b][:], xT[:, n, half * 512:(half + 1) * 512],
                       ALU.add, reads=[("ps", b)], writes=[xkey(n, half)])

        check(6)
        skeys = [k for k in S.state if isinstance(k, tuple) and k[0] in ("vg", "vln", "stats", "mixT", "m2", "varb", "tcen")] + \
                ["sgubc", "sptmp", "mv", "sdv0", "sdv1", "xh"]
        S.add("dve", lambda e: e.memset(epst[:, 2:3], 0.0), reads=[], writes=skeys + ["Mreg", "epsm2"])
        ffn(0, wgu0_d, wd0_d)

        check(7)
        if mode == "p1":
            for q in range(4):
                dma("sp", x1_d[:, 4 * q:4 * q + 4, :], xT[:, 4 * q:4 * q + 4, :],
                    [("xT", kc, bk) for kc in range(4 * q, 4 * q + 4) for bk in (0, 1)], [("x1out", q)])
            x1done[0] = True

        dma("sp", ccs[:], ccs_d, [], ["ccs"])
        rmsnorm(V_MIXG1, MAINB, xsrc, hdst, xkey, hkey)
        stg = [sb("stg", [128, 2, 16, 128], BF16, S_OFF + 16384 + i * 8192) for i in range(2)]
        sc = 0
        for tt in range(8 if DBG_L1 >= 2 else 0):
            k = tt % 2
            for g in range(8):
                b = next_bank()
                for j in range(2):
                    mm(b, 512, hT[:, 2 * g + j, tt * 128:(tt + 1) * 128], ccs[:, j, :], j == 0, j == 1,
                       reads=[hkey(2 * g + j, tt // 4), "ccs"])
                so = stg[k][:, :, 2 * g:2 * g + 2, :]
                pi = psum[b][:].rearrange("p (a c j) -> p a c j", a=2, c=2)
                sc += 1
                if sc % 2 == 0:
                    act(so, pi, AF.Copy, reads=[("ps", b)], writes=[("stg", k, g)])
                else:
                    S.add("dve", lambda e, so=so, pi=pi: e.tensor_copy(so, pi),
                          reads=[("ps", b)], writes=[("stg", k, g)])
            for ab in range(2 if DBG_L1 >= 3 else 0):
                dma("sp", abown_d[:, ab, :, tt, :].rearrange("k p j -> p k j"), stg[k][:, ab, :, :],
                    [("stg", k, g) for g in range(8)], [("abown", pss, tt, ab)])

    x1done = [False]

    class _Stop(Exception):
        pass

    def check(k):
        if DEBUG_STOP is not None and k > DEBUG_STOP:
            raise _Stop()

    if mode == "p1":
        try:
            part1(xT_d, xh_d, abown_d, 0)
        except _Stop:
            pass
        if not x1done[0]:
            for q in range(4):
                dma("sp", x1_d[:, 4 * q:4 * q + 4, :], xT[:, 4 * q:4 * q + 4, :],
                    [("xT", kc, bk) for kc in range(4 * q, 4 * q + 4) for bk in (0, 1)], [("x1out", q)])
    elif mode == "fused":
        part1(xTo_d, xho_d, abfull_d[0], 0)
        part1(xT_d, xh_d, abfull_d[1], 1)
        ab_keys = [k for k in S.state if isinstance(k, tuple) and k[0] == "abown"]
        S.add("sp", None, reads=ab_keys, writes=[])

    if do2:
        if mode == "p2":
            load_xT(xT_d)
        Cs = sb("Cs", [128, 16, T], BF16, H_OFF)
        Ss = sb("Ss", [128, 16, T], BF16, S_OFF)
        hkeys = [hkey(kc, bk) for kc in range(KC) for bk in (0, 1, 2)]
        skeys2 = [k for k in S.state if isinstance(k, tuple) and k[0] in ("aT", "sg", "stg")]
        S.add("act", lambda e: e.activation(epst[:, 2:3], epst[:, 0:1], AF.Copy), reads=["eps0"],
              writes=[k for k in S.state if isinstance(k, tuple) and k[0] == "aT"] + ["YTfence", "epsm2"])
        for hh in range(2):
            dma("sp", Cs[:, 8 * hh:8 * hh + 8, :], csn_d[:, 0, 8 * hh:8 * hh + 8, :], [], hkeys + [("Cs", hh)])
            dma("sp", Ss[:, 8 * hh:8 * hh + 8, :], csn_d[:, 1, 8 * hh:8 * hh + 8, :], [], skeys2 + [("Ss", hh)])
        YT = mixT
        for ck in range(16):
            s = next_slot()
            sv = ring[s][:, :].rearrange("p (a r f) -> p a r f", a=2, r=2)
            for ab in range(2):
                dma("sp", sv[:, ab, :, :], abfull_d[:, ck, ab, :, :, :].rearrange("r p t j -> p r (t j)"),
                    [], [("ring", s)])
            for kb in range(2):
                b = next_bank()
                i = 0
                for ab in range(2):
                    Mx = Cs if ab == 0 else Ss
                    for st in range(16):
                        r, tt = st // 8, st % 8
                        mm(b, 512, sv[:, ab, r, tt * 128:(tt + 1) * 128], Mx[:, st, kb * 512:(kb + 1) * 512],
                           i == 0, i == 31,
                           reads=[("ring", s), ("Cs" if ab == 0 else "Ss", st // 8)])
                        i += 1
                act(YT[:, ck, kb * 512:(kb + 1) * 512], psum[b][:], AF.Copy, reads=[("ps", b)],
                    writes=[("YT", ck, kb)])
        for n in range(KC):
            s = next_slot()
            dma("pool", ring[s][:, 0:2048], wf_d[n], [], [("ring", s)])
            W = ring[s][:, 0:2048].rearrange("p (k n) -> p k n", k=KC)
            for half in range(2):
                b = next_bank()
                for kc in range(KC):
                    mm(b, 512, W[:, kc, :], YT[:, kc, half * 512:(half + 1) * 512], kc == 0, kc == KC - 1,
                       reads=[("ring", s), ("YT", kc, half)])
                dve_stt(xT[:, n, half * 512:(half + 1) * 512], psum[b][:], vcol(V_FNETB + n),
                        xT[:, n, half * 512:(half + 1) * 512], ALU.add, ALU.add,
                        reads=[("ps", b), "vec"], writes=[xkey(n, half)])
        ykeys = [("YT", ck, kb) for ck in range(16) for kb in range(2)] + [("Ss", 0), ("Ss", 1), ("Cs", 0), ("Cs", 1)]
        S.add("dve", lambda e: e.memset(epst[:, 3:4], 0.0), reads=[], writes=ykeys + ["Mreg", "epsm3"] + hkeys)
        ffn(1, wgu1_d, wd1_d)
        rmsnorm(V_FING, MAINB, xsrc, xsrc, xkey, xkey)
        okeys = []
        for bk in range(2):
            for q in range(2):
                dma("sp", out_d[:, 8 * q:8 * q + 8, bk * 512:(bk + 1) * 512],
                    xT[:, 8 * q:8 * q + 8, bk * 512:(bk + 1) * 512],
                    [("xT", kc, bk) for kc in range(8 * q, 8 * q + 8)], [("out", bk, q)])
                okeys.append(("out", bk, q))
        S.add("sp", None, reads=okeys, writes=[])
    else:
        okeys = [("x1out", q) for q in range(4)] + \
                [("abown", 0, tt, ab) for tt in range(8) for ab in range(2)]
        okeys = [k for k in okeys if k in S.state]
        S.add("sp", None, reads=okeys, writes=[])

    with ExitStack() as stack:
        stack.enter_context(nc.allow_low_precision("bf16 matmul operands, fp32 accumulation"))
        S.finalize(nc, stack)
        with nc.Block() as block:
            @block.tensor
            def _(e):
                S.emit("pe", e)

            @block.scalar
            def _(e):
                S.emit("act", e)

            @block.vector
            def _(e):
                S.emit("dve", e)

            @block.gpsimd
            def _(e):
                S.emit("pool", e)

            @block.sync
            def _(e):
                S.emit("sp", e)
    return nc


def _pc(v):
    v = np.asarray(v, np.float32)
    return np.ascontiguousarray(v.reshape(-1, 128).T)


def _wtile(W, ncols_per_tile):
    K, N = W.shape
    kc = K // 128
    nt = N // ncols_per_tile
    a = W.reshape(kc, 128, nt, ncols_per_tile).transpose(2, 1, 0, 3)
    return np.ascontiguousarray(a).reshape(nt, 128, kc * ncols_per_tile)


_CONST_CACHE = {}


def _dft_consts():
    if "c" in _CONST_CACHE:
        return _CONST_CACHE["c"]
    bf = ml_dtypes.bfloat16
    c = np.arange(256, dtype=np.float64)
    th = 2 * np.pi * np.outer(c, c) / 256.0
    cc = (np.cos(th) / 16.0).reshape(2, 128, 256)
    sc = (np.sin(th) / 16.0).reshape(2, 128, 256)
    ccs = np.concatenate([cc, sc], axis=2).transpose(1, 0, 2)
    ccs = np.ascontiguousarray(ccs).astype(np.float32).astype(bf)
    csn = {}
    for half in range(2):
        for order in ("seq", "oo"):
            if order == "seq":
                s = np.arange(S_LEN, dtype=np.int64)
            else:
                oth = 1 - half
                s = np.concatenate([np.arange(T) + oth * T, np.arange(T) + half * T]).astype(np.int64)
            k = np.arange(T, dtype=np.int64) + half * T
            ph = (np.outer(s, k) % S_LEN).astype(np.float64) * (2 * np.pi / S_LEN)
            sc_ = 1.0 / math.sqrt(S_LEN)
            co = (np.cos(ph) * sc_).reshape(16, 128, T).transpose(1, 0, 2)
            si = (-np.sin(ph) * sc_).reshape(16, 128, T).transpose(1, 0, 2)
            a = np.stack([co, si], axis=1)
            csn[(half, order)] = np.ascontiguousarray(a).astype(np.float32).astype(bf)
    _CONST_CACHE["c"] = (ccs, csn)
    return ccs, csn


_NC_CACHE = {}


def _get_nc(mode):
    if mode not in _NC_CACHE:
        _NC_CACHE[mode] = build(mode)
    return _NC_CACHE[mode]


FUSED = False


def kernel(x, mix_norm_g, ffn_norm_g, final_norm_g, ab_w_in, conv_dw_w, conv_dw_b, conv_ln_g,
           conv_ln_b, sgu_ln_g, sgu_ln_b, sgu_w, sgu_b, ab_w_out, fnet_w_out, fnet_b_out,
           ffn_w_gate, ffn_w_up, ffn_w_down):
    f32 = np.float32
    x = np.asarray(x, f32)
    ccs, csn = _dft_consts()

    vec = np.zeros((128, NVEC), f32)
    vec[:, V_MIXG0:V_MIXG0 + 16] = _pc(mix_norm_g[0])
    vec[:, V_FFNG0:V_FFNG0 + 16] = _pc(ffn_norm_g[0])
    vec[:, V_MIXG1:V_MIXG1 + 16] = _pc(mix_norm_g[1])
    vec[:, V_FFNG1:V_FFNG1 + 16] = _pc(ffn_norm_g[1])
    vec[:, V_FING:V_FING + 16] = _pc(final_norm_g)
    vec[:, V_CONVB:V_CONVB + 8] = _pc(conv_dw_b[0])
    vec[:, V_CLNG:V_CLNG + 8] = _pc(conv_ln_g[0])
    vec[:, V_CLNB:V_CLNB + 8] = _pc(conv_ln_b[0])
    vec[:, V_FNETB:V_FNETB + 16] = _pc(fnet_b_out[0])
    cw = np.asarray(conv_dw_w[0], f32)
    vec[:, V_CONVW:V_CONVW + 248] = cw.reshape(31, 8, 128).transpose(2, 1, 0).reshape(128, 248)

    sgubc = np.concatenate([np.asarray(sgu_ln_g[0], f32), np.asarray(sgu_ln_b[0], f32),
                            np.asarray(sgu_b[0], f32).reshape(-1)])
    sgubc = np.ascontiguousarray(np.broadcast_to(sgubc[None, :], (128, 3072)))
    wsT = np.ascontiguousarray(np.asarray(sgu_w[0], f32).transpose(2, 0, 1)).reshape(128, 1024)

    w_in = np.asarray(ab_w_in[0], f32)
    ta = _wtile(w_in[:, 0:1024], 128).reshape(8, 128, 1, 2048)
    tg = _wtile(w_in[:, 1024:2048], 128).reshape(8, 128, 1, 2048)
    w_conv = np.ascontiguousarray(np.concatenate([ta, tg], axis=2)).reshape(8, 128, 4096)
    w_diag = np.zeros((8, 128, 31, 128), f32)
    pi = np.arange(128)
    w_diag[:, pi, :, pi] = cw.reshape(31, 8, 128).transpose(2, 1, 0)
    w_diag = w_diag.reshape(8, 128, 31 * 128)
    w_u = _wtile(w_in[:, 2048:3072], 128)
    w_v = _wtile(w_in[:, 3072:4096], 512)
    w_o = _wtile(np.asarray(ab_w_out[0], f32), 128)
    w_f = _wtile(np.asarray(fnet_w_out[0], f32), 128)

    def gu(l):
        a = _wtile(np.asarray(ffn_w_gate[l], f32), 128).reshape(FC, 128, 1, 2048)
        b = _wtile(np.asarray(ffn_w_up[l], f32), 128).reshape(FC, 128, 1, 2048)
        return np.ascontiguousarray(np.concatenate([a, b], axis=2)).reshape(FC, 128, 4096)

    def dn(l):
        W = np.asarray(ffn_w_down[l], f32)
        a = W.reshape(2, FH, 128, 16, 128).transpose(0, 3, 2, 1, 4)
        return np.ascontiguousarray(a).reshape(2, 16, 128, FH * 128)

    w_gu0, w_gu1, w_d0, w_d1 = gu(0), gu(1), dn(0), dn(1)

    xT_l, xh_l = [], []
    for c in range(8):
        b, half = c // 2, c % 2
        xs = x[b, half * T:(half + 1) * T, :]
        xT_l.append(np.ascontiguousarray(xs.T.reshape(KC, 128, T).transpose(1, 0, 2)))
        hal = np.zeros((32, D), f32)
        if half == 1:
            hal[0:HALO] = x[b, T - HALO:T, :]
        else:
            hal[HALO:2 * HALO] = x[b, T:T + HALO, :]
        xh_l.append(np.ascontiguousarray(hal.T.reshape(KC, 128, 32).transpose(1, 0, 2)))

    cores = list(range(8))
    if FUSED:
        nc = _get_nc("fused")
        in_maps = []
        for c in cores:
            o = c ^ 1
            in_maps.append({"xT": xT_l[c], "vec": vec, "xh": xh_l[c], "xTo": xT_l[o], "xho": xh_l[o],
                            "sgubc": sgubc, "wsT": wsT,
                            "w_conv": w_conv, "w_diag": w_diag, "w_u": w_u, "w_v": w_v, "w_o": w_o, "w_gu0": w_gu0,
                            "w_d0": w_d0, "ccs": ccs, "csn": csn[(c % 2, "oo")], "w_f": w_f, "w_gu1": w_gu1,
                            "w_d1": w_d1})
        res = run_bass_kernel_spmd(nc, in_maps, core_ids=cores)
        outs = [r["outT"] for r in res.results]
    else:
        nc1 = _get_nc("p1")
        in_maps = []
        for c in cores:
            in_maps.append({"xT": xT_l[c], "vec": vec, "xh": xh_l[c], "sgubc": sgubc, "wsT": wsT,
                            "w_conv": w_conv, "w_diag": w_diag, "w_u": w_u, "w_v": w_v, "w_o": w_o, "w_gu0": w_gu0,
                            "w_d0": w_d0, "ccs": ccs})
        r1 = run_bass_kernel_spmd(nc1, in_maps, core_ids=cores).results
        nc2 = _get_nc("p2")
        in_maps = []
        for c in cores:
            p = c - (c % 2)
            abf = np.ascontiguousarray(np.stack([r1[p]["ab_own"], r1[p + 1]["ab_own"]], axis=0))
            in_maps.append({"xT": r1[c]["x1T"], "vec": vec, "ab_full": abf, "csn": csn[(c % 2, "seq")],
                            "w_f": w_f, "w_gu1": w_gu1, "w_d1": w_d1})
        res = run_bass_kernel_spmd(nc2, in_maps, core_ids=cores)
        outs = [r["outT"] for r in res.results]

    out = np.empty((4, S_LEN, D), f32)
    for c in cores:
        b, half = c // 2, c % 2
        oT = np.asarray(outs[c], f32)
        out[b, half * T:(half + 1) * T, :] = oT.transpose(1, 0, 2).reshape(D, T).T
    return out
```
